# Optimizing a Trainium2 kernel written in Bass

```python
import jax
import jax.numpy as jnp
from jax import lax
import numpy as np

D_MODEL = 1024
BATCH = 2
SEQ = 16384
DEPTH = 2

N_MEM = 256
D_BR = 512
N_BRANCH = 4
RW_HEADS = 8
RW_HD = D_BR // RW_HEADS
RW_RANK_W = 64
RW_RANK_A = 64
RW_LN_EPS = 64e-5
RW_SHIFT = 3 * D_BR + RW_RANK_W + RW_RANK_A
GLA_HEADS = 4
GLA_DK = D_BR // 2
GLA_RANK = 16
GLA_TAU = 16.0
GLA_CHUNK = 64
RET_HEADS = 4
RET_DK = D_BR // 2
RET_CHUNK = 64
ROPE_BASE = 10000.0
LRU_BLOCKS = 8
LRU_BW = D_BR // LRU_BLOCKS
LRU_CONV = 4
LRU_C = 8.0
XA_HEADS = 4
XA_HD = D_MODEL // XA_HEADS
NORM_EPS = 1e-6

IN_SPLITS = (
    ('rw_r', D_BR), ('rw_k', D_BR), ('rw_v', D_BR), ('rw_wlo', RW_RANK_W), ('rw_alo', RW_RANK_A), ('rw_g', D_BR),
    ('gla_q', GLA_DK), ('gla_k', GLA_DK), ('gla_v', D_BR), ('gla_flo', GLA_RANK), ('gla_g', D_BR),
    ('ret_q', RET_DK), ('ret_k', RET_DK), ('ret_v', D_BR), ('ret_g', D_BR),
    ('lru_x', D_BR), ('lru_g', D_BR),
    ('merge', N_BRANCH * D_MODEL),
)
N_IN = (RW_SHIFT + D_BR) + (2 * GLA_DK + 2 * D_BR + GLA_RANK) + (2 * RET_DK + 2 * D_BR) + 2 * D_BR + N_BRANCH * D_MODEL

kernel_name = 'hybrid_rwkv7_gla_retnet_rglru_gated_block'


def _rmsnorm(x, g):
    x32 = x.astype(jnp.float32)
    y = x32 * lax.rsqrt(jnp.mean(x32 * x32, -1, keepdims=True) + NORM_EPS)
    return (y * g.astype(jnp.float32)).astype(x.dtype)


def _split_last(z, sizes):
    return jnp.split(z, np.cumsum(sizes)[:-1].tolist(), axis=-1)


def _token_shift(z, mu):
    prev = jnp.pad(z[:, :-1], ((0, 0), (1, 0), (0, 0)))
    return z + (prev - z) * mu


def _rope(t, positions):
    half = t.shape[-1] // 2
    inv = ROPE_BASE ** (-jnp.arange(half, dtype=jnp.float32) / half)
    ang = positions.astype(jnp.float32)[..., None] * inv
    cos, sin = jnp.cos(ang)[:, :, None, :], jnp.sin(ang)[:, :, None, :]
    t1, t2 = t[..., :half], t[..., half:]
    return jnp.concatenate([t1 * cos - t2 * sin, t1 * sin + t2 * cos], -1)


def _rwkv7(r, k, v, w_lo, a_lo, w0, w2, a0, a2, k_k, k_a, r_k, ln_g, ln_b):
    B, S, _ = r.shape
    f32 = jnp.float32
    H, N = RW_HEADS, RW_HD
    log_w = -jax.nn.softplus(-(w0 + jnp.tanh(w_lo) @ w2).astype(f32)) - 0.5
    decay = jnp.exp(-jnp.exp(log_w))
    a = jax.nn.sigmoid((a0 + a_lo @ a2).astype(f32))
    k32 = k.astype(f32)
    kk = (k32 * k_k.astype(f32)).reshape(B, S, H, N)
    kk = kk / jnp.maximum(jnp.sqrt(jnp.sum(kk * kk, -1, keepdims=True)), 1e-12)
    k_mod = k32 * (1.0 + (a - 1.0) * k_a.astype(f32))
    heads = lambda t: t.astype(f32).reshape(B, S, H, N)
    rh, wh, kh, vh, ah = heads(r), heads(decay), heads(k_mod), heads(v), heads(a)
    bh = kk * ah

    def step(state, inp):
        r_t, w_t, k_t, v_t, kk_t, b_t = inp
        sa = jnp.einsum('bhvk,bhk->bhv', state, -kk_t)
        state = state * w_t[:, :, None, :] + sa[..., None] * b_t[:, :, None, :] + v_t[..., None] * k_t[:, :, None, :]
        return state, jnp.einsum('bhvk,bhk->bhv', state, r_t)

    xs = tuple(jnp.moveaxis(t, 1, 0) for t in (rh, wh, kh, vh, kk, bh))
    _, y = lax.scan(step, jnp.zeros((B, H, N, N), f32), xs)
    y = jnp.moveaxis(y, 0, 1)
    mean = jnp.mean(y, -1, keepdims=True)
    var = jnp.mean(jnp.square(y - mean), -1, keepdims=True)
    y = ((y - mean) * lax.rsqrt(var + RW_LN_EPS)).reshape(B, S, D_BR) * ln_g.astype(f32) + ln_b.astype(f32)
    bonus = jnp.sum(rh * kh * r_k.astype(f32), -1, keepdims=True) * vh
    return y + bonus.reshape(B, S, D_BR)


def _gla(q, k, v, f_lo, f_up, f_b, norm_g):
    B, S, _ = q.shape
    f32 = jnp.float32
    C, H = GLA_CHUNK, GLA_HEADS
    nC = S // C
    dk, dv = GLA_DK // H, D_BR // H
    log_f = jax.nn.log_sigmoid((f_lo @ f_up + f_b).astype(f32)) / GLA_TAU
    ch = lambda t, d: t.astype(f32).reshape(B, nC, C, H, d)
    q = ch(q, dk) * dk ** -0.5
    k = ch(k, dk)
    v = ch(v, dv)
    b = jnp.cumsum(ch(log_f, dk), axis=2)
    b_last = b[:, :, -1:]
    q_d = q * jnp.exp(b)
    k_d = k * jnp.exp(-b)
    k_end = k * jnp.exp(b_last - b)
    causal = jnp.tril(jnp.ones((C, C), dtype=bool))
    att = jnp.where(causal, jnp.einsum('bnthd,bnshd->bnhts', q_d, k_d), 0.0)
    o = jnp.einsum('bnhts,bnshv->bnthv', att, v)
    ds = jnp.einsum('bnshd,bnshv->bnhdv', k_end, v)
    dec = jnp.exp(b_last[:, :, 0])

    def step(state, inp):
        dec_n, ds_n = inp
        return state * dec_n[..., None] + ds_n, state

    _, s_prev = lax.scan(step, jnp.zeros((B, H, dk, dv), f32), (jnp.moveaxis(dec, 1, 0), jnp.moveaxis(ds, 1, 0)))
    s_prev = jnp.moveaxis(s_prev, 0, 1)
    o = o + jnp.einsum('bnthd,bnhdv->bnthv', q_d, s_prev)
    o = o * lax.rsqrt(jnp.mean(o * o, -1, keepdims=True) + NORM_EPS)
    return o.reshape(B, S, D_BR) * norm_g.astype(f32)


def _retention(q, k, v, positions, gn_g):
    B, S, _ = q.shape
    f32 = jnp.float32
    C, H = RET_CHUNK, RET_HEADS
    nC = S // C
    dk, dv = RET_DK // H, D_BR // H
    q = _rope(q.astype(f32).reshape(B, S, H, dk), positions)
    k = _rope(k.astype(f32).reshape(B, S, H, dk), positions) * dk ** -0.5
    q = q.reshape(B, nC, C, H, dk)
    k = k.reshape(B, nC, C, H, dk)
    v = v.astype(f32).reshape(B, nC, C, H, dv)
    log_g = jnp.log(1.0 - jnp.exp2(-5.0 - jnp.arange(H, dtype=f32)))
    idx = jnp.arange(C, dtype=f32)
    diff = idx[:, None] - idx[None, :]
    dmat = jnp.where(diff >= 0, jnp.exp(jnp.maximum(diff, 0.0)[None] * log_g[:, None, None]), 0.0)
    att = jnp.einsum('bnthd,bnshd->bnhts', q, k) * dmat
    o = jnp.einsum('bnhts,bnshv->bnthv', att, v)
    k_w = jnp.exp((C - 1.0 - idx)[:, None] * log_g)
    q_w = jnp.exp((idx + 1.0)[:, None] * log_g)
    ds = jnp.einsum('bnshd,sh,bnshv->bnhdv', k, k_w, v)
    chunk_dec = jnp.exp(C * log_g)[None, :, None, None]

    def step(state, ds_n):
        return state * chunk_dec + ds_n, state

    _, s_prev = lax.scan(step, jnp.zeros((B, H, dk, dv), f32), jnp.moveaxis(ds, 1, 0))
    s_prev = jnp.moveaxis(s_prev, 0, 1)
    o = o + jnp.einsum('bnthd,bnhdv->bnthv', q, s_prev) * q_w[:, :, None]
    mean = jnp.mean(o, -1, keepdims=True)
    var = jnp.mean(jnp.square(o - mean), -1, keepdims=True)
    o = (o - mean) * lax.rsqrt(var + NORM_EPS)
    return o.reshape(B, S, D_BR) * gn_g.astype(f32)


def _rglru(xb, conv_w, conv_b, wa, ba, wx, bx, lam):
    B, S, _ = xb.shape
    f32 = jnp.float32
    xc = lax.conv_general_dilated(xb, conv_w[:, None, :], window_strides=(1,), padding=[(LRU_CONV - 1, 0)],
                                  dimension_numbers=('NWC', 'WIO', 'NWC'), feature_group_count=D_BR) + conv_b
    xh = xc.reshape(B, S, LRU_BLOCKS, LRU_BW)
    r = jax.nn.sigmoid((jnp.einsum('bsnc,ncd->bsnd', xh, wa).reshape(B, S, D_BR) + ba).astype(f32))
    i = jax.nn.sigmoid((jnp.einsum('bsnc,ncd->bsnd', xh, wx).reshape(B, S, D_BR) + bx).astype(f32))
    log_a = -LRU_C * r * jax.nn.softplus(-lam.astype(f32))
    a = jnp.exp(log_a)
    u = jnp.sqrt(-jnp.expm1(2.0 * log_a)) * (i * xc.astype(f32))

    def combine(left, right):
        a1, b1 = left
        a2, b2 = right
        return a1 * a2, a2 * b1 + b2

    _, h = lax.associative_scan(combine, (a, u), axis=1)
    return h


def _mixer_sublayer(un, positions, w_in, rw_mu, rw_w0, rw_w2, rw_a0, rw_a2, rw_k_k, rw_k_a, rw_r_k,
                    rw_ln_g, rw_ln_b, gla_f_up, gla_f_b, gla_norm_g, ret_gn_g, lru_conv_w, lru_conv_b,
                    lru_wa, lru_ba, lru_wx, lru_bx, lru_lambda, w_branch, w_out):
    B, S, _ = un.shape
    f32 = jnp.float32
    z = un @ w_in
    p = dict(zip([n for n, _ in IN_SPLITS], _split_last(z, [w for _, w in IN_SPLITS])))
    mu_r, mu_k, mu_v, mu_w, mu_a = _split_last(rw_mu, [D_BR, D_BR, D_BR, RW_RANK_W, RW_RANK_A])

    y_rw = _rwkv7(_token_shift(p['rw_r'], mu_r), _token_shift(p['rw_k'], mu_k), _token_shift(p['rw_v'], mu_v),
                  _token_shift(p['rw_wlo'], mu_w), _token_shift(p['rw_alo'], mu_a),
                  rw_w0, rw_w2, rw_a0, rw_a2, rw_k_k, rw_k_a, rw_r_k, rw_ln_g, rw_ln_b)
    y_rw = y_rw * jax.nn.silu(p['rw_g'].astype(f32))
    y_gla = _gla(p['gla_q'], p['gla_k'], p['gla_v'], p['gla_flo'], gla_f_up, gla_f_b, gla_norm_g)
    y_gla = y_gla * jax.nn.silu(p['gla_g'].astype(f32))
    y_ret = _retention(p['ret_q'], p['ret_k'], p['ret_v'], positions, ret_gn_g)
    y_ret = y_ret * jax.nn.silu(p['ret_g'].astype(f32))
    y_lru = _rglru(p['lru_x'], lru_conv_w, lru_conv_b, lru_wa, lru_ba, lru_wx, lru_bx, lru_lambda)
    y_lru = y_lru * jax.nn.silu(p['lru_g'].astype(f32))

    gates = p['merge'].reshape(B, S, N_BRANCH, D_MODEL)
    branches = (y_rw, y_gla, y_ret, y_lru)
    merged = jnp.zeros_like(un)
    for n in range(N_BRANCH):
        merged = merged + jax.nn.sigmoid(gates[:, :, n]) * (branches[n].astype(un.dtype) @ w_branch[n])
    return merged @ w_out


def _cross_attn(xn, mn, wq, wkv, wo):
    B, S, _ = xn.shape
    M = mn.shape[1]
    q = (xn @ wq).reshape(B, S, XA_HEADS, XA_HD)
    k, v = jnp.split(mn @ wkv, 2, axis=-1)
    k = k.reshape(B, M, XA_HEADS, XA_HD)
    v = v.reshape(B, M, XA_HEADS, XA_HD)
    s = jnp.einsum('bshd,bmhd->bhsm', q, k).astype(jnp.float32) * XA_HD ** -0.5
    pr = jax.nn.softmax(s, axis=-1).astype(xn.dtype)
    o = jnp.einsum('bhsm,bmhd->bshd', pr, v).reshape(B, S, D_MODEL)
    return o @ wo


def setup_inputs(seed: int = 0) -> dict:
    key = jax.random.key(seed)
    k = jax.random.split(key, 40)
    f32 = jnp.float32
    nrm = lambda kk, shape, scale: scale * jax.random.normal(kk, shape, f32)
    gain = lambda kk, shape: 1.0 + 0.02 * jax.random.normal(kk, shape, f32)
    L = DEPTH
    x = jax.random.normal(k[0], (BATCH, SEQ, D_MODEL), f32)
    mem = jax.random.normal(k[1], (BATCH, N_MEM, D_MODEL), f32)
    positions = (jnp.arange(SEQ, dtype=jnp.int32)[None, :]
                 + jax.random.randint(k[2], (BATCH, 1), 0, 1024, dtype=jnp.int32))
    a_init = jax.random.uniform(k[3], (L, D_BR), f32, 0.9, 0.999)
    s_init = a_init ** (1.0 / LRU_C)
    lru_lambda = jnp.log(s_init) - jnp.log1p(-s_init)
    return {
        'x': x,
        'mem': mem,
        'positions': positions,
        'mix_norm_g': gain(k[4], (L, D_MODEL)),
        'w_in': nrm(k[5], (L, D_MODEL, N_IN), D_MODEL ** -0.5),
        'rw_mu': jax.random.uniform(k[6], (L, RW_SHIFT), f32),
        'rw_w0': nrm(k[7], (L, D_BR), 0.5),
        'rw_w2': nrm(k[8], (L, RW_RANK_W, D_BR), RW_RANK_W ** -0.5),
        'rw_a0': nrm(k[9], (L, D_BR), 0.5),
        'rw_a2': nrm(k[10], (L, RW_RANK_A, D_BR), RW_RANK_A ** -0.5),
        'rw_k_k': 1.0 + nrm(k[11], (L, D_BR), 0.1),
        'rw_k_a': 1.0 + nrm(k[12], (L, D_BR), 0.1),
        'rw_r_k': nrm(k[13], (L, RW_HEADS, RW_HD), 0.1),
        'rw_ln_g': gain(k[14], (L, D_BR)),
        'rw_ln_b': nrm(k[15], (L, D_BR), 0.01),
        'gla_f_up': nrm(k[16], (L, GLA_RANK, GLA_DK), GLA_RANK ** -0.5),
        'gla_f_b': nrm(k[17], (L, GLA_DK), 0.1),
        'gla_norm_g': gain(k[18], (L, D_BR)),
        'ret_gn_g': gain(k[19], (L, D_BR)),
        'lru_conv_w': nrm(k[20], (L, LRU_CONV, D_BR), LRU_CONV ** -0.5),
        'lru_conv_b': nrm(k[21], (L, D_BR), 0.01),
        'lru_wa': nrm(k[22], (L, LRU_BLOCKS, LRU_BW, LRU_BW), LRU_BW ** -0.5),
        'lru_ba': nrm(k[23], (L, D_BR), 0.01),
        'lru_wx': nrm(k[24], (L, LRU_BLOCKS, LRU_BW, LRU_BW), LRU_BW ** -0.5),
        'lru_bx': nrm(k[25], (L, D_BR), 0.01),
        'lru_lambda': lru_lambda,
        'w_branch': nrm(k[26], (L, N_BRANCH, D_BR, D_MODEL), D_BR ** -0.5),
        'w_out': nrm(k[27], (L, D_MODEL, D_MODEL), D_MODEL ** -0.5),
        'xa_norm_g': gain(k[28], (L, D_MODEL)),
        'xa_mem_norm_g': gain(k[29], (L, D_MODEL)),
        'xa_wq': nrm(k[30], (L, D_MODEL, D_MODEL), D_MODEL ** -0.5),
        'xa_wkv': nrm(k[31], (L, D_MODEL, 2 * D_MODEL), D_MODEL ** -0.5),
        'xa_wo': nrm(k[32], (L, D_MODEL, D_MODEL), D_MODEL ** -0.5),
        'final_norm_g': gain(k[33], (D_MODEL,)),
    }


def reference(x, mem, positions, mix_norm_g, w_in, rw_mu, rw_w0, rw_w2, rw_a0, rw_a2, rw_k_k, rw_k_a,
              rw_r_k, rw_ln_g, rw_ln_b, gla_f_up, gla_f_b, gla_norm_g, ret_gn_g, lru_conv_w, lru_conv_b,
              lru_wa, lru_ba, lru_wx, lru_bx, lru_lambda, w_branch, w_out, xa_norm_g, xa_mem_norm_g,
              xa_wq, xa_wkv, xa_wo, final_norm_g):
    h = x
    for l in range(DEPTH):
        un = _rmsnorm(h, mix_norm_g[l])
        h = h + _mixer_sublayer(un, positions, w_in[l], rw_mu[l], rw_w0[l], rw_w2[l], rw_a0[l], rw_a2[l],
                                rw_k_k[l], rw_k_a[l], rw_r_k[l], rw_ln_g[l], rw_ln_b[l], gla_f_up[l],
                                gla_f_b[l], gla_norm_g[l], ret_gn_g[l], lru_conv_w[l], lru_conv_b[l],
                                lru_wa[l], lru_ba[l], lru_wx[l], lru_bx[l], lru_lambda[l], w_branch[l], w_out[l])
        xn = _rmsnorm(h, xa_norm_g[l])
        mn = _rmsnorm(mem, xa_mem_norm_g[l])
        h = h + _cross_attn(xn, mn, xa_wq[l], xa_wkv[l], xa_wo[l])
    return _rmsnorm(h, final_norm_g)
```

```python
import numpy as np
from contextlib import ExitStack
import concourse.bass as bass
import concourse.mybir as mybir
from concourse.bass_utils import run_bass_kernel_spmd

F32 = mybir.dt.float32
BF16 = mybir.dt.bfloat16
I32 = mybir.dt.int32
ALU = mybir.AluOpType
AF = mybir.ActivationFunctionType
AX = mybir.AxisListType

D = 1024
KC = 8
NMEM = 256
DBR = 512
NORM_EPS = 1e-6
RW_LN_EPS = 64e-5
RET_G = [1.0 - 2.0 ** (-5.0 - h) for h in range(4)]

_src = {}
_o = 0
for _n, _w in (('rw_r', 512), ('rw_k', 512), ('rw_v', 512), ('rw_wlo', 64), ('rw_alo', 64), ('rw_g', 512),
               ('gla_q', 256), ('gla_k', 256), ('gla_v', 512), ('gla_flo', 16), ('gla_g', 512),
               ('ret_q', 256), ('ret_k', 256), ('ret_v', 512), ('ret_g', 512),
               ('lru_x', 512), ('lru_g', 512), ('merge', 4096)):
    _src[_n] = (_o, _w)
    _o += _w
N_IN = _o
MU_OFF = {'rw_r': 0, 'rw_k': 512, 'rw_v': 1024, 'rw_wlo': 1536, 'rw_alo': 1600}
AUG = []
for _n in ('rw_r', 'rw_k', 'rw_v', 'rw_wlo', 'rw_alo'):
    AUG.append((_n + '_c', _src[_n][1], _src[_n][0], 'mu1m', MU_OFF[_n]))
    AUG.append((_n + '_p', _src[_n][1], _src[_n][0], 'mu', MU_OFF[_n]))
for _j in range(4):
    AUG.append(('lru_x%d' % _j, 512, _src['lru_x'][0], 'conv', _j))
for _n in ('rw_g', 'gla_q', 'gla_k', 'gla_v', 'gla_flo', 'gla_g', 'ret_q', 'ret_k', 'ret_v', 'ret_g', 'lru_g'):
    AUG.append((_n, _src[_n][1], _src[_n][0], 'plain', 0))
AUG.append(('ret_qs', 256, _src['ret_q'][0], 'swap', 0))
AUG.append(('ret_ks', 256, _src['ret_k'][0], 'swap', 0))
for _j in range(8):
    AUG.append(('merge%d' % _j, 512, _src['merge'][0] + 512 * _j, 'plain', 0))
AUGOFF = {}
_o = 0
for _a in AUG:
    AUGOFF[_a[0]] = _o
    _o += _a[1]
NAUG = _o


class Res:
    __slots__ = ('w', 'r', 'ds')

    def __init__(self):
        self.w = None
        self.r = {}
        self.ds = None


class KB:
    def __init__(self, nc, es):
        self.nc = nc
        self.es = es
        self.eng = dict(pe=nc.tensor, dve=nc.vector, act=nc.scalar, pool=nc.gpsimd, sp=nc.sync)
        self.semh = {}
        self.cnt = {}
        for e in self.eng:
            self.semh[e] = es.enter_context(nc.semaphore('c_' + e))
            self.cnt[e] = 0
        self.seen = {e: {} for e in self.eng}
        self.nd = 0
        self.strict = True
        self.ninst = 0
        self.scope = es

    def new_scope(self):
        keys = list(self.cnt.keys())
        for e in self.eng:
            need = {k: (self.cnt[k], self.cnt[k]) for k in keys if k != e and self.cnt[k] > 0}
            self._waits(e, need)
        for e in self.eng:
            self.emit(e, lambda en: en.engine_nop() if hasattr(en, 'engine_nop') else en.nop())
        for e in self.eng:
            need = {k: (self.cnt[k], self.cnt[k]) for k in self.eng if k != e}
            self._waits(e, need)
        if self.scope is not self.es:
            self.scope.close()
        self.scope = ExitStack()

    def _need(self, reads, writes):
        need = {}

        def add(k, v, raw):
            a, b = need.get(k, (0, 0))
            need[k] = (max(a, v), max(b, v) if raw else b)
        for r in reads:
            if r.w is not None:
                add(r.w[0], r.w[1], True)
        for w in writes:
            if w.w is not None:
                add(w.w[0], w.w[1], False)
            for k, v in w.r.items():
                add(k, v, False)
        return need

    def _waits(self, e, need):
        eng = self.eng[e]
        seen = self.seen[e]
        for k, vv in need.items():
            v, vraw = vv if isinstance(vv, tuple) else (vv, vv)
            if k == e:
                if e == 'pe' or not self.strict:
                    continue
                pass
            if k[0] == 'd' and k[1:].isdigit():
                v = self.cnt[k]
            if seen.get(k, 0) >= v:
                continue
            eng.wait_ge(self.semh[k], v)
            seen[k] = v
            self.ninst += 1

    def emit(self, e, fn, reads=(), writes=()):
        self._waits(e, self._need(reads, writes))
        ins = fn(self.eng[e])
        self.cnt[e] += 1
        ins.then_inc(self.semh[e], 1)
        t = (e, self.cnt[e])
        for r in reads:
            if r.r.get(e, 0) < t[1]:
                r.r[e] = t[1]
        for w in writes:
            w.w = t
            w.r = {}
        self.ninst += 1
        return ins

    def _dsem(self, res, q):
        if res.ds is None:
            res.ds = {}
        if q not in res.ds:
            k = 'd%d' % self.nd
            self.nd += 1
            self.semh[k] = self.es.enter_context(self.nc.semaphore(k))
            self.cnt[k] = 0
            res.ds[q] = k
        return res.ds[q]

    def dma(self, q, out, in_, sb, reads=(), writes=(), batch=False):
        k = self._dsem(sb, q)
        need = self._need(reads, writes)
        if not batch and self.cnt[k] > 0:
            need[k] = (self.cnt[k], self.cnt[k])
        self._waits(q, need)
        ins = self.eng[q].dma_start(out=out, in_=in_)
        self.cnt[k] += 16
        ins.then_inc(self.semh[k], 16)
        t = (k, self.cnt[k])
        for r in reads:
            if r.r.get(k, 0) < t[1]:
                r.r[k] = t[1]
        for w in writes:
            w.w = t
            w.r = {}
        self.ninst += 1
        return ins

    def wait_all(self, e, ress):
        need = {}
        for r in ress:
            if r.w is not None:
                k, v = r.w
                need[k] = max(need.get(k, 0), v)
            for k, v in r.r.items():
                need[k] = max(need.get(k, 0), v)
        self._waits(e, {k: (v, v) for k, v in need.items()})


class T:
    def __init__(self, kb, name, shape, dt, psum=False):
        nc = kb.nc
        if psum:
            self.t = kb.es.enter_context(nc.psum_tensor(name, shape, dt))
        else:
            self.t = kb.scope.enter_context(nc.sbuf_tensor(name, shape, dt))
        self.r = Res()

    def __getitem__(self, k):
        return self.t[k]


class _View3:
    def __init__(self, t):
        self.t, self.r = t, t.r

    def __getitem__(self, k):
        if k == slice(None):
            return self.t[:, 0:2, :]
        return self.t[k]


class _ViewC:
    def __init__(self, t, fn):
        self.t, self.r, self.v = t, t.r, fn(t)

    def __getitem__(self, k):
        return self.v[k]


class _View:
    def __init__(self, t, sl):
        self.t, self.sl, self.r = t, sl, t.r

    def __getitem__(self, k):
        a, b, c = k
        assert c == slice(None)
        return self.t[a, b, self.sl]


def build(S, DEPTH, SEG=512, TT=512, MIX=('rw', 'gla', 'ret', 'lru'), XA=True):
    nc = bass.Bass("TRN2", target_bir_lowering=False)
    es = ExitStack()
    es.enter_context(nc.allow_non_contiguous_dma(reason="small per-channel vectors / strided layouts"))
    kb = KB(nc, es)
    NSEG = S // SEG
    NT = SEG // TT
    L = DEPTH

    def din(name, shape, dt=F32):
        return nc.dram_tensor(name, list(shape), dt, kind="ExternalInput").ap()

    x = din("x", [S, D])
    out = nc.dram_tensor("out", [S, D], F32, kind="ExternalOutput").ap()
    final_norm_g = din("final_norm_g", [D])
    hT = nc.dram_tensor("hT", [D, S], F32).ap()
    hT_r = Res()

    ident = T(kb, "ident", [128, 128], F32)
    ones_f = T(kb, "ones_f", [128, 128], F32)
    kb.emit('pool', lambda e: e.memset(ones_f[:], 1.0), writes=[ones_f.r])
    kb.emit('pool', lambda e: e.memset(ident[:], 1.0), writes=[ident.r])
    kb.emit('pool', lambda e: e.affine_select(out=ident[:], in_=ident[:], pattern=[[1, 128]], base=0,
                                              channel_multiplier=-1, compare_op=ALU.is_equal, fill=0.0),
            reads=[ident.r], writes=[ident.r])

    eps_t = T(kb, "eps_t", [128, 4], F32)
    kb.emit('pool', lambda e: e.memset(eps_t[:, 0:1], NORM_EPS), writes=[eps_t.r])
    kb.emit('pool', lambda e: e.memset(eps_t[:, 1:2], RW_LN_EPS), writes=[eps_t.r])
    kb.emit('pool', lambda e: e.memset(eps_t[:, 2:3], 1.0), writes=[eps_t.r])
    kb.emit('pool', lambda e: e.memset(eps_t[:, 3:4], 0.0), writes=[eps_t.r])
    ones_b = T(kb, "ones_b", [128, 128], BF16)
    kb.emit('pool', lambda e: e.memset(ones_b[:], 1.0), writes=[ones_b.r])
    sq = T(kb, "sq", [128, TT], F32)
    rstd = T(kb, "rstd", [128, TT], F32)
    PS = [T(kb, "ps%d" % i, [128, 512], F32, psum=True) for i in range(8)]
    kb.PS = PS
    psi = [0]

    def ps():
        p = PS[psi[0] % 8]
        psi[0] += 1
        return p

    _tl = {}

    def TL(name, shape, dt):
        if name not in _tl:
            _tl[name] = T(kb, name, shape, dt)
        return _tl[name]

    def load_vec_fm(name, ap1d, n=D):
        t = TL(name, [128, n // 128], F32)
        kb.dma('sp', t[:], ap1d.rearrange("(k p) -> p k", p=128), t.r, writes=[t.r])
        return t

    fng = load_vec_fm("fng", final_norm_g)

    kb.new_scope()
    xin = [T(kb, "xin%d" % i, [128, D], F32) for i in range(2)]
    xtr = [T(kb, "xtr%d" % i, [128, KC, 128], F32) for i in range(2)]
    hT_v = hT.rearrange("(k p) s -> p k s", p=128)
    for tb in range(S // 128):
        xi = xin[tb % 2]
        xo = xtr[tb % 2]
        kb.dma('sp', xi[:], x[tb * 128:(tb + 1) * 128, :], xi.r, writes=[xi.r])
        for half in range(2):
            p = ps()
            for kk in range(4):
                k = half * 4 + kk
                kb.emit('pe', lambda e, p=p, kk=kk, k=k: e.transpose(out=p[:, kk * 128:(kk + 1) * 128],
                                                                     in_=xi[:, k * 128:(k + 1) * 128], identity=ident[:]),
                        reads=[xi.r, ident.r], writes=[p.r])
            eng = 'act' if half else 'dve'
            if eng == 'act':
                kb.emit('act', lambda e, p=p, half=half: e.activation(
                    out=xo[:, half * 4:(half + 1) * 4, :].rearrange("p k s -> p (k s)"), in_=p[:], func=AF.Copy),
                        reads=[p.r], writes=[xo.r])
            else:
                kb.emit('dve', lambda e, p=p, half=half: e.tensor_copy(
                    out=xo[:, half * 4:(half + 1) * 4, :].rearrange("p k s -> p (k s)"), in_=p[:]),
                        reads=[p.r], writes=[xo.r])
        kb.dma('pool', hT_v[:, :, tb * 128:(tb + 1) * 128], xo[:], xo.r, reads=[xo.r], writes=[hT_r])


    def rmsnorm_fm(hs, g, dst_fn, dst_res):
        p = ps()
        for k in range(KC):
            kb.emit('act', lambda e, k=k: e.activation(out=sq[:], in_=hs[:, k, :], func=AF.Square),
                    reads=[hs.r], writes=[sq.r])
            kb.emit('pe', lambda e, k=k, p=p: e.matmul(p[:], lhsT=ones_f[:], rhs=sq[:], start=(k == 0), stop=(k == KC - 1)),
                    reads=[sq.r, ones_f.r], writes=[p.r])
        kb.emit('act', lambda e, p=p: e.activation(out=rstd[:], in_=p[:], func=AF.Ln, scale=1.0 / D, bias=eps_t[:, 0:1]),
                reads=[p.r, eps_t.r], writes=[rstd.r])
        kb.emit('act', lambda e: e.activation(out=rstd[:], in_=rstd[:], func=AF.Exp, scale=-0.5),
                reads=[rstd.r], writes=[rstd.r])
        for k in range(KC):
            kb.emit('dve', lambda e, k=k: e.scalar_tensor_tensor(out=dst_fn(k), in0=hs[:, k, :], scalar=g[:, k:k + 1],
                                                                 in1=rstd[:], op0=ALU.mult, op1=ALU.mult),
                    reads=[hs.r, g.r, rstd.r], writes=[dst_res])

    kb.new_scope()
    mem = din("mem", [NMEM, D])
    positions = din("positions", [S], I32)
    P = {}
    for nm, shp in (('mix_norm_g', [2, D]), ('w_in', [2, D, N_IN]), ('rw_mu', [2, 1664]), ('rw_w0', [2, 512]),
                    ('rw_w2', [2, 64, 512]), ('rw_a0', [2, 512]), ('rw_a2', [2, 64, 512]), ('rw_k_k', [2, 512]),
                    ('rw_k_a', [2, 512]), ('rw_r_k', [2, 8, 64]), ('rw_ln_g', [2, 512]), ('rw_ln_b', [2, 512]),
                    ('gla_f_up', [2, 16, 256]), ('gla_f_b', [2, 256]), ('gla_norm_g', [2, 512]), ('ret_gn_g', [2, 512]),
                    ('lru_conv_w', [2, 4, 512]), ('lru_conv_b', [2, 512]), ('lru_wa', [2, 8, 64, 64]), ('lru_ba', [2, 512]),
                    ('lru_wx', [2, 8, 64, 64]), ('lru_bx', [2, 512]), ('lru_lambda', [2, 512]),
                    ('w_branch', [2, 4, 512, D]), ('w_out', [2, D, D]), ('xa_norm_g', [2, D]), ('xa_mem_norm_g', [2, D]),
                    ('xa_wq', [2, D, D]), ('xa_wkv', [2, D, 2 * D]), ('xa_wo', [2, D, D])):
        P[nm] = din(nm, shp)

    def dscr(name, shape, dt=BF16):
        return nc.dram_tensor(name, list(shape), dt).ap()

    Wb = [dscr("Wb%d" % l, [D, NAUG]) for l in range(L)]
    WBR = [dscr("WBR%d" % l, [4 * 512, D]) for l in range(L)]
    WO = [dscr("WO%d" % l, [D, D]) for l in range(L)]
    WQ = [dscr("WQ%d" % l, [D, D]) for l in range(L)]
    WKV = [dscr("WKV%d" % l, [D, 2 * D]) for l in range(L)]
    WXO = [dscr("WXO%d" % l, [D, D]) for l in range(L)]
    wdram_r = Res()

    PW = 2048
    wld = [T(kb, "wld%d" % i, [128, PW], F32) for i in range(2)]
    wst = [T(kb, "wst%d" % i, [128, PW], BF16) for i in range(2)]
    scl = T(kb, "scl", [128, 512], F32)
    pi = [0]

    def prep(src, dst, rows, n, scale=None, swap=False):
        for rc in range(rows // 128):
            a = wld[pi[0] % 2]
            b = wst[pi[0] % 2]
            pi[0] += 1
            kb.dma('sp', a[:, 0:n], src[rc * 128:(rc + 1) * 128, :], a.r, writes=[a.r])
            if swap:
                av = a[:, 0:n].rearrange("p (h two d) -> p h two d", two=2, d=32)
                bv = b[:, 0:n].rearrange("p (h two d) -> p h two d", two=2, d=32)
                kb.emit('dve', lambda e: e.tensor_scalar(out=bv[:, :, 0, :], in0=av[:, :, 1, :], scalar1=-1.0, scalar2=None, op0=ALU.mult),
                        reads=[a.r], writes=[b.r])
                kb.emit('dve', lambda e: e.tensor_copy(out=bv[:, :, 1, :], in_=av[:, :, 0, :]), reads=[a.r, b.r], writes=[b.r])
            elif scale is None:
                if pi[0] % 2:
                    kb.emit('act', lambda e: e.activation(out=b[:, 0:n], in_=a[:, 0:n], func=AF.Copy),
                            reads=[a.r], writes=[b.r])
                else:
                    kb.emit('dve', lambda e: e.tensor_copy(out=b[:, 0:n], in_=a[:, 0:n]), reads=[a.r], writes=[b.r])
            else:
                kb.emit('dve', lambda e: e.tensor_tensor(out=b[:, 0:n], in0=a[:, 0:n], in1=scale[:, 0:n], op=ALU.mult),
                        reads=[a.r, scale.r], writes=[b.r])
            kb.dma('pool', dst[rc * 128:(rc + 1) * 128, :], b[:, 0:n], b.r, reads=[b.r], writes=[wdram_r])

    memt = T(kb, "memt", [128, 2, D], F32)
    memT32 = T(kb, "memT32", [128, KC, NMEM], F32)
    memT_d = nc.dram_tensor("memT_d", [D, NMEM], F32).ap()
    memT_r = Res()
    kb.dma('sp', memt[:], mem.rearrange("(b p) d -> p b d", p=128), memt.r, writes=[memt.r])
    for bk in range(2):
        for half in range(2):
            p = ps()
            for kk in range(4):
                k = half * 4 + kk
                kb.emit('pe', lambda e, p=p, kk=kk, k=k, bk=bk: e.transpose(out=p[:, kk * 128:(kk + 1) * 128],
                                                                            in_=memt[:, bk, k * 128:(k + 1) * 128], identity=ident[:]),
                        reads=[memt.r, ident.r], writes=[p.r])
            kb.emit('dve', lambda e, p=p, half=half, bk=bk: e.tensor_copy(
                out=memT32[:, half * 4:(half + 1) * 4, bk * 128:(bk + 1) * 128],
                in_=p[:].rearrange("p (k s) -> p k s", k=4)), reads=[p.r], writes=[memT32.r])


    kb.dma('pool', memT_d.rearrange("(k p) m -> p k m", p=128), memT32[:], memT32.r, reads=[memT32.r], writes=[memT_r])
    for l in range(L):
        for (an, w, so, kind, arg) in AUG:
            src = P['w_in'][l, :, so:so + w]
            dst = Wb[l][:, AUGOFF[an]:AUGOFF[an] + w]
            if kind == 'plain':
                prep(src, dst, D, w)
            elif kind == 'swap':
                prep(src, dst, D, w, swap=True)
            else:
                if kind in ('mu', 'mu1m'):
                    row = P['rw_mu'][l, arg:arg + w]
                else:
                    row = P['lru_conv_w'][l, arg, :]
                kb.dma('sp', scl[:, 0:w], row.partition_broadcast(128), scl.r, writes=[scl.r])
                if kind == 'mu1m':
                    kb.emit('dve', lambda e, w=w: e.tensor_scalar(out=scl[:, 0:w], in0=scl[:, 0:w], scalar1=-1.0, scalar2=1.0,
                                                                   op0=ALU.mult, op1=ALU.add), reads=[scl.r], writes=[scl.r])
                prep(src, dst, D, w, scale=scl)
        prep(P['w_branch'][l].rearrange("n r c -> (n r) c"), WBR[l], 4 * 512, D)
        prep(P['w_out'][l], WO[l], D, D)
        prep(P['xa_wq'][l], WQ[l], D, D)
        prep(P['xa_wkv'][l], WKV[l], D, 2 * D)
        prep(P['xa_wo'][l], WXO[l], D, D)

    kb.new_scope()
    _tl.clear()
    kT = T(kb, "kT", [128, KC, NMEM], BF16)
    vtm = T(kb, "vtm", [128, 2, D], BF16)
    hseg = T(kb, "hseg", [128, KC, SEG], F32)
    un = T(kb, "un", [128, KC, 3 + SEG], BF16)
    halo = T(kb, "halo", [128, KC, 3], BF16)
    yg = [T(kb, "yg%d" % n, [128, 4, SEG], BF16) for n in range(4)]
    merged = T(kb, "merged", [128, KC, SEG], BF16)
    NW = 3
    NC8_ = TT // 64
    twb = T(kb, "twb", [64, TT], BF16)
    alb = T(kb, "alb", [64, TT], BF16)
    AR = T(kb, "AR", [128, NC8_, 128], BF16)
    BK32 = T(kb, "BK32", [128, 2, TT], F32)
    BKb = T(kb, "BKb", [128, 2, TT], BF16)
    VT = T(kb, "VT", [128, NC8_, 64], BF16)
    BTm = T(kb, "BTm", [128, NC8_, 64], BF16)
    KTm = T(kb, "KTm", [128, NC8_, 64], BF16)
    Mall = T(kb, "Mall", [128, NC8_, 320], BF16)
    AN = [T(kb, "AN%d" % i, [128, NC8_, 128], BF16) for i in range(2)]
    TTb = T(kb, "TTb", [128, NC8_, 64], BF16)
    vb = T(kb, "vb", [128, TT], BF16)
    Xb = T(kb, "Xb", [128, 64], BF16)
    Ub = T(kb, "Ub", [128, 64], BF16)
    gC = T(kb, "gC", [128, NC8_], F32)
    def _interleave(gens):
        gens = [g for g in gens if g is not None]
        while gens:
            for g in list(gens):
                try:
                    next(g)
                except StopIteration:
                    gens.remove(g)
    identb = T(kb, "identb", [128, 1, 64], BF16)
    kb.emit('dve', lambda e: e.tensor_copy(out=identb[0:64, 0, :], in_=ident[0:64, 0:64]), reads=[ident.r], writes=[identb.r])
    kb.emit('dve', lambda e: e.tensor_copy(out=identb[64:128, 0, :], in_=ident[64:128, 64:128]), reads=[ident.r], writes=[identb.r])
    ones_bd = T(kb, "ones_bd", [128, 128], F32)
    kb.emit('pool', lambda e: e.memset(ones_bd[:], 0.0), writes=[ones_bd.r])
    kb.emit('pool', lambda e: e.memset(ones_bd[0:64, 0:64], 1.0), writes=[ones_bd.r])
    kb.emit('pool', lambda e: e.memset(ones_bd[64:128, 64:128], 1.0), writes=[ones_bd.r])
    maskR = T(kb, "maskR", [128, 320], F32)
    kb.emit('pool', lambda e: e.memset(maskR[:], 1.0), writes=[maskR.r])
    for hv_ in (slice(0, 64), slice(64, 128)):
        for (c0_, strict, transposed) in ((0, True, False), (64, False, False), (128, True, False), (192, False, False), (256, True, True)):
            kb.emit('pool', lambda e, c0_=c0_, strict=strict, transposed=transposed, hv_=hv_: e.affine_select(
                out=maskR[hv_, c0_:c0_ + 64], in_=maskR[hv_, c0_:c0_ + 64], pattern=[[-1 if transposed else 1, 64]], base=0,
                channel_multiplier=(1 if transposed else -1), compare_op=(ALU.is_gt if strict else ALU.is_ge), fill=0.0),
                    reads=[maskR.r], writes=[maskR.r])
    vT = [T(kb, "vT%d" % i, [128, 512], BF16) for i in range(TT // 128)]
    qd = T(kb, "qd", [64, 4, TT], BF16)
    kd = T(kb, "kd", [64, 4, TT], BF16)
    kd32 = T(kb, "kd32", [64, 4, TT], F32)
    kdT = T(kb, "kdT", [128, 256], BF16)
    attb = T(kb, "attb", [128, 512], BF16)
    oall = T(kb, "oall", [128, 4, TT], F32)
    ebl = T(kb, "ebl", [64, 4, TT // 128], F32)
    flo_b = T(kb, "flo_b", [16, TT], BF16)
    wflo = T(kb, "wflo", [128, KC, 16], BF16)
    posi = T(kb, "posi", [64, TT], I32)
    cosT = T(kb, "cosT", [128, TT], F32)
    sinT = T(kb, "sinT", [128, TT], F32)
    yT, bonus = cosT, sinT
    tmp2 = T(kb, "tmp2", [128, TT], F32)
    sgate = T(kb, "sgate", [128, TT], BF16)
    RWB = [(AR, BKb, VT, BTm, KTm, gC, sinT, sgate),
           (T(kb, "AR_b", [128, NC8_, 128], BF16), T(kb, "BKb_b", [128, 2, TT], BF16), T(kb, "VT_b", [128, NC8_, 64], BF16),
            T(kb, "BTm_b", [128, NC8_, 64], BF16), T(kb, "KTm_b", [128, NC8_, 64], BF16), T(kb, "gC_b", [128, NC8_], F32),
            T(kb, "bonus_b", [128, TT], F32), T(kb, "sgate_b", [128, TT], BF16))]

    mask4 = T(kb, "mask4", [128, 4, 128], F32)
    kb.emit('pool', lambda e: e.memset(mask4[:], 1.0), writes=[mask4.r])
    kb.emit('pool', lambda e: e.affine_select(out=mask4[:], in_=mask4[:], pattern=[[0, 4], [1, 128]], base=0, channel_multiplier=-1,
                                              compare_op=ALU.is_ge, fill=0.0), reads=[mask4.r], writes=[mask4.r])
    ebR = T(kb, "ebR", [64, 4, 128], F32)
    enbR = T(kb, "enbR", [64, 4, 128], F32)
    iot = T(kb, "iot", [64, 128], F32)
    invf = T(kb, "invf", [64, 1], F32)
    kb.emit('pool', lambda e: e.iota(iot[:], pattern=[[1, 128]], base=1, channel_multiplier=0, allow_small_or_imprecise_dtypes=True), writes=[iot.r])
    for h in range(4):
        lg = float(np.log(RET_G[h]))
        kb.emit('act', lambda e, h=h, lg=lg: e.activation(out=ebR[:, h, :], in_=iot[:], func=AF.Exp, scale=lg), reads=[iot.r], writes=[ebR.r])
        kb.emit('act', lambda e, h=h, lg=lg: e.activation(out=enbR[:, h, :], in_=iot[:], func=AF.Exp, scale=-lg), reads=[iot.r], writes=[enbR.r])
    kb.emit('dve', lambda e: e.tensor_scalar(out=enbR[:], in0=enbR[:], scalar1=0.125, scalar2=None, op0=ALU.mult), reads=[enbR.r], writes=[enbR.r])
    pidx = T(kb, "pidx", [64, 2], F32)
    kb.emit('pool', lambda e: e.iota(pidx[:, 0:1], pattern=[[0, 1]], base=0, channel_multiplier=1, allow_small_or_imprecise_dtypes=True), writes=[pidx.r])
    kb.emit('dve', lambda e: e.tensor_scalar(out=pidx[:, 1:2], in0=pidx[:, 0:1], scalar1=32.0, scalar2=-32.0, op0=ALU.is_ge, op1=ALU.mult), reads=[pidx.r], writes=[pidx.r])
    kb.emit('dve', lambda e: e.tensor_tensor(out=pidx[:, 0:1], in0=pidx[:, 0:1], in1=pidx[:, 1:2], op=ALU.add), reads=[pidx.r], writes=[pidx.r])
    kb.emit('act', lambda e: e.activation(out=invf[:], in_=pidx[:, 0:1], func=AF.Exp, scale=float(-np.log(10000.0) / 32.0)), reads=[pidx.r], writes=[invf.r])
    wt = [T(kb, "wt%d" % i, [128, KC, 512], BF16) for i in range(NW)]
    wi = [0]

    def wtile():
        t = wt[wi[0] % NW]
        wi[0] += 1
        return t

    def load_w(t, src2d, ncols, col0=0, kc=KC, batch=False):
        kb.dma('sp', t[:, 0:kc, col0:col0 + ncols], src2d.rearrange("(k p) c -> p k c", p=128), t.r,
               reads=[wdram_r], writes=[t.r], batch=batch)

    WK = [T(kb, "wk%d" % i, [128, TT], F32) for i in range(10)]
    WB16 = [T(kb, "wb%d" % i, [128, TT], BF16) for i in range(4)]

    def fm_proj(dst_ps, wtile_, c0, ncols, t0, shift=0, start=True, stop=True, kc=KC):
        for k in range(kc):
            kb.emit('pe', lambda e, k=k: e.matmul(dst_ps[0:ncols, :], lhsT=wtile_[:, k, c0:c0 + ncols],
                                                  rhs=un[:, k, 3 + t0 + shift:3 + t0 + shift + TT],
                                                  start=(start and k == 0), stop=(stop and k == kc - 1)),
                    reads=[wtile_.r, un.r], writes=[dst_ps.r])

    def rmsnorm_cols(src, g, dst, ncol):
        p = ps()
        for k in range(KC):
            kb.emit('act', lambda e, k=k: e.activation(out=sq[:, 0:ncol], in_=src[:, k, 0:ncol], func=AF.Square),
                    reads=[src.r], writes=[sq.r])
            kb.emit('pe', lambda e, k=k, p=p: e.matmul(p[:, 0:ncol], lhsT=ones_f[:], rhs=sq[:, 0:ncol], start=(k == 0), stop=(k == KC - 1)),
                    reads=[sq.r, ones_f.r], writes=[p.r])
        kb.emit('act', lambda e, p=p: e.activation(out=rstd[:, 0:ncol], in_=p[:, 0:ncol], func=AF.Ln, scale=1.0 / D, bias=eps_t[:, 0:1]),
                reads=[p.r, eps_t.r], writes=[rstd.r])
        kb.emit('act', lambda e: e.activation(out=rstd[:, 0:ncol], in_=rstd[:, 0:ncol], func=AF.Exp, scale=-0.5),
                reads=[rstd.r], writes=[rstd.r])
        for k in range(KC):
            kb.emit('dve', lambda e, k=k: e.scalar_tensor_tensor(out=dst[:, k, 0:ncol], in0=src[:, k, 0:ncol], scalar=g[:, k:k + 1],
                                                                 in1=rstd[:, 0:ncol], op0=ALU.mult, op1=ALU.mult),
                    reads=[src.r, g.r, rstd.r], writes=[dst.r])

    for l in range(L):
        mng = load_vec_fm("mng", P['mix_norm_g'][l])
        xng = load_vec_fm("xng", P['xa_norm_g'][l])
        xmg = load_vec_fm("xmg", P['xa_mem_norm_g'][l])
        mnT = _ViewC(merged, lambda t: t[:, :, 0:NMEM])
        mem32 = _ViewC(oall, lambda t: t[:].rearrange("p h (a t) -> p (h a) t", a=2))
        kb.dma('sp', mem32[:, :, :], memT_d.rearrange("(k p) m -> p k m", p=128), oall.r, reads=[memT_r], writes=[oall.r])
        rmsnorm_cols(mem32, xmg, mnT, NMEM)
        for j in range(KC):
            w = wtile()
            load_w(w, WKV[l][:, j * 128:(j + 1) * 128], 128)
            p = ps()
            for k in range(KC):
                kb.emit('pe', lambda e, k=k, p=p, w=w: e.matmul(p[:, 0:NMEM], lhsT=w[:, k, 0:128], rhs=mnT[:, k, :],
                                                                start=(k == 0), stop=(k == KC - 1)),
                        reads=[w.r, mnT.r], writes=[p.r])
            kb.emit('act', lambda e, p=p, j=j: e.activation(out=kT[:, j, :], in_=p[:, 0:NMEM], func=AF.Copy),
                    reads=[p.r], writes=[kT.r])
        for c2 in range(2):
            w = wtile()
            load_w(w, WKV[l][:, D + c2 * 512:D + (c2 + 1) * 512], 512)
            for bk in range(2):
                p = ps()
                for k in range(KC):
                    kb.emit('pe', lambda e, k=k, p=p, w=w, bk=bk: e.matmul(p[:], lhsT=mnT[:, k, bk * 128:(bk + 1) * 128], rhs=w[:, k, :],
                                                                           start=(k == 0), stop=(k == KC - 1)),
                            reads=[w.r, mnT.r], writes=[p.r])
                kb.emit('act', lambda e, p=p, bk=bk, c2=c2: e.activation(out=vtm[:, bk, c2 * 512:(c2 + 1) * 512], in_=p[:], func=AF.Copy),
                        reads=[p.r], writes=[vtm.r])

        if 'rw' in MIX:
            def hv(name, ap1d):
                t_ = TL(name, [128, 4], F32)
                kb.dma('sp', t_[:], ap1d.rearrange("(hp p) -> p hp", p=128), t_.r, writes=[t_.r])
                return t_
            w0t = hv("w0t", P['rw_w0'][l])
            a0t = hv("a0t", P['rw_a0'][l])
            kkt = hv("kkt", P['rw_k_k'][l])
            kat = hv("kat", P['rw_k_a'][l])
            rkt = hv("rkt", P['rw_r_k'][l].rearrange("h k -> (h k)"))
            lngt = hv("lngt", P['rw_ln_g'][l])
            lnbt = hv("lnbt", P['rw_ln_b'][l])
            w2b = TL("w2b", [64, 512], BF16)
            a2b = TL("a2b", [64, 512], BF16)
            for (nmw, dstb, stg) in (('rw_w2', w2b, WK[2]), ('rw_a2', a2b, WK[3])):
                kb.dma('sp', stg[0:64, :], P[nmw][l], stg.r, writes=[stg.r])
                kb.emit('dve', lambda e, dstb=dstb, stg=stg: e.tensor_copy(out=dstb[:], in_=stg[0:64, :]), reads=[stg.r], writes=[dstb.r])
            STs = [TL("ST_%d" % h_, [128, 64], BF16) for h_ in range(4)]
            for t_ in STs:
                kb.emit('pool', lambda e, t_=t_: e.memset(t_[:], 0.0), writes=[t_.r])
        if 'gla' in MIX or 'ret' in MIX:
            fup32 = TL("fup32", [16, 256], F32)
            fup_b = T(kb, "fup_b%d" % l, [16, 256], BF16)
            kb.dma('sp', fup32[:], P['gla_f_up'][l], fup32.r, writes=[fup32.r])
            kb.emit('dve', lambda e: e.tensor_copy(out=fup_b[:], in_=fup32[:]), reads=[fup32.r], writes=[fup_b.r])
            nfb = TL("nfb", [64, 4], F32)
            kb.dma('sp', nfb[:], P['gla_f_b'][l].rearrange("(h d) -> d h", d=64), nfb.r, writes=[nfb.r])
            kb.emit('dve', lambda e: e.tensor_scalar(out=nfb[:], in0=nfb[:], scalar1=-1.0, scalar2=None, op0=ALU.mult), reads=[nfb.r], writes=[nfb.r])
            gng = load_vec_fm("gng", P['gla_norm_g'][l], 512)
            rgg = load_vec_fm("rgg", P['ret_gn_g'][l], 512)
            Sg32 = TL("Sg32", [64, 4, 128], F32)
            Sgb = TL("Sgb", [64, 4, 128], BF16)
            Sr32 = TL("Sr32", [64, 4, 128], F32)
            Srb = TL("Srb", [64, 4, 128], BF16)
            for t_ in (Sg32, Sgb, Sr32, Srb):
                kb.emit('pool', lambda e, t_=t_: e.memset(t_[:], 0.0), writes=[t_.r])
        if 'lru' in MIX:
            lcb = load_vec_fm("lcb", P['lru_conv_b'][l], 512)
            lba = load_vec_fm("lba", P['lru_ba'][l], 512)
            lbx = load_vec_fm("lbx", P['lru_bx'][l], 512)
            llam = load_vec_fm("llam", P['lru_lambda'][l], 512)
            lc = TL("lc", [128, 4], F32)
            lc2 = TL("lc2", [128, 4], F32)
            kb.emit('act', lambda e: e.activation(out=lc[:], in_=llam[:], func=AF.Exp, scale=-1.0), reads=[llam.r], writes=[lc.r])
            kb.emit('act', lambda e: e.activation(out=lc[:], in_=lc[:], func=AF.Ln, bias=eps_t[:, 2:3]), reads=[lc.r, eps_t.r], writes=[lc.r])
            kb.emit('dve', lambda e: e.tensor_scalar(out=lc2[:], in0=lc[:], scalar1=-16.0, scalar2=None, op0=ALU.mult), reads=[lc.r], writes=[lc2.r])
            kb.emit('dve', lambda e: e.tensor_scalar(out=lc[:], in0=lc[:], scalar1=-8.0, scalar2=None, op0=ALU.mult), reads=[lc.r], writes=[lc.r])
            wab = TL("wab", [128, 2, 4, 128], BF16)
            for wi_, nmw in enumerate(('lru_wa', 'lru_wx')):
                stg = WK[wi_]
                sv = stg[:].rearrange("p (j o) -> p j o", j=4)
                kb.emit('pool', lambda e, stg=stg: e.memset(stg[:], 0.0), writes=[stg.r])
                for bk in range(8):
                    j, hb = bk // 2, bk % 2
                    kb.dma('sp', sv[hb * 64:(hb + 1) * 64, j, hb * 64:(hb + 1) * 64], P[nmw][l, bk], stg.r,
                           writes=[stg.r], batch=(bk > 0))
                kb.emit('dve', lambda e, sv=sv, wi_=wi_: e.tensor_copy(out=wab[:, wi_, :, :], in_=sv), reads=[stg.r], writes=[wab.r])
            lcarry = TL("lcarry", [128, 4], F32)
            kb.emit('pool', lambda e: e.memset(lcarry[:], 0.0), writes=[lcarry.r])

        for sg in range(NSEG):
            s0 = sg * SEG
            kb.dma('sp', hseg[:], hT_v[:, :, s0:s0 + SEG], hseg.r, reads=[hT_r], writes=[hseg.r])
            if sg == 0:
                kb.emit('dve', lambda e: e.memset(un[:, :, 0:3], 0.0), writes=[un.r])
            else:
                kb.emit('dve', lambda e: e.tensor_copy(out=un[:, :, 0:3], in_=halo[:]), reads=[halo.r], writes=[un.r])
            for tt in range(NT):
                hv = _View(hseg, slice(tt * TT, (tt + 1) * TT))
                rmsnorm_fm(hv, mng, lambda k, tt=tt: un[:, k, 3 + tt * TT:3 + (tt + 1) * TT], un.r)
            kb.emit('dve', lambda e: e.tensor_copy(out=halo[:], in_=un[:, :, SEG:SEG + 3]), reads=[un.r], writes=[halo.r])

            for n in range(4):
                if ('rw', 'gla', 'ret', 'lru')[n] not in MIX:
                    kb.emit('pool', lambda e, n=n: e.memset(yg[n][:], 0.0), writes=[yg[n].r])

            if 'lru' in MIX:
                for j in range(4):
                    w = wtile()
                    for v in range(4):
                        c = AUGOFF['lru_x%d' % v] + j * 128
                        load_w(w, Wb[l][:, c:c + 128], 128, col0=v * 128, batch=(v > 0))
                    wg = wtile()
                    c = AUGOFF['lru_g'] + j * 128
                    load_w(wg, Wb[l][:, c:c + 128], 128)
                    for tt in range(NT):
                        t0 = tt * TT
                        _o = 5 * ((j * NT + tt) % 2)
                        xc, xcb, rr, ii, aa, uu = WK[_o], WB16[(j * NT + tt) % 2], WK[_o + 1], WK[_o + 2], WK[_o + 3], WK[_o + 4]
                        hh = rr
                        p = ps()
                        for v in range(4):
                            fm_proj(p, w, v * 128, 128, t0, shift=v - 3, start=(v == 0), stop=(v == 3))
                        kb.emit('act', lambda e, p=p, j=j: e.activation(out=xc[:], in_=p[:], func=AF.Identity, bias=lcb[:, j:j + 1]),
                                reads=[p.r, lcb.r], writes=[xc.r])
                        kb.emit('dve', lambda e: e.tensor_copy(out=xcb[:], in_=xc[:]), reads=[xc.r], writes=[xcb.r])
                        p1 = ps()
                        kb.emit('pe', lambda e, p1=p1, j=j: e.matmul(p1[:], lhsT=wab[:, 0, j, :], rhs=xcb[:], start=True, stop=True),
                                reads=[wab.r, xcb.r], writes=[p1.r])
                        p2 = ps()
                        kb.emit('pe', lambda e, p2=p2, j=j: e.matmul(p2[:], lhsT=wab[:, 1, j, :], rhs=xcb[:], start=True, stop=True),
                                reads=[wab.r, xcb.r], writes=[p2.r])
                        kb.emit('act', lambda e, p1=p1, j=j: e.activation(out=rr[:], in_=p1[:], func=AF.Sigmoid, bias=lba[:, j:j + 1]),
                                reads=[p1.r, lba.r], writes=[rr.r])
                        kb.emit('act', lambda e, p2=p2, j=j: e.activation(out=ii[:], in_=p2[:], func=AF.Sigmoid, bias=lbx[:, j:j + 1]),
                                reads=[p2.r, lbx.r], writes=[ii.r])
                        kb.emit('act', lambda e, j=j: e.activation(out=aa[:], in_=rr[:], func=AF.Exp, scale=lc[:, j:j + 1]),
                                reads=[rr.r, lc.r], writes=[aa.r])
                        kb.emit('act', lambda e, j=j: e.activation(out=uu[:], in_=rr[:], func=AF.Exp, scale=lc2[:, j:j + 1]),
                                reads=[rr.r, lc2.r], writes=[uu.r])
                        kb.emit('act', lambda e: e.activation(out=uu[:], in_=uu[:], func=AF.Ln, scale=-1.0, bias=eps_t[:, 2:3]),
                                reads=[uu.r, eps_t.r], writes=[uu.r])
                        kb.emit('act', lambda e: e.activation(out=uu[:], in_=uu[:], func=AF.Exp, scale=0.5), reads=[uu.r], writes=[uu.r])
                        kb.emit('dve', lambda e: e.tensor_tensor(out=ii[:], in0=ii[:], in1=xc[:], op=ALU.mult), reads=[ii.r, xc.r], writes=[ii.r])
                        kb.emit('dve', lambda e: e.tensor_tensor(out=uu[:], in0=uu[:], in1=ii[:], op=ALU.mult), reads=[uu.r, ii.r], writes=[uu.r])
                        kb.emit('dve', lambda e, j=j: e.tensor_tensor_scan(out=hh[:], data0=aa[:], data1=uu[:], initial=lcarry[:, j:j + 1],
                                                                           op0=ALU.mult, op1=ALU.add),
                                reads=[aa.r, uu.r, lcarry.r], writes=[hh.r])
                        kb.emit('act', lambda e, j=j: e.activation(out=lcarry[:, j:j + 1], in_=hh[:, TT - 1:TT], func=AF.Copy),
                                reads=[hh.r], writes=[lcarry.r])
                        p3 = ps()
                        fm_proj(p3, wg, 0, 128, t0)
                        kb.emit('act', lambda e, p3=p3: e.activation(out=ii[:], in_=p3[:], func=AF.Silu), reads=[p3.r], writes=[ii.r])
                        kb.emit('dve', lambda e, j=j, t0=t0: e.tensor_tensor(out=yg[3][:, j, t0:t0 + TT], in0=hh[:], in1=ii[:], op=ALU.mult),
                                reads=[hh.r, ii.r], writes=[yg[3].r])

            if 'rw' in MIX:
                NC8 = TT // 64
                C0 = float(np.exp(-0.5))
                HV = (slice(0, 64), slice(64, 128))
                for tt in range(NT):
                    t0 = tt * TT
                    wc = wtile()
                    for i_, nm_ in enumerate(('rw_wlo_c', 'rw_wlo_p', 'rw_alo_c', 'rw_alo_p')):
                        load_w(wc, Wb[l][:, AUGOFF[nm_]:AUGOFF[nm_] + 64], 64, col0=i_ * 64, batch=(i_ > 0))
                    p = ps()
                    fm_proj(p, wc, 0, 64, t0, start=True, stop=False)
                    fm_proj(p, wc, 64, 64, t0, shift=-1, start=False, stop=True)
                    kb.emit('act', lambda e, p=p: e.activation(out=twb[:], in_=p[0:64, :], func=AF.Tanh), reads=[p.r], writes=[twb.r])
                    p = ps()
                    fm_proj(p, wc, 128, 64, t0, start=True, stop=False)
                    fm_proj(p, wc, 192, 64, t0, shift=-1, start=False, stop=True)
                    kb.emit('act', lambda e, p=p: e.activation(out=alb[:], in_=p[0:64, :], func=AF.Copy), reads=[p.r], writes=[alb.r])
                    def P1(hp, B):
                        AR, BKb, VT, BTm, KTm, gC, bonus, sgate = B
                        wa_ = wtile()
                        for i_, nm_ in enumerate(('rw_r_c', 'rw_r_p', 'rw_k_c', 'rw_k_p')):
                            c = AUGOFF[nm_] + hp * 128
                            load_w(wa_, Wb[l][:, c:c + 128], 128, col0=i_ * 128, batch=(i_ > 0))
                        wb_ = wtile()
                        for i_, nm_ in enumerate(('rw_v_c', 'rw_v_p', 'rw_g')):
                            c = AUGOFF[nm_] + hp * 128
                            load_w(wb_, Wb[l][:, c:c + 128], 128, col0=i_ * 128, batch=(i_ > 0))
                        r32, k32, v32, sg, asg, kkn, kmod, cs, eG, tmp = [WK[i] for i in range(10)]

                        def proj2(wt_, ccur, cprev):
                            pp = ps()
                            fm_proj(pp, wt_, ccur, 128, t0, start=True, stop=False)
                            fm_proj(pp, wt_, cprev, 128, t0, shift=-1, start=False, stop=True)
                            return pp
                        pr = proj2(wa_, 0, 128)
                        kb.emit('act', lambda e, pr=pr: e.activation(out=r32[:], in_=pr[:], func=AF.Copy), reads=[pr.r], writes=[r32.r])
                        yield
                        pk = proj2(wa_, 256, 384)
                        kb.emit('act', lambda e, pk=pk: e.activation(out=k32[:], in_=pk[:], func=AF.Copy), reads=[pk.r], writes=[k32.r])
                        yield
                        pv = proj2(wb_, 0, 128)
                        kb.emit('act', lambda e, pv=pv: e.activation(out=v32[:], in_=pv[:], func=AF.Copy), reads=[pv.r], writes=[v32.r])
                        yield
                        pw = ps()
                        kb.emit('pe', lambda e, pw=pw, hp=hp: e.matmul(pw[:], lhsT=w2b[:, hp * 128:(hp + 1) * 128], rhs=twb[:], start=True, stop=True),
                                reads=[w2b.r, twb.r], writes=[pw.r])
                        kb.emit('act', lambda e, pw=pw, hp=hp: e.activation(out=sg[:], in_=pw[:], func=AF.Sigmoid, bias=w0t[:, hp:hp + 1]),
                                reads=[pw.r, w0t.r], writes=[sg.r])
                        yield
                        pa_ = ps()
                        kb.emit('pe', lambda e, pa_=pa_, hp=hp: e.matmul(pa_[:], lhsT=a2b[:, hp * 128:(hp + 1) * 128], rhs=alb[:], start=True, stop=True),
                                reads=[a2b.r, alb.r], writes=[pa_.r])
                        kb.emit('act', lambda e, pa_=pa_, hp=hp: e.activation(out=asg[:], in_=pa_[:], func=AF.Sigmoid, bias=a0t[:, hp:hp + 1]),
                                reads=[pa_.r, a0t.r], writes=[asg.r])
                        yield
                        kb.emit('dve', lambda e, hp=hp: e.tensor_scalar(out=kkn[:], in0=k32[:], scalar1=kkt[:, hp:hp + 1], scalar2=None, op0=ALU.mult),
                                reads=[k32.r, kkt.r], writes=[kkn.r])
                        yield
                        kb.emit('act', lambda e: e.activation(out=tmp[:], in_=kkn[:], func=AF.Square), reads=[kkn.r], writes=[tmp.r])
                        yield
                        pn = ps()
                        kb.emit('pe', lambda e, pn=pn: e.matmul(pn[:], lhsT=ones_bd[:], rhs=tmp[:], start=True, stop=True), reads=[ones_bd.r, tmp.r], writes=[pn.r])
                        kb.emit('act', lambda e, pn=pn: e.activation(out=tmp[:], in_=pn[:], func=AF.Ln, bias=eps_t[:, 3:4]), reads=[pn.r, eps_t.r], writes=[tmp.r])
                        yield
                        kb.emit('act', lambda e: e.activation(out=tmp[:], in_=tmp[:], func=AF.Exp, scale=-0.5), reads=[tmp.r], writes=[tmp.r])
                        yield
                        kb.emit('dve', lambda e: e.tensor_tensor(out=kkn[:], in0=kkn[:], in1=tmp[:], op=ALU.mult), reads=[kkn.r, tmp.r], writes=[kkn.r])
                        yield
                        kb.emit('dve', lambda e, hp=hp: e.tensor_scalar(out=tmp[:], in0=asg[:], scalar1=-1.0, scalar2=kat[:, hp:hp + 1], op0=ALU.add, op1=ALU.mult),
                                reads=[asg.r, kat.r], writes=[tmp.r])
                        yield
                        kb.emit('dve', lambda e: e.scalar_tensor_tensor(out=kmod[:], in0=tmp[:], scalar=1.0, in1=k32[:], op0=ALU.add, op1=ALU.mult),
                                reads=[tmp.r, k32.r], writes=[kmod.r])
                        yield
                        kb.emit('dve', lambda e, hp=hp: e.scalar_tensor_tensor(out=tmp[:], in0=r32[:], scalar=rkt[:, hp:hp + 1], in1=kmod[:], op0=ALU.mult, op1=ALU.mult),
                                reads=[r32.r, rkt.r, kmod.r], writes=[tmp.r])
                        yield
                        pbn = ps()
                        kb.emit('pe', lambda e, pbn=pbn: e.matmul(pbn[:], lhsT=ones_bd[:], rhs=tmp[:], start=True, stop=True), reads=[ones_bd.r, tmp.r], writes=[pbn.r])
                        kb.emit('dve', lambda e, pbn=pbn: e.tensor_tensor(out=bonus[:], in0=pbn[:], in1=v32[:], op=ALU.mult), reads=[pbn.r, v32.r], writes=[bonus.r])
                        yield
                        for c in range(NC8):
                            kb.emit('dve', lambda e, c=c: e.tensor_tensor_scan(out=cs[:, c * 64:(c + 1) * 64], data0=ones_f[:, 0:64], data1=sg[:, c * 64:(c + 1) * 64],
                                                                               initial=0.0, op0=ALU.mult, op1=ALU.add), reads=[sg.r, ones_f.r], writes=[cs.r])
                            yield
                        kb.emit('act', lambda e: e.activation(out=eG[:], in_=cs[:], func=AF.Exp, scale=-C0), reads=[cs.r], writes=[eG.r])
                        yield
                        kb.emit('act', lambda e: e.activation(out=gC[:], in_=eG[:].rearrange("p (c t) -> p c t", t=64)[:, :, 63], func=AF.Copy), reads=[eG.r], writes=[gC.r])
                        yield
                        kb.emit('dve', lambda e: e.tensor_tensor(out=AR[:, :, 64:128], in0=r32[:].rearrange("p (c t) -> p c t", t=64),
                                                                 in1=eG[:].rearrange("p (c t) -> p c t", t=64), op=ALU.mult), reads=[r32.r, eG.r], writes=[AR.r])
                        yield
                        kb.emit('dve', lambda e: e.tensor_tensor(out=tmp[:], in0=cs[:], in1=sg[:], op=ALU.subtract), reads=[cs.r, sg.r], writes=[tmp.r])
                        yield
                        kb.emit('act', lambda e: e.activation(out=tmp[:], in_=tmp[:], func=AF.Exp, scale=-C0), reads=[tmp.r], writes=[tmp.r])
                        yield
                        kb.emit('dve', lambda e: e.scalar_tensor_tensor(out=AR[:, :, 0:64], in0=kkn[:].rearrange("p (c t) -> p c t", t=64), scalar=-1.0,
                                                                        in1=tmp[:].rearrange("p (c t) -> p c t", t=64), op0=ALU.mult, op1=ALU.mult),
                                reads=[kkn.r, tmp.r], writes=[AR.r])
                        yield
                        kb.emit('act', lambda e: e.activation(out=eG[:], in_=cs[:], func=AF.Exp, scale=C0), reads=[cs.r], writes=[eG.r])
                        yield
                        kb.emit('dve', lambda e: e.tensor_tensor(out=tmp[:], in0=kkn[:], in1=asg[:], op=ALU.mult), reads=[kkn.r, asg.r], writes=[tmp.r])
                        yield
                        kb.emit('dve', lambda e: e.tensor_tensor(out=BK32[:, 0, :], in0=tmp[:], in1=eG[:], op=ALU.mult), reads=[tmp.r, eG.r], writes=[BK32.r])
                        yield
                        kb.emit('dve', lambda e: e.tensor_tensor(out=BK32[:, 1, :], in0=kmod[:], in1=eG[:], op=ALU.mult), reads=[kmod.r, eG.r], writes=[BK32.r])
                        yield
                        kb.emit('act', lambda e: e.activation(out=BKb[:], in_=BK32[:], func=AF.Copy), reads=[BK32.r], writes=[BKb.r])
                        yield
                        kb.emit('act', lambda e: e.activation(out=vb[:], in_=v32[:], func=AF.Copy), reads=[v32.r], writes=[vb.r])
                        yield
                        for (srcfn, srcres, dstt) in ((lambda c, hv: vb[hv, c * 64:(c + 1) * 64], vb.r, VT), (lambda c, hv: BKb[hv, 0, c * 64:(c + 1) * 64], BKb.r, BTm),
                                                      (lambda c, hv: BKb[hv, 1, c * 64:(c + 1) * 64], BKb.r, KTm)):
                            ptp = ps()
                            for c in range(NC8):
                                for hv in HV:
                                    kb.emit('pe', lambda e, ptp=ptp, c=c, hv=hv, srcfn=srcfn: e.matmul(ptp[hv, c * 64:(c + 1) * 64], lhsT=srcfn(c, hv), rhs=identb[hv, 0, :], start=True, stop=True),
                                            reads=[srcres, identb.r], writes=[ptp.r])
                            kb.emit('act', lambda e, ptp=ptp, dstt=dstt: e.activation(out=dstt[:].rearrange("p c v -> p (c v)"), in_=ptp[:], func=AF.Copy),
                                    reads=[ptp.r], writes=[dstt.r])
                            yield
                        pg = ps()
                        fm_proj(pg, wb_, 256, 128, t0)
                        kb.emit('act', lambda e, pg=pg: e.activation(out=sgate[:], in_=pg[:], func=AF.Silu), reads=[pg.r], writes=[sgate.r])
                        yield
                        yield
                    def P2S(hp, B):
                        AR, BKb, VT, BTm, KTm, gC, bonus, sgate = B
                        tmp = tmp2
                        for c in range(NC8):
                            pm_ = ps()
                            for hv in HV:
                                kb.emit('pe', lambda e, pm_=pm_, c=c, hv=hv: e.matmul(pm_[hv, 0:128], lhsT=BKb[hv, 0, c * 64:(c + 1) * 64], rhs=AR[hv, c, :], start=True, stop=True),
                                        reads=[BKb.r, AR.r], writes=[pm_.r])
                                kb.emit('pe', lambda e, pm_=pm_, c=c, hv=hv: e.matmul(pm_[hv, 128:256], lhsT=BKb[hv, 1, c * 64:(c + 1) * 64], rhs=AR[hv, c, :], start=True, stop=True),
                                        reads=[BKb.r, AR.r], writes=[pm_.r])
                                kb.emit('pe', lambda e, pm_=pm_, c=c, hv=hv: e.matmul(pm_[hv, 256:320], lhsT=AR[hv, c, 0:64], rhs=BKb[hv, 0, c * 64:(c + 1) * 64], start=True, stop=True),
                                        reads=[BKb.r, AR.r], writes=[pm_.r])
                            kb.emit('dve', lambda e, pm_=pm_, c=c: e.tensor_tensor(out=Mall[:, c, :], in0=pm_[:, 0:320], in1=maskR[:], op=ALU.mult),
                                    reads=[pm_.r, maskR.r], writes=[Mall.r])
                            yield
                        kb.emit('dve', lambda e: e.tensor_copy(out=AN[0][:, :, 0:64], in_=Mall[:, :, 256:320]), reads=[Mall.r], writes=[AN[0].r])
                        yield
                        kb.emit('dve', lambda e: e.tensor_copy(out=AN[0][:, :, 64:128], in_=Mall[:, :, 0:64]), reads=[Mall.r], writes=[AN[0].r])
                        yield
                        kb.emit('dve', lambda e: e.tensor_tensor(out=TTb[:], in0=Mall[:, :, 0:64], in1=identb[:, 0:1, :].to_broadcast([128, NC8, 64]), op=ALU.add),
                                reads=[Mall.r, identb.r], writes=[TTb.r])
                        yield
                        for lev in range(5):
                            src_, dst_ = AN[lev % 2], AN[(lev + 1) % 2]
                            for half in range(2):
                                pd = ps()
                                for cc in range(4):
                                    c = half * 4 + cc
                                    for hv in HV:
                                        kb.emit('pe', lambda e, pd=pd, c=c, cc=cc, src_=src_, hv=hv: e.matmul(pd[hv, cc * 128:cc * 128 + 64], lhsT=src_[hv, c, 64:128], rhs=src_[hv, c, 0:64], start=True, stop=True),
                                                reads=[src_.r], writes=[pd.r])
                                        if lev < 4:
                                            kb.emit('pe', lambda e, pd=pd, c=c, cc=cc, src_=src_, hv=hv: e.matmul(pd[hv, cc * 128 + 64:cc * 128 + 128], lhsT=src_[hv, c, 0:64], rhs=src_[hv, c, 64:128], start=True, stop=True),
                                                    reads=[src_.r], writes=[pd.r])
                                kb.emit('act', lambda e, pd=pd, half=half, dst_=dst_: e.activation(out=dst_[:, half * 4:(half + 1) * 4, :].rearrange("p c x -> p (c x)"), in_=pd[:], func=AF.Copy),
                                        reads=[pd.r], writes=[dst_.r])
                                yield
                            pt_ = ps()
                            for c in range(NC8):
                                for hv in HV:
                                    kb.emit('pe', lambda e, pt_=pt_, c=c, dst_=dst_, hv=hv: e.matmul(pt_[hv, c * 64:(c + 1) * 64], lhsT=dst_[hv, c, 0:64], rhs=TTb[hv, c, :], start=True, stop=True),
                                            reads=[dst_.r, TTb.r], writes=[pt_.r])
                            kb.emit('dve', lambda e, pt_=pt_: e.tensor_tensor(out=TTb[:].rearrange("p c x -> p (c x)"), in0=TTb[:].rearrange("p c x -> p (c x)"), in1=pt_[:], op=ALU.add),
                                    reads=[pt_.r, TTb.r], writes=[TTb.r])
                            yield
                        ST = STs[hp]
                        for c in range(NC8):
                            px = ps()
                            for hv in HV:
                                kb.emit('pe', lambda e, px=px, c=c, hv=hv: e.matmul(px[hv, 0:64], lhsT=AR[hv, c, 0:64], rhs=ST[hv, :], start=True, stop=False), reads=[AR.r, ST.r], writes=[px.r])
                                kb.emit('pe', lambda e, px=px, c=c, hv=hv: e.matmul(px[hv, 0:64], lhsT=Mall[hv, c, 128:192], rhs=VT[hv, c, :], start=False, stop=True), reads=[Mall.r, VT.r], writes=[px.r])
                            kb.emit('act', lambda e, px=px: e.activation(out=Xb[:], in_=px[:, 0:64], func=AF.Copy), reads=[px.r], writes=[Xb.r])
                            yield
                            pu = ps()
                            for hv in HV:
                                kb.emit('pe', lambda e, pu=pu, c=c, hv=hv: e.matmul(pu[hv, 0:64], lhsT=TTb[hv, c, :], rhs=Xb[hv, :], start=True, stop=True), reads=[TTb.r, Xb.r], writes=[pu.r])
                            kb.emit('dve', lambda e, pu=pu: e.tensor_copy(out=Ub[:], in_=pu[:, 0:64]), reads=[pu.r], writes=[Ub.r])
                            yield
                            py = ps()
                            pS = ps()
                            for hv in HV:
                                kb.emit('pe', lambda e, py=py, c=c, hv=hv: e.matmul(py[hv, 0:64], lhsT=ST[hv, :], rhs=AR[hv, c, 64:128], start=True, stop=False), reads=[AR.r, ST.r], writes=[py.r])
                                kb.emit('pe', lambda e, py=py, c=c, hv=hv: e.matmul(py[hv, 0:64], lhsT=Ub[hv, :], rhs=Mall[hv, c, 64:128], start=False, stop=False), reads=[Ub.r, Mall.r], writes=[py.r])
                                kb.emit('pe', lambda e, py=py, c=c, hv=hv: e.matmul(py[hv, 0:64], lhsT=VT[hv, c, :], rhs=Mall[hv, c, 192:256], start=False, stop=True), reads=[VT.r, Mall.r], writes=[py.r])
                            for hv in HV:
                                kb.emit('pe', lambda e, pS=pS, c=c, hv=hv: e.matmul(pS[hv, 0:64], lhsT=BTm[hv, c, :], rhs=Ub[hv, :], start=True, stop=False), reads=[BTm.r, Ub.r], writes=[pS.r])
                                kb.emit('pe', lambda e, pS=pS, c=c, hv=hv: e.matmul(pS[hv, 0:64], lhsT=KTm[hv, c, :], rhs=VT[hv, c, :], start=False, stop=False), reads=[KTm.r, VT.r], writes=[pS.r])
                                kb.emit('pe', lambda e, pS=pS, hv=hv: e.matmul(pS[hv, 0:64], lhsT=identb[hv, 0, :], rhs=ST[hv, :], start=False, stop=True), reads=[identb.r, ST.r], writes=[pS.r])
                            kb.emit('act', lambda e, py=py, c=c: e.activation(out=yT[:, c * 64:(c + 1) * 64], in_=py[:, 0:64], func=AF.Copy), reads=[py.r], writes=[yT.r])
                            yield
                            kb.emit('dve', lambda e, pS=pS, c=c: e.tensor_scalar(out=ST[:], in0=pS[:, 0:64], scalar1=gC[:, c:c + 1], scalar2=None, op0=ALU.mult),
                                    reads=[pS.r, gC.r], writes=[ST.r])
                            yield
                        pm2 = ps()
                        kb.emit('pe', lambda e, pm2=pm2: e.matmul(pm2[:], lhsT=ones_bd[:], rhs=yT[:], start=True, stop=True), reads=[ones_bd.r, yT.r], writes=[pm2.r])
                        kb.emit('dve', lambda e, pm2=pm2: e.scalar_tensor_tensor(out=yT[:], in0=pm2[:], scalar=-1.0 / 64.0, in1=yT[:], op0=ALU.mult, op1=ALU.add),
                                reads=[pm2.r, yT.r], writes=[yT.r])
                        yield
                        kb.emit('act', lambda e: e.activation(out=tmp[:], in_=yT[:], func=AF.Square), reads=[yT.r], writes=[tmp.r])
                        yield
                        pv2 = ps()
                        kb.emit('pe', lambda e, pv2=pv2: e.matmul(pv2[:], lhsT=ones_bd[:], rhs=tmp[:], start=True, stop=True), reads=[ones_bd.r, tmp.r], writes=[pv2.r])
                        kb.emit('act', lambda e, pv2=pv2: e.activation(out=tmp[:], in_=pv2[:], func=AF.Ln, scale=1.0 / 64.0, bias=eps_t[:, 1:2]), reads=[pv2.r, eps_t.r], writes=[tmp.r])
                        yield
                        kb.emit('act', lambda e: e.activation(out=tmp[:], in_=tmp[:], func=AF.Exp, scale=-0.5), reads=[tmp.r], writes=[tmp.r])
                        yield
                        kb.emit('dve', lambda e: e.tensor_tensor(out=yT[:], in0=yT[:], in1=tmp[:], op=ALU.mult), reads=[yT.r, tmp.r], writes=[yT.r])
                        yield
                        kb.emit('dve', lambda e, hp=hp: e.tensor_scalar(out=yT[:], in0=yT[:], scalar1=lngt[:, hp:hp + 1], scalar2=lnbt[:, hp:hp + 1], op0=ALU.mult, op1=ALU.add),
                                reads=[yT.r, lngt.r, lnbt.r], writes=[yT.r])
                        yield
                        kb.emit('dve', lambda e: e.tensor_tensor(out=yT[:], in0=yT[:], in1=bonus[:], op=ALU.add), reads=[yT.r, bonus.r], writes=[yT.r])
                        yield
                        kb.emit('dve', lambda e, hp=hp: e.tensor_tensor(out=yg[0][:, hp, t0:t0 + TT], in0=yT[:], in1=sgate[:], op=ALU.mult), reads=[yT.r, sgate.r], writes=[yg[0].r])
                        yield
                        yield
                    prev = None
                    for hp in range(4):
                        gens = [P1(hp, RWB[hp % 2])] + ([prev] if prev is not None else [])
                        _interleave(gens)
                        prev = P2S(hp, RWB[hp % 2])
                    _interleave([prev])

            for n_, nm in ((1, 'gla'), (2, 'ret')):
                if nm not in MIX:
                    continue
                isg = (nm == 'gla')
                S32, Sb = (Sg32, Sgb) if isg else (Sr32, Srb)
                gn = gng if isg else rgg
                wqk = wtile()
                load_w(wqk, Wb[l][:, AUGOFF[nm + '_q']:AUGOFF[nm + '_q'] + 256], 256, col0=0)
                load_w(wqk, Wb[l][:, AUGOFF[nm + '_k']:AUGOFF[nm + '_k'] + 256], 256, col0=256, batch=True)
                wv = wtile()
                load_w(wv, Wb[l][:, AUGOFF[nm + '_v']:AUGOFF[nm + '_v'] + 512], 512)
                if isg:
                    wg = wtile()
                    load_w(wg, Wb[l][:, AUGOFF[nm + '_g']:AUGOFF[nm + '_g'] + 512], 512)
                if isg:
                    kb.dma('sp', wflo[:], Wb[l][:, AUGOFF['gla_flo']:AUGOFF['gla_flo'] + 16].rearrange("(k p) c -> p k c", p=128),
                           wflo.r, reads=[wdram_r], writes=[wflo.r])
                else:
                    wsw = wtile()
                    load_w(wsw, Wb[l][:, AUGOFF['ret_qs']:AUGOFF['ret_qs'] + 256], 256, col0=0)
                    load_w(wsw, Wb[l][:, AUGOFF['ret_ks']:AUGOFF['ret_ks'] + 256], 256, col0=256, batch=True)
                for tt in range(NT):
                    t0 = tt * TT
                    NCH = TT // 128
                    for c in range(NCH):
                        p = ps()
                        for k in range(KC):
                            kb.emit('pe', lambda e, k=k, p=p, c=c: e.matmul(p[:], lhsT=un[:, k, 3 + t0 + c * 128:3 + t0 + (c + 1) * 128], rhs=wv[:, k, :],
                                                                            start=(k == 0), stop=(k == KC - 1)), reads=[un.r, wv.r], writes=[p.r])
                        kb.emit('act', lambda e, p=p, c=c: e.activation(out=vT[c][:], in_=p[:], func=AF.Copy), reads=[p.r], writes=[vT[c].r])
                    if isg:
                        pf = ps()
                        fm_proj(pf, wflo, 0, 16, t0)
                        kb.emit('act', lambda e, pf=pf: e.activation(out=flo_b[:], in_=pf[0:16, :], func=AF.Copy), reads=[pf.r], writes=[flo_b.r])
                    else:
                        kb.dma('sp', posi[:], positions[s0 + t0:s0 + t0 + TT].partition_broadcast(64), posi.r, writes=[posi.r])
                        A_, B_, C_ = WK[0], WK[1], WK[2]
                        kb.emit('dve', lambda e: e.tensor_copy(out=A_[0:64, :], in_=posi[:]), reads=[posi.r], writes=[A_.r])
                        kb.emit('dve', lambda e: e.tensor_scalar(out=A_[0:64, :], in0=A_[0:64, :], scalar1=invf[:, 0:1], scalar2=1.0 / (2 * np.pi),
                                                                 op0=ALU.mult, op1=ALU.mult), reads=[A_.r, invf.r], writes=[A_.r])
                        kb.emit('dve', lambda e: e.tensor_copy(out=posi[:], in_=A_[0:64, :]), reads=[A_.r], writes=[posi.r])
                        kb.emit('dve', lambda e: e.tensor_copy(out=B_[0:64, :], in_=posi[:]), reads=[posi.r], writes=[B_.r])
                        kb.emit('dve', lambda e: e.tensor_tensor(out=A_[0:64, :], in0=A_[0:64, :], in1=B_[0:64, :], op=ALU.subtract),
                                reads=[A_.r, B_.r], writes=[A_.r])
                        kb.emit('act', lambda e: e.activation(out=B_[0:64, :], in_=A_[0:64, :], func=AF.Sin, scale=float(np.pi)), reads=[A_.r], writes=[B_.r])
                        kb.emit('act', lambda e: e.activation(out=C_[0:64, :], in_=A_[0:64, :], func=AF.Sin, scale=float(np.pi / 2)), reads=[A_.r], writes=[C_.r])
                        kb.emit('dve', lambda e: e.tensor_tensor(out=cosT[0:64, :], in0=B_[0:64, :], in1=B_[0:64, :], op=ALU.mult), reads=[B_.r], writes=[cosT.r])
                        kb.emit('dve', lambda e: e.tensor_scalar(out=cosT[0:64, :], in0=cosT[0:64, :], scalar1=-2.0, scalar2=1.0, op0=ALU.mult, op1=ALU.add),
                                reads=[cosT.r], writes=[cosT.r])
                        kb.emit('dve', lambda e: e.tensor_tensor(out=C_[0:64, :], in0=C_[0:64, :], in1=C_[0:64, :], op=ALU.mult), reads=[C_.r], writes=[C_.r])
                        kb.emit('dve', lambda e: e.tensor_scalar(out=C_[0:64, :], in0=C_[0:64, :], scalar1=-4.0, scalar2=2.0, op0=ALU.mult, op1=ALU.add),
                                reads=[C_.r], writes=[C_.r])
                        kb.emit('dve', lambda e: e.tensor_tensor(out=sinT[0:64, :], in0=C_[0:64, :], in1=B_[0:64, :], op=ALU.mult), reads=[C_.r, B_.r], writes=[sinT.r])
                    for h in range(4):
                        pq = ps()
                        fm_proj(pq, wqk, h * 64, 64, t0)
                        pk = ps()
                        fm_proj(pk, wqk, 256 + h * 64, 64, t0)
                        if isg:
                            plf = ps()
                            kb.emit('pe', lambda e, plf=plf, h=h: e.matmul(plf[0:64, :], lhsT=fup_b[0:16, h * 64:(h + 1) * 64], rhs=flo_b[0:16, :], start=True, stop=True),
                                    reads=[fup_b.r, flo_b.r], writes=[plf.r])
                            cs, eb, enb = (WK[0], WK[1], WK[2]) if h % 2 == 0 else (WK[5], WK[6], WK[7])
                            kb.emit('act', lambda e, plf=plf, h=h: e.activation(out=cs[0:64, :], in_=plf[0:64, :], func=AF.Exp, scale=-1.0, bias=nfb[:, h:h + 1]),
                                    reads=[plf.r, nfb.r], writes=[cs.r])
                            kb.emit('act', lambda e: e.activation(out=cs[0:64, :], in_=cs[0:64, :], func=AF.Ln, bias=eps_t[0:64, 2:3]), reads=[cs.r, eps_t.r], writes=[cs.r])
                            for c in range(NCH):
                                kb.emit('dve', lambda e, c=c: e.tensor_tensor_scan(out=eb[0:64, c * 128:(c + 1) * 128], data0=ones_f[0:64, :], data1=cs[0:64, c * 128:(c + 1) * 128],
                                                                                   initial=0.0, op0=ALU.mult, op1=ALU.add), reads=[cs.r, ones_f.r], writes=[eb.r])
                            kb.emit('act', lambda e: e.activation(out=enb[0:64, :], in_=eb[0:64, :], func=AF.Exp, scale=1.0 / 16.0), reads=[eb.r], writes=[enb.r])
                            kb.emit('act', lambda e: e.activation(out=eb[0:64, :], in_=eb[0:64, :], func=AF.Exp, scale=-1.0 / 16.0), reads=[eb.r], writes=[eb.r])
                            kb.emit('dve', lambda e, pq=pq, h=h: e.scalar_tensor_tensor(out=qd[:, h, :], in0=pq[0:64, :], scalar=0.125, in1=eb[0:64, :], op0=ALU.mult, op1=ALU.mult),
                                    reads=[pq.r, eb.r], writes=[qd.r])
                            kb.emit('dve', lambda e, pk=pk, h=h: e.tensor_tensor(out=kd32[:, h, :], in0=pk[0:64, :], in1=enb[0:64, :], op=ALU.mult),
                                    reads=[pk.r, enb.r], writes=[kd32.r])
                            for c in range(NCH):
                                kb.emit('act', lambda e, c=c, h=h: e.activation(out=ebl[:, h, c:c + 1], in_=eb[0:64, c * 128 + 127:c * 128 + 128], func=AF.Copy),
                                        reads=[eb.r], writes=[ebl.r])
                        else:
                            pqs = ps()
                            fm_proj(pqs, wsw, h * 64, 64, t0)
                            pks = ps()
                            fm_proj(pks, wsw, 256 + h * 64, 64, t0)
                            q1, q2 = (WK[3], WK[4]) if h % 2 == 0 else (WK[8], WK[9])
                            for (pa, pb, dst, tab) in ((pq, pqs, qd, ebR), (pk, pks, kd32, enbR)):
                                kb.emit('dve', lambda e, pa=pa: e.tensor_tensor(out=q1[0:64, :], in0=pa[0:64, :], in1=cosT[0:64, :], op=ALU.mult), reads=[pa.r, cosT.r], writes=[q1.r])
                                kb.emit('dve', lambda e, pb=pb: e.tensor_tensor(out=q2[0:64, :], in0=pb[0:64, :], in1=sinT[0:64, :], op=ALU.mult), reads=[pb.r, sinT.r], writes=[q2.r])
                                kb.emit('dve', lambda e: e.tensor_tensor(out=q1[0:64, :], in0=q1[0:64, :], in1=q2[0:64, :], op=ALU.add), reads=[q1.r, q2.r], writes=[q1.r])
                                kb.emit('dve', lambda e, dst=dst, tab=tab, h=h: e.tensor_tensor(
                                    out=dst[:, h, :].rearrange("p (c t) -> p c t", t=128), in0=q1[0:64, :].rearrange("p (c t) -> p c t", t=128),
                                    in1=tab[:, h:h + 1, :].to_broadcast([64, NCH, 128]), op=ALU.mult), reads=[q1.r, tab.r], writes=[dst.r])
                    kb.emit('act', lambda e: e.activation(out=kd[:], in_=kd32[:], func=AF.Copy), reads=[kd32.r], writes=[kd.r])
                    for c in range(NCH):
                        cs_ = slice(c * 128, (c + 1) * 128)
                        pa = ps()
                        for h in range(4):
                            kb.emit('pe', lambda e, pa=pa, h=h, cs_=cs_: e.matmul(pa[:, h * 128:(h + 1) * 128], lhsT=kd[:, h, cs_], rhs=qd[:, h, cs_], start=True, stop=True),
                                    reads=[kd.r, qd.r], writes=[pa.r])
                        kb.emit('dve', lambda e, pa=pa: e.tensor_tensor(out=attb[:], in0=pa[:], in1=mask4[:], op=ALU.mult), reads=[pa.r, mask4.r], writes=[attb.r])
                        po = ps()
                        for h in range(4):
                            kb.emit('pe', lambda e, po=po, h=h, c=c: e.matmul(po[:, h * 128:(h + 1) * 128], lhsT=vT[c][:, h * 128:(h + 1) * 128], rhs=attb[:, h * 128:(h + 1) * 128],
                                                                              start=True, stop=False), reads=[vT[c].r, attb.r], writes=[po.r])
                            kb.emit('pe', lambda e, po=po, h=h, cs_=cs_: e.matmul(po[:, h * 128:(h + 1) * 128], lhsT=Sb[:, h, :], rhs=qd[:, h, cs_], start=False, stop=True),
                                    reads=[Sb.r, qd.r], writes=[po.r])
                        kb.emit('act', lambda e, po=po, cs_=cs_: e.activation(out=oall[:, :, cs_], in_=po[:].rearrange("p (h t) -> p h t", h=4), func=AF.Copy),
                                reads=[po.r], writes=[oall.r])
                        pt = ps()
                        for h in range(4):
                            kb.emit('pe', lambda e, pt=pt, h=h, cs_=cs_: e.transpose(out=pt[:, h * 64:(h + 1) * 64], in_=kd32[:, h, cs_], identity=ident[0:64, 0:64]),
                                    reads=[kd32.r, ident.r], writes=[pt.r])
                        kb.emit('dve', lambda e, pt=pt: e.tensor_copy(out=kdT[:], in_=pt[:, 0:256]), reads=[pt.r], writes=[kdT.r])
                        pss = ps()
                        for h in range(4):
                            kb.emit('pe', lambda e, pss=pss, h=h, c=c: e.matmul(pss[0:64, h * 128:(h + 1) * 128], lhsT=kdT[:, h * 64:(h + 1) * 64], rhs=vT[c][:, h * 128:(h + 1) * 128],
                                                                                start=True, stop=True), reads=[kdT.r, vT[c].r], writes=[pss.r])
                        kb.emit('dve', lambda e, pss=pss: e.tensor_tensor(out=S32[:].rearrange("p h v -> p (h v)"), in0=S32[:].rearrange("p h v -> p (h v)"), in1=pss[0:64, :], op=ALU.add),
                                reads=[pss.r, S32.r], writes=[S32.r])
                        for h in range(4):
                            if isg:
                                kb.emit('dve', lambda e, h=h, c=c: e.tensor_scalar(out=S32[:, h, :], in0=S32[:, h, :], scalar1=ebl[:, h, c:c + 1], scalar2=None, op0=ALU.mult),
                                        reads=[S32.r, ebl.r], writes=[S32.r])
                            else:
                                kb.emit('dve', lambda e, h=h: e.tensor_scalar(out=S32[:, h, :], in0=S32[:, h, :], scalar1=float(RET_G[h] ** 128), scalar2=None, op0=ALU.mult),
                                        reads=[S32.r], writes=[S32.r])
                        kb.emit('act', lambda e: e.activation(out=Sb[:], in_=S32[:], func=AF.Copy), reads=[S32.r], writes=[Sb.r])
                    if not isg:
                        wg = wtile()
                        load_w(wg, Wb[l][:, AUGOFF[nm + '_g']:AUGOFF[nm + '_g'] + 512], 512)
                    for h in range(4):
                        o_h, sqq, rs_, gg = (WK[5], WK[6], WK[7], WK[8]) if h % 2 == 0 else (WK[0], WK[1], WK[2], WK[3])
                        if isg:
                            kb.emit('act', lambda e, h=h: e.activation(out=sqq[:], in_=oall[:, h, :], func=AF.Square), reads=[oall.r], writes=[sqq.r])
                            pv_ = ps()
                            kb.emit('pe', lambda e, pv_=pv_: e.matmul(pv_[:], lhsT=ones_f[:], rhs=sqq[:], start=True, stop=True), reads=[ones_f.r, sqq.r], writes=[pv_.r])
                            src_o = None
                        else:
                            pm_ = ps()
                            kb.emit('pe', lambda e, pm_=pm_, h=h: e.matmul(pm_[:], lhsT=ones_f[:], rhs=oall[:, h, :], start=True, stop=True), reads=[ones_f.r, oall.r], writes=[pm_.r])
                            kb.emit('dve', lambda e, pm_=pm_, h=h: e.scalar_tensor_tensor(out=o_h[:], in0=pm_[:], scalar=-1.0 / 128.0, in1=oall[:, h, :], op0=ALU.mult, op1=ALU.add),
                                    reads=[pm_.r, oall.r], writes=[o_h.r])
                            kb.emit('act', lambda e: e.activation(out=sqq[:], in_=o_h[:], func=AF.Square), reads=[o_h.r], writes=[sqq.r])
                            pv_ = ps()
                            kb.emit('pe', lambda e, pv_=pv_: e.matmul(pv_[:], lhsT=ones_f[:], rhs=sqq[:], start=True, stop=True), reads=[ones_f.r, sqq.r], writes=[pv_.r])
                        kb.emit('act', lambda e, pv_=pv_: e.activation(out=rs_[:], in_=pv_[:], func=AF.Ln, scale=1.0 / 128.0, bias=eps_t[:, 0:1]), reads=[pv_.r, eps_t.r], writes=[rs_.r])
                        kb.emit('act', lambda e: e.activation(out=rs_[:], in_=rs_[:], func=AF.Exp, scale=-0.5), reads=[rs_.r], writes=[rs_.r])
                        if isg:
                            kb.emit('dve', lambda e, h=h: e.scalar_tensor_tensor(out=rs_[:], in0=oall[:, h, :], scalar=gn[:, h:h + 1], in1=rs_[:], op0=ALU.mult, op1=ALU.mult),
                                    reads=[oall.r, gn.r, rs_.r], writes=[rs_.r])
                        else:
                            kb.emit('dve', lambda e, h=h: e.scalar_tensor_tensor(out=rs_[:], in0=o_h[:], scalar=gn[:, h:h + 1], in1=rs_[:], op0=ALU.mult, op1=ALU.mult),
                                    reads=[o_h.r, gn.r, rs_.r], writes=[rs_.r])
                        pg = ps()
                        fm_proj(pg, wg, h * 128, 128, t0)
                        kb.emit('act', lambda e, pg=pg: e.activation(out=gg[:], in_=pg[:], func=AF.Silu), reads=[pg.r], writes=[gg.r])
                        kb.emit('dve', lambda e, h=h, n_=n_: e.tensor_tensor(out=yg[n_][:, h, t0:t0 + TT], in0=rs_[:], in1=gg[:], op=ALU.mult),
                                reads=[rs_.r, gg.r], writes=[yg[n_].r])

            for j in range(KC):
                wgt = wtile()
                for n in range(4):
                    c = AUGOFF['merge%d' % (2 * n + j // 4)] + (j % 4) * 128
                    load_w(wgt, Wb[l][:, c:c + 128], 128, col0=n * 128, batch=(n > 0))
                wbr = wtile()
                for n in range(4):
                    load_w(wbr, WBR[l][n * 512:(n + 1) * 512, j * 128:(j + 1) * 128], 128, col0=n * 128, kc=4, batch=(n > 0))
                for tt in range(NT):
                    t0 = tt * TT
                    acc = WK[6]
                    for n in range(4):
                        gt = WK[7] if n % 2 == 0 else WK[8]
                        pg = ps()
                        fm_proj(pg, wgt, n * 128, 128, t0)
                        pb = ps()
                        for k in range(4):
                            kb.emit('pe', lambda e, k=k, n=n, pb=pb, t0=t0: e.matmul(pb[:], lhsT=wbr[:, k, n * 128:(n + 1) * 128],
                                                                                      rhs=yg[n][:, k, t0:t0 + TT], start=(k == 0), stop=(k == 3)),
                                    reads=[wbr.r, yg[n].r], writes=[pb.r])
                        kb.emit('act', lambda e, pg=pg: e.activation(out=gt[:], in_=pg[:], func=AF.Sigmoid), reads=[pg.r], writes=[gt.r])
                        if n == 0:
                            kb.emit('dve', lambda e, pb=pb: e.tensor_tensor(out=acc[:], in0=gt[:], in1=pb[:], op=ALU.mult),
                                    reads=[gt.r, pb.r], writes=[acc.r])
                        else:
                            kb.emit('dve', lambda e, pb=pb: e.tensor_tensor(out=gt[:], in0=gt[:], in1=pb[:], op=ALU.mult),
                                    reads=[gt.r, pb.r], writes=[gt.r])
                            if n < 3:
                                kb.emit('dve', lambda e: e.tensor_tensor(out=acc[:], in0=acc[:], in1=gt[:], op=ALU.add),
                                        reads=[acc.r, gt.r], writes=[acc.r])
                            else:
                                kb.emit('dve', lambda e, j=j, t0=t0: e.tensor_tensor(out=merged[:, j, t0:t0 + TT], in0=acc[:], in1=gt[:], op=ALU.add),
                                        reads=[acc.r, gt.r], writes=[merged.r])
            for j in range(KC):
                w = wtile()
                load_w(w, WO[l][:, j * 128:(j + 1) * 128], 128)
                for tt in range(NT):
                    t0 = tt * TT
                    p = ps()
                    for k in range(KC):
                        kb.emit('pe', lambda e, k=k, p=p, w=w, t0=t0: e.matmul(p[:], lhsT=w[:, k, 0:128], rhs=merged[:, k, t0:t0 + TT],
                                                                               start=(k == 0), stop=(k == KC - 1)),
                                reads=[w.r, merged.r], writes=[p.r])
                    kb.emit('dve', lambda e, p=p, j=j, t0=t0: e.tensor_tensor(out=hseg[:, j, t0:t0 + TT], in0=hseg[:, j, t0:t0 + TT], in1=p[:], op=ALU.add),
                            reads=[p.r, hseg.r], writes=[hseg.r])

            if XA:
                for tt in range(NT):
                    hv = _View(hseg, slice(tt * TT, (tt + 1) * TT))
                    rmsnorm_fm(hv, xng, lambda k, tt=tt: un[:, k, 3 + tt * TT:3 + (tt + 1) * TT], un.r)
                for j in range(KC):
                    w = wtile()
                    load_w(w, WQ[l][:, j * 128:(j + 1) * 128], 128)
                    for tt in range(NT):
                        t0 = tt * TT
                        p = ps()
                        fm_proj(p, w, 0, 128, t0)
                        kb.emit('act', lambda e, p=p, j=j, t0=t0: e.activation(out=merged[:, j, t0:t0 + TT], in_=p[:], func=AF.Copy),
                                reads=[p.r], writes=[merged.r])
                for tt in range(NT):
                    t0 = tt * TT
                    for hh_ in range(4):
                        pT = [WB16[0], WB16[1]] if hh_ % 2 == 0 else [WB16[2], WB16[3]]
                        psum_ = ps()
                        for mb in range(2):
                            p = ps()
                            for k2 in range(2):
                                kb.emit('pe', lambda e, p=p, k2=k2, mb=mb, hh_=hh_, t0=t0: e.matmul(
                                    p[:], lhsT=kT[:, hh_ * 2 + k2, mb * 128:(mb + 1) * 128], rhs=merged[:, hh_ * 2 + k2, t0:t0 + TT],
                                    start=(k2 == 0), stop=(k2 == 1)), reads=[kT.r, merged.r], writes=[p.r])
                            kb.emit('act', lambda e, p=p, mb=mb: e.activation(out=pT[mb][:], in_=p[:], func=AF.Exp, scale=1.0 / 16.0),
                                    reads=[p.r], writes=[pT[mb].r])
                        for mb in range(2):
                            kb.emit('pe', lambda e, mb=mb, psum_=psum_: e.matmul(psum_[:], lhsT=ones_b[:], rhs=pT[mb][:], start=(mb == 0), stop=(mb == 1)),
                                    reads=[ones_b.r, pT[mb].r], writes=[psum_.r])
                        rs = WK[8] if hh_ % 2 == 0 else WK[9]
                        kb.emit('act', lambda e, psum_=psum_, rs=rs: e.activation(out=rs[:], in_=psum_[:], func=AF.Ln), reads=[psum_.r], writes=[rs.r])
                        kb.emit('act', lambda e, rs=rs: e.activation(out=rs[:], in_=rs[:], func=AF.Exp, scale=-1.0), reads=[rs.r], writes=[rs.r])
                        for dh in range(2):
                            po = ps()
                            for mb in range(2):
                                kb.emit('pe', lambda e, po=po, mb=mb, dh=dh, hh_=hh_: e.matmul(
                                    po[:], lhsT=vtm[:, mb, hh_ * 256 + dh * 128:hh_ * 256 + (dh + 1) * 128], rhs=pT[mb][:],
                                    start=(mb == 0), stop=(mb == 1)), reads=[vtm.r, pT[mb].r], writes=[po.r])
                            dst = yg[hh_ // 2]
                            kb.emit('dve', lambda e, po=po, dst=dst, hh_=hh_, dh=dh, t0=t0: e.tensor_tensor(
                                out=dst[:, (hh_ % 2) * 2 + dh, t0:t0 + TT], in0=po[:], in1=rs[:], op=ALU.mult),
                                    reads=[po.r, rs.r], writes=[dst.r])
                for j in range(KC):
                    w = wtile()
                    load_w(w, WXO[l][:, j * 128:(j + 1) * 128], 128)
                    for tt in range(NT):
                        t0 = tt * TT
                        p = ps()
                        for k in range(KC):
                            src = yg[k // 4]
                            kb.emit('pe', lambda e, k=k, p=p, w=w, t0=t0, src=src: e.matmul(p[:], lhsT=w[:, k, 0:128], rhs=src[:, k % 4, t0:t0 + TT],
                                                                                            start=(k == 0), stop=(k == KC - 1)),
                                    reads=[w.r, src.r], writes=[p.r])
                        kb.emit('dve', lambda e, p=p, j=j, t0=t0: e.tensor_tensor(out=hseg[:, j, t0:t0 + TT], in0=hseg[:, j, t0:t0 + TT], in1=p[:], op=ALU.add),
                                reads=[p.r, hseg.r], writes=[hseg.r])
            kb.dma('pool', hT_v[:, :, s0:s0 + SEG], hseg[:], hseg.r, reads=[hseg.r], writes=[hT_r])

    kb.new_scope()
    _tl.clear()
    hs_t = [T(kb, "hs%d" % i, [128, KC, TT], F32) for i in range(2)]
    yn = T(kb, "yn", [128, KC, TT], F32)
    otm = [T(kb, "otm%d" % i, [128, D], F32) for i in range(2)]
    oi = 0
    for tt in range(S // TT):
        hs = hs_t[tt % 2]
        kb.dma('sp', hs[:], hT_v[:, :, tt * TT:(tt + 1) * TT], hs.r, reads=[hT_r], writes=[hs.r])
        rmsnorm_fm(hs, fng, lambda k: yn[:, k, :], yn.r)
        for tb in range(TT // 128):
            ot = otm[oi % 2]
            oi += 1
            for half in range(2):
                p = ps()
                for kk in range(4):
                    k = half * 4 + kk
                    kb.emit('pe', lambda e, p=p, kk=kk, k=k, tb=tb: e.transpose(
                        out=p[:, kk * 128:(kk + 1) * 128], in_=yn[:, k, tb * 128:(tb + 1) * 128], identity=ident[:]),
                            reads=[yn.r, ident.r], writes=[p.r])
                if half:
                    kb.emit('act', lambda e, p=p, ot=ot: e.activation(out=ot[:, 512:1024], in_=p[:], func=AF.Copy),
                            reads=[p.r], writes=[ot.r])
                else:
                    kb.emit('dve', lambda e, p=p, ot=ot: e.tensor_copy(out=ot[:, 0:512], in_=p[:]),
                            reads=[p.r], writes=[ot.r])
            r0 = tt * TT + tb * 128
            kb.dma('pool', out[r0:r0 + 128, :], ot[:], ot.r, reads=[ot.r])
    kb.wait_all('pool', [t.r for t in otm])
    kb.new_scope()
    kb.scope.close()
    build.ninst = kb.ninst
    return nc, es


PARAM_NAMES = ('mix_norm_g', 'w_in', 'rw_mu', 'rw_w0', 'rw_w2', 'rw_a0', 'rw_a2', 'rw_k_k', 'rw_k_a', 'rw_r_k', 'rw_ln_g',
               'rw_ln_b', 'gla_f_up', 'gla_f_b', 'gla_norm_g', 'ret_gn_g', 'lru_conv_w', 'lru_conv_b', 'lru_wa', 'lru_ba',
               'lru_wx', 'lru_bx', 'lru_lambda', 'w_branch', 'w_out', 'xa_norm_g', 'xa_mem_norm_g', 'xa_wq', 'xa_wkv',
               'xa_wo', 'final_norm_g')


def core_inputs(inputs, b, S):
    m = {"x": np.ascontiguousarray(inputs['x'][b, :S]), "mem": np.ascontiguousarray(inputs['mem'][b]),
         "positions": np.ascontiguousarray(inputs['positions'][b, :S]).astype(np.int32)}
    for n in PARAM_NAMES:
        m[n] = np.ascontiguousarray(inputs[n])
    return m


def kernel(**inputs):
    S = inputs['x'].shape[1]
    nc, es = build(S, 2)
    in_maps = [core_inputs(inputs, c % 2, S) for c in range(8)]
    res = run_bass_kernel_spmd(nc, in_maps, core_ids=list(range(8)))
    es.close()
    return np.stack([res.results[0]["out"], res.results[1]["out"]], axis=0)
```

```python
import numpy as np
from contextlib import ExitStack
import concourse.bass as bass
import concourse.mybir as mybir
from concourse.bass_utils import run_bass_kernel_spmd

F32 = mybir.dt.float32
BF16 = mybir.dt.bfloat16
I32 = mybir.dt.int32
ALU = mybir.AluOpType
AF = mybir.ActivationFunctionType
AX = mybir.AxisListType

D = 1024
KC = 8
NMEM = 256
DBR = 512
NORM_EPS = 1e-6
RW_LN_EPS = 64e-5
RET_G = [1.0 - 2.0 ** (-5.0 - h) for h in range(4)]

_src = {}
_o = 0
for _n, _w in (('rw_r', 512), ('rw_k', 512), ('rw_v', 512), ('rw_wlo', 64), ('rw_alo', 64), ('rw_g', 512),
               ('gla_q', 256), ('gla_k', 256), ('gla_v', 512), ('gla_flo', 16), ('gla_g', 512),
               ('ret_q', 256), ('ret_k', 256), ('ret_v', 512), ('ret_g', 512),
               ('lru_x', 512), ('lru_g', 512), ('merge', 4096)):
    _src[_n] = (_o, _w)
    _o += _w
N_IN = _o
MU_OFF = {'rw_r': 0, 'rw_k': 512, 'rw_v': 1024, 'rw_wlo': 1536, 'rw_alo': 1600}
AUG = []
for _n in ('rw_r', 'rw_k', 'rw_v', 'rw_wlo', 'rw_alo'):
    AUG.append((_n + '_c', _src[_n][1], _src[_n][0], 'mu1m', MU_OFF[_n]))
    AUG.append((_n + '_p', _src[_n][1], _src[_n][0], 'mu', MU_OFF[_n]))
for _j in range(4):
    AUG.append(('lru_x%d' % _j, 512, _src['lru_x'][0], 'conv', _j))
for _n in ('rw_g', 'gla_q', 'gla_k', 'gla_v', 'gla_flo', 'gla_g', 'ret_q', 'ret_k', 'ret_v', 'ret_g', 'lru_g'):
    AUG.append((_n, _src[_n][1], _src[_n][0], 'plain', 0))
AUG.append(('ret_qs', 256, _src['ret_q'][0], 'swap', 0))
AUG.append(('ret_ks', 256, _src['ret_k'][0], 'swap', 0))
for _j in range(8):
    AUG.append(('merge%d' % _j, 512, _src['merge'][0] + 512 * _j, 'plain', 0))
AUGOFF = {}
_o = 0
for _a in AUG:
    AUGOFF[_a[0]] = _o
    _o += _a[1]
NAUG = _o


class Res:
    __slots__ = ('w', 'r', 'ds')

    def __init__(self):
        self.w = None
        self.r = {}
        self.ds = None


class KB:
    def __init__(self, nc, es):
        self.nc = nc
        self.es = es
        self.eng = dict(pe=nc.tensor, dve=nc.vector, act=nc.scalar, pool=nc.gpsimd, sp=nc.sync)
        self.semh = {}
        self.cnt = {}
        for e in self.eng:
            self.semh[e] = es.enter_context(nc.semaphore('c_' + e))
            self.cnt[e] = 0
        self.seen = {e: {} for e in self.eng}
        self.nd = 0
        self.strict = True
        self.ninst = 0
        self.scope = es

    def new_scope(self):
        keys = list(self.cnt.keys())
        for e in self.eng:
            need = {k: (self.cnt[k], self.cnt[k]) for k in keys if k != e and self.cnt[k] > 0}
            self._waits(e, need)
        for e in self.eng:
            self.emit(e, lambda en: en.engine_nop() if hasattr(en, 'engine_nop') else en.nop())
        for e in self.eng:
            need = {k: (self.cnt[k], self.cnt[k]) for k in self.eng if k != e}
            self._waits(e, need)
        if self.scope is not self.es:
            self.scope.close()
        self.scope = ExitStack()

    def _need(self, reads, writes):
        need = {}

        def add(k, v, raw):
            a, b = need.get(k, (0, 0))
            need[k] = (max(a, v), max(b, v) if raw else b)
        for r in reads:
            if r.w is not None:
                add(r.w[0], r.w[1], True)
        for w in writes:
            if w.w is not None:
                add(w.w[0], w.w[1], False)
            for k, v in w.r.items():
                add(k, v, False)
        return need

    def _waits(self, e, need):
        eng = self.eng[e]
        seen = self.seen[e]
        for k, vv in need.items():
            v, vraw = vv if isinstance(vv, tuple) else (vv, vv)
            if k == e:
                if e == 'pe' or not self.strict:
                    continue
                pass
            if k[0] == 'd' and k[1:].isdigit():
                v = self.cnt[k]
            if seen.get(k, 0) >= v:
                continue
            eng.wait_ge(self.semh[k], v)
            seen[k] = v
            self.ninst += 1

    def emit(self, e, fn, reads=(), writes=()):
        self._waits(e, self._need(reads, writes))
        ins = fn(self.eng[e])
        self.cnt[e] += 1
        ins.then_inc(self.semh[e], 1)
        t = (e, self.cnt[e])
        for r in reads:
            if r.r.get(e, 0) < t[1]:
                r.r[e] = t[1]
        for w in writes:
            w.w = t
            w.r = {}
        self.ninst += 1
        return ins

    def _dsem(self, res, q):
        if res.ds is None:
            res.ds = {}
        if q not in res.ds:
            k = 'd%d' % self.nd
            self.nd += 1
            self.semh[k] = self.es.enter_context(self.nc.semaphore(k))
            self.cnt[k] = 0
            res.ds[q] = k
        return res.ds[q]

    def dma(self, q, out, in_, sb, reads=(), writes=(), batch=False):
        k = self._dsem(sb, q)
        need = self._need(reads, writes)
        if not batch and self.cnt[k] > 0:
            need[k] = (self.cnt[k], self.cnt[k])
        self._waits(q, need)
        ins = self.eng[q].dma_start(out=out, in_=in_)
        self.cnt[k] += 16
        ins.then_inc(self.semh[k], 16)
        t = (k, self.cnt[k])
        for r in reads:
            if r.r.get(k, 0) < t[1]:
                r.r[k] = t[1]
        for w in writes:
            w.w = t
            w.r = {}
        self.ninst += 1
        return ins

    def wait_all(self, e, ress):
        need = {}
        for r in ress:
            if r.w is not None:
                k, v = r.w
                need[k] = max(need.get(k, 0), v)
            for k, v in r.r.items():
                need[k] = max(need.get(k, 0), v)
        self._waits(e, {k: (v, v) for k, v in need.items()})


class T:
    def __init__(self, kb, name, shape, dt, psum=False):
        nc = kb.nc
        if psum:
            self.t = kb.es.enter_context(nc.psum_tensor(name, shape, dt))
        else:
            self.t = kb.scope.enter_context(nc.sbuf_tensor(name, shape, dt))
        self.r = Res()

    def __getitem__(self, k):
        return self.t[k]


class _View3:
    def __init__(self, t):
        self.t, self.r = t, t.r

    def __getitem__(self, k):
        if k == slice(None):
            return self.t[:, 0:2, :]
        return self.t[k]


class _ViewC:
    def __init__(self, t, fn):
        self.t, self.r, self.v = t, t.r, fn(t)

    def __getitem__(self, k):
        return self.v[k]


class _View:
    def __init__(self, t, sl):
        self.t, self.sl, self.r = t, sl, t.r

    def __getitem__(self, k):
        a, b, c = k
        assert c == slice(None)
        return self.t[a, b, self.sl]


def build(S, DEPTH, SEG=512, TT=512, MIX=('rw', 'gla', 'ret', 'lru'), XA=True):
    nc = bass.Bass("TRN2", target_bir_lowering=False)
    es = ExitStack()
    es.enter_context(nc.allow_non_contiguous_dma(reason="small per-channel vectors / strided layouts"))
    kb = KB(nc, es)
    NSEG = S // SEG
    NT = SEG // TT
    L = DEPTH

    def din(name, shape, dt=F32):
        return nc.dram_tensor(name, list(shape), dt, kind="ExternalInput").ap()

    x = din("x", [S, D])
    out = nc.dram_tensor("out", [S, D], F32, kind="ExternalOutput").ap()
    final_norm_g = din("final_norm_g", [D])
    hT = nc.dram_tensor("hT", [D, S], F32).ap()
    hT_r = Res()

    ident = T(kb, "ident", [128, 128], F32)
    ones_f = T(kb, "ones_f", [128, 128], F32)
    kb.emit('pool', lambda e: e.memset(ones_f[:], 1.0), writes=[ones_f.r])
    kb.emit('pool', lambda e: e.memset(ident[:], 1.0), writes=[ident.r])
    kb.emit('pool', lambda e: e.affine_select(out=ident[:], in_=ident[:], pattern=[[1, 128]], base=0,
                                              channel_multiplier=-1, compare_op=ALU.is_equal, fill=0.0),
            reads=[ident.r], writes=[ident.r])

    eps_t = T(kb, "eps_t", [128, 4], F32)
    kb.emit('pool', lambda e: e.memset(eps_t[:, 0:1], NORM_EPS), writes=[eps_t.r])
    kb.emit('pool', lambda e: e.memset(eps_t[:, 1:2], RW_LN_EPS), writes=[eps_t.r])
    kb.emit('pool', lambda e: e.memset(eps_t[:, 2:3], 1.0), writes=[eps_t.r])
    kb.emit('pool', lambda e: e.memset(eps_t[:, 3:4], 0.0), writes=[eps_t.r])
    ones_b = T(kb, "ones_b", [128, 128], BF16)
    kb.emit('pool', lambda e: e.memset(ones_b[:], 1.0), writes=[ones_b.r])
    sq = T(kb, "sq", [128, TT], F32)
    rstd = T(kb, "rstd", [128, TT], F32)
    PS = [T(kb, "ps%d" % i, [128, 512], F32, psum=True) for i in range(8)]
    kb.PS = PS
    psi = [0]

    def ps():
        p = PS[psi[0] % 8]
        psi[0] += 1
        return p

    _tl = {}

    def TL(name, shape, dt):
        if name not in _tl:
            _tl[name] = T(kb, name, shape, dt)
        return _tl[name]

    def load_vec_fm(name, ap1d, n=D):
        t = TL(name, [128, n // 128], F32)
        kb.dma('sp', t[:], ap1d.rearrange("(k p) -> p k", p=128), t.r, writes=[t.r])
        return t

    fng = load_vec_fm("fng", final_norm_g)

    kb.new_scope()
    xin = [T(kb, "xin%d" % i, [128, D], F32) for i in range(2)]
    xtr = [T(kb, "xtr%d" % i, [128, KC, 128], F32) for i in range(2)]
    hT_v = hT.rearrange("(k p) s -> p k s", p=128)
    for tb in range(S // 128):
        xi = xin[tb % 2]
        xo = xtr[tb % 2]
        kb.dma('sp', xi[:], x[tb * 128:(tb + 1) * 128, :], xi.r, writes=[xi.r])
        for half in range(2):
            p = ps()
            for kk in range(4):
                k = half * 4 + kk
                kb.emit('pe', lambda e, p=p, kk=kk, k=k: e.transpose(out=p[:, kk * 128:(kk + 1) * 128],
                                                                     in_=xi[:, k * 128:(k + 1) * 128], identity=ident[:]),
                        reads=[xi.r, ident.r], writes=[p.r])
            eng = 'act' if half else 'dve'
            if eng == 'act':
                kb.emit('act', lambda e, p=p, half=half: e.activation(
                    out=xo[:, half * 4:(half + 1) * 4, :].rearrange("p k s -> p (k s)"), in_=p[:], func=AF.Copy),
                        reads=[p.r], writes=[xo.r])
            else:
                kb.emit('dve', lambda e, p=p, half=half: e.tensor_copy(
                    out=xo[:, half * 4:(half + 1) * 4, :].rearrange("p k s -> p (k s)"), in_=p[:]),
                        reads=[p.r], writes=[xo.r])
        kb.dma('pool', hT_v[:, :, tb * 128:(tb + 1) * 128], xo[:], xo.r, reads=[xo.r], writes=[hT_r])


    def rmsnorm_fm(hs, g, dst_fn, dst_res):
        p = ps()
        for k in range(KC):
            kb.emit('act', lambda e, k=k: e.activation(out=sq[:], in_=hs[:, k, :], func=AF.Square),
                    reads=[hs.r], writes=[sq.r])
            kb.emit('pe', lambda e, k=k, p=p: e.matmul(p[:], lhsT=ones_f[:], rhs=sq[:], start=(k == 0), stop=(k == KC - 1)),
                    reads=[sq.r, ones_f.r], writes=[p.r])
        kb.emit('act', lambda e, p=p: e.activation(out=rstd[:], in_=p[:], func=AF.Ln, scale=1.0 / D, bias=eps_t[:, 0:1]),
                reads=[p.r, eps_t.r], writes=[rstd.r])
        kb.emit('act', lambda e: e.activation(out=rstd[:], in_=rstd[:], func=AF.Exp, scale=-0.5),
                reads=[rstd.r], writes=[rstd.r])
        for k in range(KC):
            kb.emit('dve', lambda e, k=k: e.scalar_tensor_tensor(out=dst_fn(k), in0=hs[:, k, :], scalar=g[:, k:k + 1],
                                                                 in1=rstd[:], op0=ALU.mult, op1=ALU.mult),
                    reads=[hs.r, g.r, rstd.r], writes=[dst_res])

    kb.new_scope()
    mem = din("mem", [NMEM, D])
    positions = din("positions", [S], I32)
    P = {}
    for nm, shp in (('mix_norm_g', [2, D]), ('w_in', [2, D, N_IN]), ('rw_mu', [2, 1664]), ('rw_w0', [2, 512]),
                    ('rw_w2', [2, 64, 512]), ('rw_a0', [2, 512]), ('rw_a2', [2, 64, 512]), ('rw_k_k', [2, 512]),
                    ('rw_k_a', [2, 512]), ('rw_r_k', [2, 8, 64]), ('rw_ln_g', [2, 512]), ('rw_ln_b', [2, 512]),
                    ('gla_f_up', [2, 16, 256]), ('gla_f_b', [2, 256]), ('gla_norm_g', [2, 512]), ('ret_gn_g', [2, 512]),
                    ('lru_conv_w', [2, 4, 512]), ('lru_conv_b', [2, 512]), ('lru_wa', [2, 8, 64, 64]), ('lru_ba', [2, 512]),
                    ('lru_wx', [2, 8, 64, 64]), ('lru_bx', [2, 512]), ('lru_lambda', [2, 512]),
                    ('w_branch', [2, 4, 512, D]), ('w_out', [2, D, D]), ('xa_norm_g', [2, D]), ('xa_mem_norm_g', [2, D]),
                    ('xa_wq', [2, D, D]), ('xa_wkv', [2, D, 2 * D]), ('xa_wo', [2, D, D])):
        P[nm] = din(nm, shp)

    def dscr(name, shape, dt=BF16):
        return nc.dram_tensor(name, list(shape), dt).ap()

    Wb = [dscr("Wb%d" % l, [D, NAUG]) for l in range(L)]
    WBR = [dscr("WBR%d" % l, [4 * 512, D]) for l in range(L)]
    WO = [dscr("WO%d" % l, [D, D]) for l in range(L)]
    WQ = [dscr("WQ%d" % l, [D, D]) for l in range(L)]
    WKV = [dscr("WKV%d" % l, [D, 2 * D]) for l in range(L)]
    WXO = [dscr("WXO%d" % l, [D, D]) for l in range(L)]
    wdram_r = Res()

    PW = 2048
    wld = [T(kb, "wld%d" % i, [128, PW], F32) for i in range(2)]
    wst = [T(kb, "wst%d" % i, [128, PW], BF16) for i in range(2)]
    scl = T(kb, "scl", [128, 512], F32)
    pi = [0]

    def prep(src, dst, rows, n, scale=None, swap=False):
        for rc in range(rows // 128):
            a = wld[pi[0] % 2]
            b = wst[pi[0] % 2]
            pi[0] += 1
            kb.dma('sp', a[:, 0:n], src[rc * 128:(rc + 1) * 128, :], a.r, writes=[a.r])
            if swap:
                av = a[:, 0:n].rearrange("p (h two d) -> p h two d", two=2, d=32)
                bv = b[:, 0:n].rearrange("p (h two d) -> p h two d", two=2, d=32)
                kb.emit('dve', lambda e: e.tensor_scalar(out=bv[:, :, 0, :], in0=av[:, :, 1, :], scalar1=-1.0, scalar2=None, op0=ALU.mult),
                        reads=[a.r], writes=[b.r])
                kb.emit('dve', lambda e: e.tensor_copy(out=bv[:, :, 1, :], in_=av[:, :, 0, :]), reads=[a.r, b.r], writes=[b.r])
            elif scale is None:
                if pi[0] % 2:
                    kb.emit('act', lambda e: e.activation(out=b[:, 0:n], in_=a[:, 0:n], func=AF.Copy),
                            reads=[a.r], writes=[b.r])
                else:
                    kb.emit('dve', lambda e: e.tensor_copy(out=b[:, 0:n], in_=a[:, 0:n]), reads=[a.r], writes=[b.r])
            else:
                kb.emit('dve', lambda e: e.tensor_tensor(out=b[:, 0:n], in0=a[:, 0:n], in1=scale[:, 0:n], op=ALU.mult),
                        reads=[a.r, scale.r], writes=[b.r])
            kb.dma('pool', dst[rc * 128:(rc + 1) * 128, :], b[:, 0:n], b.r, reads=[b.r], writes=[wdram_r])

    memt = T(kb, "memt", [128, 2, D], F32)
    memT32 = T(kb, "memT32", [128, KC, NMEM], F32)
    memT_d = nc.dram_tensor("memT_d", [D, NMEM], F32).ap()
    memT_r = Res()
    kb.dma('sp', memt[:], mem.rearrange("(b p) d -> p b d", p=128), memt.r, writes=[memt.r])
    for bk in range(2):
        for half in range(2):
            p = ps()
            for kk in range(4):
                k = half * 4 + kk
                kb.emit('pe', lambda e, p=p, kk=kk, k=k, bk=bk: e.transpose(out=p[:, kk * 128:(kk + 1) * 128],
                                                                            in_=memt[:, bk, k * 128:(k + 1) * 128], identity=ident[:]),
                        reads=[memt.r, ident.r], writes=[p.r])
            kb.emit('dve', lambda e, p=p, half=half, bk=bk: e.tensor_copy(
                out=memT32[:, half * 4:(half + 1) * 4, bk * 128:(bk + 1) * 128],
                in_=p[:].rearrange("p (k s) -> p k s", k=4)), reads=[p.r], writes=[memT32.r])


    kb.dma('pool', memT_d.rearrange("(k p) m -> p k m", p=128), memT32[:], memT32.r, reads=[memT32.r], writes=[memT_r])
    for l in range(L):
        for (an, w, so, kind, arg) in AUG:
            src = P['w_in'][l, :, so:so + w]
            dst = Wb[l][:, AUGOFF[an]:AUGOFF[an] + w]
            if kind == 'plain':
                prep(src, dst, D, w)
            elif kind == 'swap':
                prep(src, dst, D, w, swap=True)
            else:
                if kind in ('mu', 'mu1m'):
                    row = P['rw_mu'][l, arg:arg + w]
                else:
                    row = P['lru_conv_w'][l, arg, :]
                kb.dma('sp', scl[:, 0:w], row.partition_broadcast(128), scl.r, writes=[scl.r])
                if kind == 'mu1m':
                    kb.emit('dve', lambda e, w=w: e.tensor_scalar(out=scl[:, 0:w], in0=scl[:, 0:w], scalar1=-1.0, scalar2=1.0,
                                                                   op0=ALU.mult, op1=ALU.add), reads=[scl.r], writes=[scl.r])
                prep(src, dst, D, w, scale=scl)
        prep(P['w_branch'][l].rearrange("n r c -> (n r) c"), WBR[l], 4 * 512, D)
        prep(P['w_out'][l], WO[l], D, D)
        prep(P['xa_wq'][l], WQ[l], D, D)
        prep(P['xa_wkv'][l], WKV[l], D, 2 * D)
        prep(P['xa_wo'][l], WXO[l], D, D)

    kb.new_scope()
    _tl.clear()
    kT = T(kb, "kT", [128, KC, NMEM], BF16)
    vtm = T(kb, "vtm", [128, 2, D], BF16)
    hseg = T(kb, "hseg", [128, KC, SEG], F32)
    un = T(kb, "un", [128, KC, 3 + SEG], BF16)
    halo = T(kb, "halo", [128, KC, 3], BF16)
    yg = [T(kb, "yg%d" % n, [128, 4, SEG], BF16) for n in range(4)]
    merged = T(kb, "merged", [128, KC, SEG], BF16)
    NW = 3
    NC8_ = TT // 64
    twb = T(kb, "twb", [64, TT], BF16)
    alb = T(kb, "alb", [64, TT], BF16)
    AR = T(kb, "AR", [128, NC8_, 128], BF16)
    BK32 = T(kb, "BK32", [128, 2, TT], F32)
    BKb = T(kb, "BKb", [128, 2, TT], BF16)
    VT = T(kb, "VT", [128, NC8_, 64], BF16)
    BTm = T(kb, "BTm", [128, NC8_, 64], BF16)
    KTm = T(kb, "KTm", [128, NC8_, 64], BF16)
    Mall = T(kb, "Mall", [128, NC8_, 320], BF16)
    AN = [T(kb, "AN%d" % i, [128, NC8_, 128], BF16) for i in range(2)]
    TTb = T(kb, "TTb", [128, NC8_, 64], BF16)
    vb = T(kb, "vb", [128, TT], BF16)
    Xb = T(kb, "Xb", [128, 64], BF16)
    Ub = T(kb, "Ub", [128, 64], BF16)
    gC = T(kb, "gC", [128, NC8_], F32)
    def _interleave(gens):
        gens = [g for g in gens if g is not None]
        while gens:
            for g in list(gens):
                try:
                    next(g)
                except StopIteration:
                    gens.remove(g)
    identb = T(kb, "identb", [128, 1, 64], BF16)
    kb.emit('dve', lambda e: e.tensor_copy(out=identb[0:64, 0, :], in_=ident[0:64, 0:64]), reads=[ident.r], writes=[identb.r])
    kb.emit('dve', lambda e: e.tensor_copy(out=identb[64:128, 0, :], in_=ident[64:128, 64:128]), reads=[ident.r], writes=[identb.r])
    ones_bd = T(kb, "ones_bd", [128, 128], F32)
    kb.emit('pool', lambda e: e.memset(ones_bd[:], 0.0), writes=[ones_bd.r])
    kb.emit('pool', lambda e: e.memset(ones_bd[0:64, 0:64], 1.0), writes=[ones_bd.r])
    kb.emit('pool', lambda e: e.memset(ones_bd[64:128, 64:128], 1.0), writes=[ones_bd.r])
    maskR = T(kb, "maskR", [128, 320], F32)
    kb.emit('pool', lambda e: e.memset(maskR[:], 1.0), writes=[maskR.r])
    for hv_ in (slice(0, 64), slice(64, 128)):
        for (c0_, strict, transposed) in ((0, True, False), (64, False, False), (128, True, False), (192, False, False), (256, True, True)):
            kb.emit('pool', lambda e, c0_=c0_, strict=strict, transposed=transposed, hv_=hv_: e.affine_select(
                out=maskR[hv_, c0_:c0_ + 64], in_=maskR[hv_, c0_:c0_ + 64], pattern=[[-1 if transposed else 1, 64]], base=0,
                channel_multiplier=(1 if transposed else -1), compare_op=(ALU.is_gt if strict else ALU.is_ge), fill=0.0),
                    reads=[maskR.r], writes=[maskR.r])
    vT = [T(kb, "vT%d" % i, [128, 512], BF16) for i in range(TT // 128)]
    qd = T(kb, "qd", [64, 4, TT], BF16)
    kd = T(kb, "kd", [64, 4, TT], BF16)
    kd32 = T(kb, "kd32", [64, 4, TT], F32)
    kdT = T(kb, "kdT", [128, 256], BF16)
    attb = T(kb, "attb", [128, 512], BF16)
    oall = T(kb, "oall", [128, 4, TT], F32)
    ebl = T(kb, "ebl", [64, 4, TT // 128], F32)
    flo_b = T(kb, "flo_b", [16, TT], BF16)
    wflo = T(kb, "wflo", [128, KC, 16], BF16)
    posi = T(kb, "posi", [64, TT], I32)
    cosT = T(kb, "cosT", [128, TT], F32)
    sinT = T(kb, "sinT", [128, TT], F32)
    yT, bonus = cosT, sinT
    tmp2 = T(kb, "tmp2", [128, TT], F32)
    sgate = T(kb, "sgate", [128, TT], BF16)
    RWB = [(AR, BKb, VT, BTm, KTm, gC, sinT, sgate),
           (T(kb, "AR_b", [128, NC8_, 128], BF16), T(kb, "BKb_b", [128, 2, TT], BF16), T(kb, "VT_b", [128, NC8_, 64], BF16),
            T(kb, "BTm_b", [128, NC8_, 64], BF16), T(kb, "KTm_b", [128, NC8_, 64], BF16), T(kb, "gC_b", [128, NC8_], F32),
            T(kb, "bonus_b", [128, TT], F32), T(kb, "sgate_b", [128, TT], BF16))]

    mask4 = T(kb, "mask4", [128, 4, 128], F32)
    kb.emit('pool', lambda e: e.memset(mask4[:], 1.0), writes=[mask4.r])
    kb.emit('pool', lambda e: e.affine_select(out=mask4[:], in_=mask4[:], pattern=[[0, 4], [1, 128]], base=0, channel_multiplier=-1,
                                              compare_op=ALU.is_ge, fill=0.0), reads=[mask4.r], writes=[mask4.r])
    ebR = T(kb, "ebR", [64, 4, 128], F32)
    enbR = T(kb, "enbR", [64, 4, 128], F32)
    iot = T(kb, "iot", [64, 128], F32)
    invf = T(kb, "invf", [64, 1], F32)
    kb.emit('pool', lambda e: e.iota(iot[:], pattern=[[1, 128]], base=1, channel_multiplier=0, allow_small_or_imprecise_dtypes=True), writes=[iot.r])
    for h in range(4):
        lg = float(np.log(RET_G[h]))
        kb.emit('act', lambda e, h=h, lg=lg: e.activation(out=ebR[:, h, :], in_=iot[:], func=AF.Exp, scale=lg), reads=[iot.r], writes=[ebR.r])
        kb.emit('act', lambda e, h=h, lg=lg: e.activation(out=enbR[:, h, :], in_=iot[:], func=AF.Exp, scale=-lg), reads=[iot.r], writes=[enbR.r])
    kb.emit('dve', lambda e: e.tensor_scalar(out=enbR[:], in0=enbR[:], scalar1=0.125, scalar2=None, op0=ALU.mult), reads=[enbR.r], writes=[enbR.r])
    pidx = T(kb, "pidx", [64, 2], F32)
    kb.emit('pool', lambda e: e.iota(pidx[:, 0:1], pattern=[[0, 1]], base=0, channel_multiplier=1, allow_small_or_imprecise_dtypes=True), writes=[pidx.r])
    kb.emit('dve', lambda e: e.tensor_scalar(out=pidx[:, 1:2], in0=pidx[:, 0:1], scalar1=32.0, scalar2=-32.0, op0=ALU.is_ge, op1=ALU.mult), reads=[pidx.r], writes=[pidx.r])
    kb.emit('dve', lambda e: e.tensor_tensor(out=pidx[:, 0:1], in0=pidx[:, 0:1], in1=pidx[:, 1:2], op=ALU.add), reads=[pidx.r], writes=[pidx.r])
    kb.emit('act', lambda e: e.activation(out=invf[:], in_=pidx[:, 0:1], func=AF.Exp, scale=float(-np.log(10000.0) / 32.0)), reads=[pidx.r], writes=[invf.r])
    wt = [T(kb, "wt%d" % i, [128, KC, 512], BF16) for i in range(NW)]
    wi = [0]

    def wtile():
        t = wt[wi[0] % NW]
        wi[0] += 1
        return t

    def load_w(t, src2d, ncols, col0=0, kc=KC, batch=False):
        kb.dma('sp', t[:, 0:kc, col0:col0 + ncols], src2d.rearrange("(k p) c -> p k c", p=128), t.r,
               reads=[wdram_r], writes=[t.r], batch=batch)

    WK = [T(kb, "wk%d" % i, [128, TT], F32) for i in range(10)]
    WB16 = [T(kb, "wb%d" % i, [128, TT], BF16) for i in range(4)]

    def fm_proj(dst_ps, wtile_, c0, ncols, t0, shift=0, start=True, stop=True, kc=KC):
        for k in range(kc):
            kb.emit('pe', lambda e, k=k: e.matmul(dst_ps[0:ncols, :], lhsT=wtile_[:, k, c0:c0 + ncols],
                                                  rhs=un[:, k, 3 + t0 + shift:3 + t0 + shift + TT],
                                                  start=(start and k == 0), stop=(stop and k == kc - 1)),
                    reads=[wtile_.r, un.r], writes=[dst_ps.r])

    def rmsnorm_cols(src, g, dst, ncol):
        p = ps()
        for k in range(KC):
            kb.emit('act', lambda e, k=k: e.activation(out=sq[:, 0:ncol], in_=src[:, k, 0:ncol], func=AF.Square),
                    reads=[src.r], writes=[sq.r])
            kb.emit('pe', lambda e, k=k, p=p: e.matmul(p[:, 0:ncol], lhsT=ones_f[:], rhs=sq[:, 0:ncol], start=(k == 0), stop=(k == KC - 1)),
                    reads=[sq.r, ones_f.r], writes=[p.r])
        kb.emit('act', lambda e, p=p: e.activation(out=rstd[:, 0:ncol], in_=p[:, 0:ncol], func=AF.Ln, scale=1.0 / D, bias=eps_t[:, 0:1]),
                reads=[p.r, eps_t.r], writes=[rstd.r])
        kb.emit('act', lambda e: e.activation(out=rstd[:, 0:ncol], in_=rstd[:, 0:ncol], func=AF.Exp, scale=-0.5),
                reads=[rstd.r], writes=[rstd.r])
        for k in range(KC):
            kb.emit('dve', lambda e, k=k: e.scalar_tensor_tensor(out=dst[:, k, 0:ncol], in0=src[:, k, 0:ncol], scalar=g[:, k:k + 1],
                                                                 in1=rstd[:, 0:ncol], op0=ALU.mult, op1=ALU.mult),
                    reads=[src.r, g.r, rstd.r], writes=[dst.r])

    for l in range(L):
        mng = load_vec_fm("mng", P['mix_norm_g'][l])
        xng = load_vec_fm("xng", P['xa_norm_g'][l])
        xmg = load_vec_fm("xmg", P['xa_mem_norm_g'][l])
        mnT = _ViewC(merged, lambda t: t[:, :, 0:NMEM])
        mem32 = _ViewC(oall, lambda t: t[:].rearrange("p h (a t) -> p (h a) t", a=2))
        kb.dma('sp', mem32[:, :, :], memT_d.rearrange("(k p) m -> p k m", p=128), oall.r, reads=[memT_r], writes=[oall.r])
        rmsnorm_cols(mem32, xmg, mnT, NMEM)
        for j in range(KC):
            w = wtile()
            load_w(w, WKV[l][:, j * 128:(j + 1) * 128], 128)
            p = ps()
            for k in range(KC):
                kb.emit('pe', lambda e, k=k, p=p, w=w: e.matmul(p[:, 0:NMEM], lhsT=w[:, k, 0:128], rhs=mnT[:, k, :],
                                                                start=(k == 0), stop=(k == KC - 1)),
                        reads=[w.r, mnT.r], writes=[p.r])
            kb.emit('act', lambda e, p=p, j=j: e.activation(out=kT[:, j, :], in_=p[:, 0:NMEM], func=AF.Copy),
                    reads=[p.r], writes=[kT.r])
        for c2 in range(2):
            w = wtile()
            load_w(w, WKV[l][:, D + c2 * 512:D + (c2 + 1) * 512], 512)
            for bk in range(2):
                p = ps()
                for k in range(KC):
                    kb.emit('pe', lambda e, k=k, p=p, w=w, bk=bk: e.matmul(p[:], lhsT=mnT[:, k, bk * 128:(bk + 1) * 128], rhs=w[:, k, :],
                                                                           start=(k == 0), stop=(k == KC - 1)),
                            reads=[w.r, mnT.r], writes=[p.r])
                kb.emit('act', lambda e, p=p, bk=bk, c2=c2: e.activation(out=vtm[:, bk, c2 * 512:(c2 + 1) * 512], in_=p[:], func=AF.Copy),
                        reads=[p.r], writes=[vtm.r])

        if 'rw' in MIX:
            def hv(name, ap1d):
                t_ = TL(name, [128, 4], F32)
                kb.dma('sp', t_[:], ap1d.rearrange("(hp p) -> p hp", p=128), t_.r, writes=[t_.r])
                return t_
            w0t = hv("w0t", P['rw_w0'][l])
            a0t = hv("a0t", P['rw_a0'][l])
            kkt = hv("kkt", P['rw_k_k'][l])
            kat = hv("kat", P['rw_k_a'][l])
            rkt = hv("rkt", P['rw_r_k'][l].rearrange("h k -> (h k)"))
            lngt = hv("lngt", P['rw_ln_g'][l])
            lnbt = hv("lnbt", P['rw_ln_b'][l])
            w2b = TL("w2b", [64, 512], BF16)
            a2b = TL("a2b", [64, 512], BF16)
            for (nmw, dstb, stg) in (('rw_w2', w2b, WK[2]), ('rw_a2', a2b, WK[3])):
                kb.dma('sp', stg[0:64, :], P[nmw][l], stg.r, writes=[stg.r])
                kb.emit('dve', lambda e, dstb=dstb, stg=stg: e.tensor_copy(out=dstb[:], in_=stg[0:64, :]), reads=[stg.r], writes=[dstb.r])
            STs = [TL("ST_%d" % h_, [128, 64], BF16) for h_ in range(4)]
            for t_ in STs:
                kb.emit('pool', lambda e, t_=t_: e.memset(t_[:], 0.0), writes=[t_.r])
        if 'gla' in MIX or 'ret' in MIX:
            fup32 = TL("fup32", [16, 256], F32)
            fup_b = T(kb, "fup_b%d" % l, [16, 256], BF16)
            kb.dma('sp', fup32[:], P['gla_f_up'][l], fup32.r, writes=[fup32.r])
            kb.emit('dve', lambda e: e.tensor_copy(out=fup_b[:], in_=fup32[:]), reads=[fup32.r], writes=[fup_b.r])
            nfb = TL("nfb", [64, 4], F32)
            kb.dma('sp', nfb[:], P['gla_f_b'][l].rearrange("(h d) -> d h", d=64), nfb.r, writes=[nfb.r])
            kb.emit('dve', lambda e: e.tensor_scalar(out=nfb[:], in0=nfb[:], scalar1=-1.0, scalar2=None, op0=ALU.mult), reads=[nfb.r], writes=[nfb.r])
            gng = load_vec_fm("gng", P['gla_norm_g'][l], 512)
            rgg = load_vec_fm("rgg", P['ret_gn_g'][l], 512)
            Sg32 = TL("Sg32", [64, 4, 128], F32)
            Sgb = TL("Sgb", [64, 4, 128], BF16)
            Sr32 = TL("Sr32", [64, 4, 128], F32)
            Srb = TL("Srb", [64, 4, 128], BF16)
            for t_ in (Sg32, Sgb, Sr32, Srb):
                kb.emit('pool', lambda e, t_=t_: e.memset(t_[:], 0.0), writes=[t_.r])
        if 'lru' in MIX:
            lcb = load_vec_fm("lcb", P['lru_conv_b'][l], 512)
            lba = load_vec_fm("lba", P['lru_ba'][l], 512)
            lbx = load_vec_fm("lbx", P['lru_bx'][l], 512)
            llam = load_vec_fm("llam", P['lru_lambda'][l], 512)
            lc = TL("lc", [128, 4], F32)
            lc2 = TL("lc2", [128, 4], F32)
            kb.emit('act', lambda e: e.activation(out=lc[:], in_=llam[:], func=AF.Exp, scale=-1.0), reads=[llam.r], writes=[lc.r])
            kb.emit('act', lambda e: e.activation(out=lc[:], in_=lc[:], func=AF.Ln, bias=eps_t[:, 2:3]), reads=[lc.r, eps_t.r], writes=[lc.r])
            kb.emit('dve', lambda e: e.tensor_scalar(out=lc2[:], in0=lc[:], scalar1=-16.0, scalar2=None, op0=ALU.mult), reads=[lc.r], writes=[lc2.r])
            kb.emit('dve', lambda e: e.tensor_scalar(out=lc[:], in0=lc[:], scalar1=-8.0, scalar2=None, op0=ALU.mult), reads=[lc.r], writes=[lc.r])
            wab = TL("wab", [128, 2, 4, 128], BF16)
            for wi_, nmw in enumerate(('lru_wa', 'lru_wx')):
                stg = WK[wi_]
                sv = stg[:].rearrange("p (j o) -> p j o", j=4)
                kb.emit('pool', lambda e, stg=stg: e.memset(stg[:], 0.0), writes=[stg.r])
                for bk in range(8):
                    j, hb = bk // 2, bk % 2
                    kb.dma('sp', sv[hb * 64:(hb + 1) * 64, j, hb * 64:(hb + 1) * 64], P[nmw][l, bk], stg.r,
                           writes=[stg.r], batch=(bk > 0))
                kb.emit('dve', lambda e, sv=sv, wi_=wi_: e.tensor_copy(out=wab[:, wi_, :, :], in_=sv), reads=[stg.r], writes=[wab.r])
            lcarry = TL("lcarry", [128, 4], F32)
            kb.emit('pool', lambda e: e.memset(lcarry[:], 0.0), writes=[lcarry.r])

        for sg in range(NSEG):
            s0 = sg * SEG
            kb.dma('sp', hseg[:], hT_v[:, :, s0:s0 + SEG], hseg.r, reads=[hT_r], writes=[hseg.r])
            if sg == 0:
                kb.emit('dve', lambda e: e.memset(un[:, :, 0:3], 0.0), writes=[un.r])
            else:
                kb.emit('dve', lambda e: e.tensor_copy(out=un[:, :, 0:3], in_=halo[:]), reads=[halo.r], writes=[un.r])
            for tt in range(NT):
                hv = _View(hseg, slice(tt * TT, (tt + 1) * TT))
                rmsnorm_fm(hv, mng, lambda k, tt=tt: un[:, k, 3 + tt * TT:3 + (tt + 1) * TT], un.r)
            kb.emit('dve', lambda e: e.tensor_copy(out=halo[:], in_=un[:, :, SEG:SEG + 3]), reads=[un.r], writes=[halo.r])

            for n in range(4):
                if ('rw', 'gla', 'ret', 'lru')[n] not in MIX:
                    kb.emit('pool', lambda e, n=n: e.memset(yg[n][:], 0.0), writes=[yg[n].r])

            if 'lru' in MIX:
                for j in range(4):
                    w = wtile()
                    for v in range(4):
                        c = AUGOFF['lru_x%d' % v] + j * 128
                        load_w(w, Wb[l][:, c:c + 128], 128, col0=v * 128, batch=(v > 0))
                    wg = wtile()
                    c = AUGOFF['lru_g'] + j * 128
                    load_w(wg, Wb[l][:, c:c + 128], 128)
                    for tt in range(NT):
                        t0 = tt * TT
                        _o = 5 * ((j * NT + tt) % 2)
                        xc, xcb, rr, ii, aa, uu = WK[_o], WB16[(j * NT + tt) % 2], WK[_o + 1], WK[_o + 2], WK[_o + 3], WK[_o + 4]
                        hh = rr
                        p = ps()
                        for v in range(4):
                            fm_proj(p, w, v * 128, 128, t0, shift=v - 3, start=(v == 0), stop=(v == 3))
                        kb.emit('act', lambda e, p=p, j=j: e.activation(out=xc[:], in_=p[:], func=AF.Identity, bias=lcb[:, j:j + 1]),
                                reads=[p.r, lcb.r], writes=[xc.r])
                        kb.emit('dve', lambda e: e.tensor_copy(out=xcb[:], in_=xc[:]), reads=[xc.r], writes=[xcb.r])
                        p1 = ps()
                        kb.emit('pe', lambda e, p1=p1, j=j: e.matmul(p1[:], lhsT=wab[:, 0, j, :], rhs=xcb[:], start=True, stop=True),
                                reads=[wab.r, xcb.r], writes=[p1.r])
                        p2 = ps()
                        kb.emit('pe', lambda e, p2=p2, j=j: e.matmul(p2[:], lhsT=wab[:, 1, j, :], rhs=xcb[:], start=True, stop=True),
                                reads=[wab.r, xcb.r], writes=[p2.r])
                        kb.emit('act', lambda e, p1=p1, j=j: e.activation(out=rr[:], in_=p1[:], func=AF.Sigmoid, bias=lba[:, j:j + 1]),
                                reads=[p1.r, lba.r], writes=[rr.r])
                        kb.emit('act', lambda e, p2=p2, j=j: e.activation(out=ii[:], in_=p2[:], func=AF.Sigmoid, bias=lbx[:, j:j + 1]),
                                reads=[p2.r, lbx.r], writes=[ii.r])
                        kb.emit('act', lambda e, j=j: e.activation(out=aa[:], in_=rr[:], func=AF.Exp, scale=lc[:, j:j + 1]),
                                reads=[rr.r, lc.r], writes=[aa.r])
                        kb.emit('act', lambda e, j=j: e.activation(out=uu[:], in_=rr[:], func=AF.Exp, scale=lc2[:, j:j + 1]),
                                reads=[rr.r, lc2.r], writes=[uu.r])
                        kb.emit('act', lambda e: e.activation(out=uu[:], in_=uu[:], func=AF.Ln, scale=-1.0, bias=eps_t[:, 2:3]),
                                reads=[uu.r, eps_t.r], writes=[uu.r])
                        kb.emit('act', lambda e: e.activation(out=uu[:], in_=uu[:], func=AF.Exp, scale=0.5), reads=[uu.r], writes=[uu.r])
                        kb.emit('dve', lambda e: e.tensor_tensor(out=ii[:], in0=ii[:], in1=xc[:], op=ALU.mult), reads=[ii.r, xc.r], writes=[ii.r])
                        kb.emit('dve', lambda e: e.tensor_tensor(out=uu[:], in0=uu[:], in1=ii[:], op=ALU.mult), reads=[uu.r, ii.r], writes=[uu.r])
                        kb.emit('dve', lambda e, j=j: e.tensor_tensor_scan(out=hh[:], data0=aa[:], data1=uu[:], initial=lcarry[:, j:j + 1],
                                                                           op0=ALU.mult, op1=ALU.add),
                                reads=[aa.r, uu.r, lcarry.r], writes=[hh.r])
                        kb.emit('act', lambda e, j=j: e.activation(out=lcarry[:, j:j + 1], in_=hh[:, TT - 1:TT], func=AF.Copy),
                                reads=[hh.r], writes=[lcarry.r])
                        p3 = ps()
                        fm_proj(p3, wg, 0, 128, t0)
                        kb.emit('act', lambda e, p3=p3: e.activation(out=ii[:], in_=p3[:], func=AF.Silu), reads=[p3.r], writes=[ii.r])
                        kb.emit('dve', lambda e, j=j, t0=t0: e.tensor_tensor(out=yg[3][:, j, t0:t0 + TT], in0=hh[:], in1=ii[:], op=ALU.mult),
                                reads=[hh.r, ii.r], writes=[yg[3].r])

            if 'rw' in MIX:
                NC8 = TT // 64
                C0 = float(np.exp(-0.5))
                HV = (slice(0, 64), slice(64, 128))
                for tt in range(NT):
                    t0 = tt * TT
                    wc = wtile()
                    for i_, nm_ in enumerate(('rw_wlo_c', 'rw_wlo_p', 'rw_alo_c', 'rw_alo_p')):
                        load_w(wc, Wb[l][:, AUGOFF[nm_]:AUGOFF[nm_] + 64], 64, col0=i_ * 64, batch=(i_ > 0))
                    p = ps()
                    fm_proj(p, wc, 0, 64, t0, start=True, stop=False)
                    fm_proj(p, wc, 64, 64, t0, shift=-1, start=False, stop=True)
                    kb.emit('act', lambda e, p=p: e.activation(out=twb[:], in_=p[0:64, :], func=AF.Tanh), reads=[p.r], writes=[twb.r])
                    p = ps()
                    fm_proj(p, wc, 128, 64, t0, start=True, stop=False)
                    fm_proj(p, wc, 192, 64, t0, shift=-1, start=False, stop=True)
                    kb.emit('act', lambda e, p=p: e.activation(out=alb[:], in_=p[0:64, :], func=AF.Copy), reads=[p.r], writes=[alb.r])
                    def P1(hp, B):
                        AR, BKb, VT, BTm, KTm, gC, bonus, sgate = B
                        wa_ = wtile()
                        for i_, nm_ in enumerate(('rw_r_c', 'rw_r_p', 'rw_k_c', 'rw_k_p')):
                            c = AUGOFF[nm_] + hp * 128
                            load_w(wa_, Wb[l][:, c:c + 128], 128, col0=i_ * 128, batch=(i_ > 0))
                        wb_ = wtile()
                        for i_, nm_ in enumerate(('rw_v_c', 'rw_v_p', 'rw_g')):
                            c = AUGOFF[nm_] + hp * 128
                            load_w(wb_, Wb[l][:, c:c + 128], 128, col0=i_ * 128, batch=(i_ > 0))
                        r32, k32, v32, sg, asg, kkn, kmod, cs, eG, tmp = [WK[i] for i in range(10)]

                        def proj2(wt_, ccur, cprev):
                            pp = ps()
                            fm_proj(pp, wt_, ccur, 128, t0, start=True, stop=False)
                            fm_proj(pp, wt_, cprev, 128, t0, shift=-1, start=False, stop=True)
                            return pp
                        pr = proj2(wa_, 0, 128)
                        kb.emit('act', lambda e, pr=pr: e.activation(out=r32[:], in_=pr[:], func=AF.Copy), reads=[pr.r], writes=[r32.r])
                        yield
                        pk = proj2(wa_, 256, 384)
                        kb.emit('act', lambda e, pk=pk: e.activation(out=k32[:], in_=pk[:], func=AF.Copy), reads=[pk.r], writes=[k32.r])
                        yield
                        pv = proj2(wb_, 0, 128)
                        kb.emit('act', lambda e, pv=pv: e.activation(out=v32[:], in_=pv[:], func=AF.Copy), reads=[pv.r], writes=[v32.r])
                        yield
                        pw = ps()
                        kb.emit('pe', lambda e, pw=pw, hp=hp: e.matmul(pw[:], lhsT=w2b[:, hp * 128:(hp + 1) * 128], rhs=twb[:], start=True, stop=True),
                                reads=[w2b.r, twb.r], writes=[pw.r])
                        kb.emit('act', lambda e, pw=pw, hp=hp: e.activation(out=sg[:], in_=pw[:], func=AF.Sigmoid, bias=w0t[:, hp:hp + 1]),
                                reads=[pw.r, w0t.r], writes=[sg.r])
                        yield
                        pa_ = ps()
                        kb.emit('pe', lambda e, pa_=pa_, hp=hp: e.matmul(pa_[:], lhsT=a2b[:, hp * 128:(hp + 1) * 128], rhs=alb[:], start=True, stop=True),
                                reads=[a2b.r, alb.r], writes=[pa_.r])
                        kb.emit('act', lambda e, pa_=pa_, hp=hp: e.activation(out=asg[:], in_=pa_[:], func=AF.Sigmoid, bias=a0t[:, hp:hp + 1]),
                                reads=[pa_.r, a0t.r], writes=[asg.r])
                        yield
                        kb.emit('dve', lambda e, hp=hp: e.tensor_scalar(out=kkn[:], in0=k32[:], scalar1=kkt[:, hp:hp + 1], scalar2=None, op0=ALU.mult),
                                reads=[k32.r, kkt.r], writes=[kkn.r])
                        yield
                        kb.emit('act', lambda e: e.activation(out=tmp[:], in_=kkn[:], func=AF.Square), reads=[kkn.r], writes=[tmp.r])
                        yield
                        pn = ps()
                        kb.emit('pe', lambda e, pn=pn: e.matmul(pn[:], lhsT=ones_bd[:], rhs=tmp[:], start=True, stop=True), reads=[ones_bd.r, tmp.r], writes=[pn.r])
                        kb.emit('act', lambda e, pn=pn: e.activation(out=tmp[:], in_=pn[:], func=AF.Ln, bias=eps_t[:, 3:4]), reads=[pn.r, eps_t.r], writes=[tmp.r])
                        yield
                        kb.emit('act', lambda e: e.activation(out=tmp[:], in_=tmp[:], func=AF.Exp, scale=-0.5), reads=[tmp.r], writes=[tmp.r])
                        yield
                        kb.emit('dve', lambda e: e.tensor_tensor(out=kkn[:], in0=kkn[:], in1=tmp[:], op=ALU.mult), reads=[kkn.r, tmp.r], writes=[kkn.r])
                        yield
                        kb.emit('dve', lambda e, hp=hp: e.tensor_scalar(out=tmp[:], in0=asg[:], scalar1=-1.0, scalar2=kat[:, hp:hp + 1], op0=ALU.add, op1=ALU.mult),
                                reads=[asg.r, kat.r], writes=[tmp.r])
                        yield
                        kb.emit('dve', lambda e: e.scalar_tensor_tensor(out=kmod[:], in0=tmp[:], scalar=1.0, in1=k32[:], op0=ALU.add, op1=ALU.mult),
                                reads=[tmp.r, k32.r], writes=[kmod.r])
                        yield
                        kb.emit('dve', lambda e, hp=hp: e.scalar_tensor_tensor(out=tmp[:], in0=r32[:], scalar=rkt[:, hp:hp + 1], in1=kmod[:], op0=ALU.mult, op1=ALU.mult),
                                reads=[r32.r, rkt.r, kmod.r], writes=[tmp.r])
                        yield
                        pbn = ps()
                        kb.emit('pe', lambda e, pbn=pbn: e.matmul(pbn[:], lhsT=ones_bd[:], rhs=tmp[:], start=True, stop=True), reads=[ones_bd.r, tmp.r], writes=[pbn.r])
                        kb.emit('dve', lambda e, pbn=pbn: e.tensor_tensor(out=bonus[:], in0=pbn[:], in1=v32[:], op=ALU.mult), reads=[pbn.r, v32.r], writes=[bonus.r])
                        yield
                        for c in range(NC8):
                            kb.emit('dve', lambda e, c=c: e.tensor_tensor_scan(out=cs[:, c * 64:(c + 1) * 64], data0=ones_f[:, 0:64], data1=sg[:, c * 64:(c + 1) * 64],
                                                                               initial=0.0, op0=ALU.mult, op1=ALU.add), reads=[sg.r, ones_f.r], writes=[cs.r])
                            yield
                        kb.emit('act', lambda e: e.activation(out=eG[:], in_=cs[:], func=AF.Exp, scale=-C0), reads=[cs.r], writes=[eG.r])
                        yield
                        kb.emit('act', lambda e: e.activation(out=gC[:], in_=eG[:].rearrange("p (c t) -> p c t", t=64)[:, :, 63], func=AF.Copy), reads=[eG.r], writes=[gC.r])
                        yield
                        kb.emit('dve', lambda e: e.tensor_tensor(out=AR[:, :, 64:128], in0=r32[:].rearrange("p (c t) -> p c t", t=64),
                                                                 in1=eG[:].rearrange("p (c t) -> p c t", t=64), op=ALU.mult), reads=[r32.r, eG.r], writes=[AR.r])
                        yield
                        kb.emit('dve', lambda e: e.tensor_tensor(out=tmp[:], in0=cs[:], in1=sg[:], op=ALU.subtract), reads=[cs.r, sg.r], writes=[tmp.r])
                        yield
                        kb.emit('act', lambda e: e.activation(out=tmp[:], in_=tmp[:], func=AF.Exp, scale=-C0), reads=[tmp.r], writes=[tmp.r])
                        yield
                        kb.emit('dve', lambda e: e.scalar_tensor_tensor(out=AR[:, :, 0:64], in0=kkn[:].rearrange("p (c t) -> p c t", t=64), scalar=-1.0,
                                                                        in1=tmp[:].rearrange("p (c t) -> p c t", t=64), op0=ALU.mult, op1=ALU.mult),
                                reads=[kkn.r, tmp.r], writes=[AR.r])
                        yield
                        kb.emit('act', lambda e: e.activation(out=eG[:], in_=cs[:], func=AF.Exp, scale=C0), reads=[cs.r], writes=[eG.r])
                        yield
                        kb.emit('dve', lambda e: e.tensor_tensor(out=tmp[:], in0=kkn[:], in1=asg[:], op=ALU.mult), reads=[kkn.r, asg.r], writes=[tmp.r])
                        yield
                        kb.emit('dve', lambda e: e.tensor_tensor(out=BK32[:, 0, :], in0=tmp[:], in1=eG[:], op=ALU.mult), reads=[tmp.r, eG.r], writes=[BK32.r])
                        yield
                        kb.emit('dve', lambda e: e.tensor_tensor(out=BK32[:, 1, :], in0=kmod[:], in1=eG[:], op=ALU.mult), reads=[kmod.r, eG.r], writes=[BK32.r])
                        yield
                        kb.emit('act', lambda e: e.activation(out=BKb[:], in_=BK32[:], func=AF.Copy), reads=[BK32.r], writes=[BKb.r])
                        yield
                        kb.emit('act', lambda e: e.activation(out=vb[:], in_=v32[:], func=AF.Copy), reads=[v32.r], writes=[vb.r])
                        yield
                        for (srcfn, srcres, dstt) in ((lambda c, hv: vb[hv, c * 64:(c + 1) * 64], vb.r, VT), (lambda c, hv: BKb[hv, 0, c * 64:(c + 1) * 64], BKb.r, BTm),
                                                      (lambda c, hv: BKb[hv, 1, c * 64:(c + 1) * 64], BKb.r, KTm)):
                            ptp = ps()
                            for c in range(NC8):
                                for hv in HV:
                                    kb.emit('pe', lambda e, ptp=ptp, c=c, hv=hv, srcfn=srcfn: e.matmul(ptp[hv, c * 64:(c + 1) * 64], lhsT=srcfn(c, hv), rhs=identb[hv, 0, :], start=True, stop=True),
                                            reads=[srcres, identb.r], writes=[ptp.r])
                            kb.emit('act', lambda e, ptp=ptp, dstt=dstt: e.activation(out=dstt[:].rearrange("p c v -> p (c v)"), in_=ptp[:], func=AF.Copy),
                                    reads=[ptp.r], writes=[dstt.r])
                            yield
                        pg = ps()
                        fm_proj(pg, wb_, 256, 128, t0)
                        kb.emit('act', lambda e, pg=pg: e.activation(out=sgate[:], in_=pg[:], func=AF.Silu), reads=[pg.r], writes=[sgate.r])
                        yield
                        yield
                    def P2S(hp, B):
                        AR, BKb, VT, BTm, KTm, gC, bonus, sgate = B
                        tmp = tmp2
                        for c in range(NC8):
                            pm_ = ps()
                            for hv in HV:
                                kb.emit('pe', lambda e, pm_=pm_, c=c, hv=hv: e.matmul(pm_[hv, 0:128], lhsT=BKb[hv, 0, c * 64:(c + 1) * 64], rhs=AR[hv, c, :], start=True, stop=True),
                                        reads=[BKb.r, AR.r], writes=[pm_.r])
                                kb.emit('pe', lambda e, pm_=pm_, c=c, hv=hv: e.matmul(pm_[hv, 128:256], lhsT=BKb[hv, 1, c * 64:(c + 1) * 64], rhs=AR[hv, c, :], start=True, stop=True),
                                        reads=[BKb.r, AR.r], writes=[pm_.r])
                                kb.emit('pe', lambda e, pm_=pm_, c=c, hv=hv: e.matmul(pm_[hv, 256:320], lhsT=AR[hv, c, 0:64], rhs=BKb[hv, 0, c * 64:(c + 1) * 64], start=True, stop=True),
                                        reads=[BKb.r, AR.r], writes=[pm_.r])
                            kb.emit('dve', lambda e, pm_=pm_, c=c: e.tensor_tensor(out=Mall[:, c, :], in0=pm_[:, 0:320], in1=maskR[:], op=ALU.mult),
                                    reads=[pm_.r, maskR.r], writes=[Mall.r])
                            yield
                        kb.emit('dve', lambda e: e.tensor_copy(out=AN[0][:, :, 0:64], in_=Mall[:, :, 256:320]), reads=[Mall.r], writes=[AN[0].r])
                        yield
                        kb.emit('dve', lambda e: e.tensor_copy(out=AN[0][:, :, 64:128], in_=Mall[:, :, 0:64]), reads=[Mall.r], writes=[AN[0].r])
                        yield
                        kb.emit('dve', lambda e: e.tensor_tensor(out=TTb[:], in0=Mall[:, :, 0:64], in1=identb[:, 0:1, :].to_broadcast([128, NC8, 64]), op=ALU.add),
                                reads=[Mall.r, identb.r], writes=[TTb.r])
                        yield
                        for lev in range(5):
                            src_, dst_ = AN[lev % 2], AN[(lev + 1) % 2]
                            for half in range(2):
                                pd = ps()
                                for cc in range(4):
                                    c = half * 4 + cc
                                    for hv in HV:
                                        kb.emit('pe', lambda e, pd=pd, c=c, cc=cc, src_=src_, hv=hv: e.matmul(pd[hv, cc * 128:cc * 128 + 64], lhsT=src_[hv, c, 64:128], rhs=src_[hv, c, 0:64], start=True, stop=True),
                                                reads=[src_.r], writes=[pd.r])
                                        if lev < 4:
                                            kb.emit('pe', lambda e, pd=pd, c=c, cc=cc, src_=src_, hv=hv: e.matmul(pd[hv, cc * 128 + 64:cc * 128 + 128], lhsT=src_[hv, c, 0:64], rhs=src_[hv, c, 64:128], start=True, stop=True),
                                                    reads=[src_.r], writes=[pd.r])
                                kb.emit('act', lambda e, pd=pd, half=half, dst_=dst_: e.activation(out=dst_[:, half * 4:(half + 1) * 4, :].rearrange("p c x -> p (c x)"), in_=pd[:], func=AF.Copy),
                                        reads=[pd.r], writes=[dst_.r])
                                yield
                            pt_ = ps()
                            for c in range(NC8):
                                for hv in HV:
                                    kb.emit('pe', lambda e, pt_=pt_, c=c, dst_=dst_, hv=hv: e.matmul(pt_[hv, c * 64:(c + 1) * 64], lhsT=dst_[hv, c, 0:64], rhs=TTb[hv, c, :], start=True, stop=True),
                                            reads=[dst_.r, TTb.r], writes=[pt_.r])
                            kb.emit('dve', lambda e, pt_=pt_: e.tensor_tensor(out=TTb[:].rearrange("p c x -> p (c x)"), in0=TTb[:].rearrange("p c x -> p (c x)"), in1=pt_[:], op=ALU.add),
                                    reads=[pt_.r, TTb.r], writes=[TTb.r])
                            yield
                        ST = STs[hp]
                        for c in range(NC8):
                            px = ps()
                            for hv in HV:
                                kb.emit('pe', lambda e, px=px, c=c, hv=hv: e.matmul(px[hv, 0:64], lhsT=AR[hv, c, 0:64], rhs=ST[hv, :], start=True, stop=False), reads=[AR.r, ST.r], writes=[px.r])
                                kb.emit('pe', lambda e, px=px, c=c, hv=hv: e.matmul(px[hv, 0:64], lhsT=Mall[hv, c, 128:192], rhs=VT[hv, c, :], start=False, stop=True), reads=[Mall.r, VT.r], writes=[px.r])
                            kb.emit('act', lambda e, px=px: e.activation(out=Xb[:], in_=px[:, 0:64], func=AF.Copy), reads=[px.r], writes=[Xb.r])
                            yield
                            pu = ps()
                            for hv in HV:
                                kb.emit('pe', lambda e, pu=pu, c=c, hv=hv: e.matmul(pu[hv, 0:64], lhsT=TTb[hv, c, :], rhs=Xb[hv, :], start=True, stop=True), reads=[TTb.r, Xb.r], writes=[pu.r])
                            kb.emit('dve', lambda e, pu=pu: e.tensor_copy(out=Ub[:], in_=pu[:, 0:64]), reads=[pu.r], writes=[Ub.r])
                            yield
                            py = ps()
                            pS = ps()
                            for hv in HV:
                                kb.emit('pe', lambda e, py=py, c=c, hv=hv: e.matmul(py[hv, 0:64], lhsT=ST[hv, :], rhs=AR[hv, c, 64:128], start=True, stop=False), reads=[AR.r, ST.r], writes=[py.r])
                                kb.emit('pe', lambda e, py=py, c=c, hv=hv: e.matmul(py[hv, 0:64], lhsT=Ub[hv, :], rhs=Mall[hv, c, 64:128], start=False, stop=False), reads=[Ub.r, Mall.r], writes=[py.r])
                                kb.emit('pe', lambda e, py=py, c=c, hv=hv: e.matmul(py[hv, 0:64], lhsT=VT[hv, c, :], rhs=Mall[hv, c, 192:256], start=False, stop=True), reads=[VT.r, Mall.r], writes=[py.r])
                            for hv in HV:
                                kb.emit('pe', lambda e, pS=pS, c=c, hv=hv: e.matmul(pS[hv, 0:64], lhsT=BTm[hv, c, :], rhs=Ub[hv, :], start=True, stop=False), reads=[BTm.r, Ub.r], writes=[pS.r])
                                kb.emit('pe', lambda e, pS=pS, c=c, hv=hv: e.matmul(pS[hv, 0:64], lhsT=KTm[hv, c, :], rhs=VT[hv, c, :], start=False, stop=False), reads=[KTm.r, VT.r], writes=[pS.r])
                                kb.emit('pe', lambda e, pS=pS, hv=hv: e.matmul(pS[hv, 0:64], lhsT=identb[hv, 0, :], rhs=ST[hv, :], start=False, stop=True), reads=[identb.r, ST.r], writes=[pS.r])
                            kb.emit('act', lambda e, py=py, c=c: e.activation(out=yT[:, c * 64:(c + 1) * 64], in_=py[:, 0:64], func=AF.Copy), reads=[py.r], writes=[yT.r])
                            yield
                            kb.emit('dve', lambda e, pS=pS, c=c: e.tensor_scalar(out=ST[:], in0=pS[:, 0:64], scalar1=gC[:, c:c + 1], scalar2=None, op0=ALU.mult),
                                    reads=[pS.r, gC.r], writes=[ST.r])
                            yield
                        pm2 = ps()
                        kb.emit('pe', lambda e, pm2=pm2: e.matmul(pm2[:], lhsT=ones_bd[:], rhs=yT[:], start=True, stop=True), reads=[ones_bd.r, yT.r], writes=[pm2.r])
                        kb.emit('dve', lambda e, pm2=pm2: e.scalar_tensor_tensor(out=yT[:], in0=pm2[:], scalar=-1.0 / 64.0, in1=yT[:], op0=ALU.mult, op1=ALU.add),
                                reads=[pm2.r, yT.r], writes=[yT.r])
                        yield
                        kb.emit('act', lambda e: e.activation(out=tmp[:], in_=yT[:], func=AF.Square), reads=[yT.r], writes=[tmp.r])
                        yield
                        pv2 = ps()
                        kb.emit('pe', lambda e, pv2=pv2: e.matmul(pv2[:], lhsT=ones_bd[:], rhs=tmp[:], start=True, stop=True), reads=[ones_bd.r, tmp.r], writes=[pv2.r])
                        kb.emit('act', lambda e, pv2=pv2: e.activation(out=tmp[:], in_=pv2[:], func=AF.Ln, scale=1.0 / 64.0, bias=eps_t[:, 1:2]), reads=[pv2.r, eps_t.r], writes=[tmp.r])
                        yield
                        kb.emit('act', lambda e: e.activation(out=tmp[:], in_=tmp[:], func=AF.Exp, scale=-0.5), reads=[tmp.r], writes=[tmp.r])
                        yield
                        kb.emit('dve', lambda e: e.tensor_tensor(out=yT[:], in0=yT[:], in1=tmp[:], op=ALU.mult), reads=[yT.r, tmp.r], writes=[yT.r])
                        yield
                        kb.emit('dve', lambda e, hp=hp: e.tensor_scalar(out=yT[:], in0=yT[:], scalar1=lngt[:, hp:hp + 1], scalar2=lnbt[:, hp:hp + 1], op0=ALU.mult, op1=ALU.add),
                                reads=[yT.r, lngt.r, lnbt.r], writes=[yT.r])
                        yield
                        kb.emit('dve', lambda e: e.tensor_tensor(out=yT[:], in0=yT[:], in1=bonus[:], op=ALU.add), reads=[yT.r, bonus.r], writes=[yT.r])
                        yield
                        kb.emit('dve', lambda e, hp=hp: e.tensor_tensor(out=yg[0][:, hp, t0:t0 + TT], in0=yT[:], in1=sgate[:], op=ALU.mult), reads=[yT.r, sgate.r], writes=[yg[0].r])
                        yield
                        yield
                    prev = None
                    for hp in range(4):
                        gens = [P1(hp, RWB[hp % 2])] + ([prev] if prev is not None else [])
                        _interleave(gens)
                        prev = P2S(hp, RWB[hp % 2])
                    _interleave([prev])

            for n_, nm in ((1, 'gla'), (2, 'ret')):
                if nm not in MIX:
                    continue
                isg = (nm == 'gla')
                S32, Sb = (Sg32, Sgb) if isg else (Sr32, Srb)
                gn = gng if isg else rgg
                wqk = wtile()
                load_w(wqk, Wb[l][:, AUGOFF[nm + '_q']:AUGOFF[nm + '_q'] + 256], 256, col0=0)
                load_w(wqk, Wb[l][:, AUGOFF[nm + '_k']:AUGOFF[nm + '_k'] + 256], 256, col0=256, batch=True)
                wv = wtile()
                load_w(wv, Wb[l][:, AUGOFF[nm + '_v']:AUGOFF[nm + '_v'] + 512], 512)
                if isg:
                    wg = wtile()
                    load_w(wg, Wb[l][:, AUGOFF[nm + '_g']:AUGOFF[nm + '_g'] + 512], 512)
                if isg:
                    kb.dma('sp', wflo[:], Wb[l][:, AUGOFF['gla_flo']:AUGOFF['gla_flo'] + 16].rearrange("(k p) c -> p k c", p=128),
                           wflo.r, reads=[wdram_r], writes=[wflo.r])
                else:
                    wsw = wtile()
                    load_w(wsw, Wb[l][:, AUGOFF['ret_qs']:AUGOFF['ret_qs'] + 256], 256, col0=0)
                    load_w(wsw, Wb[l][:, AUGOFF['ret_ks']:AUGOFF['ret_ks'] + 256], 256, col0=256, batch=True)
                for tt in range(NT):
                    t0 = tt * TT
                    NCH = TT // 128
                    for c in range(NCH):
                        p = ps()
                        for k in range(KC):
                            kb.emit('pe', lambda e, k=k, p=p, c=c: e.matmul(p[:], lhsT=un[:, k, 3 + t0 + c * 128:3 + t0 + (c + 1) * 128], rhs=wv[:, k, :],
                                                                            start=(k == 0), stop=(k == KC - 1)), reads=[un.r, wv.r], writes=[p.r])
                        kb.emit('act', lambda e, p=p, c=c: e.activation(out=vT[c][:], in_=p[:], func=AF.Copy), reads=[p.r], writes=[vT[c].r])
                    if isg:
                        pf = ps()
                        fm_proj(pf, wflo, 0, 16, t0)
                        kb.emit('act', lambda e, pf=pf: e.activation(out=flo_b[:], in_=pf[0:16, :], func=AF.Copy), reads=[pf.r], writes=[flo_b.r])
                    else:
                        kb.dma('sp', posi[:], positions[s0 + t0:s0 + t0 + TT].partition_broadcast(64), posi.r, writes=[posi.r])
                        A_, B_, C_ = WK[0], WK[1], WK[2]
                        kb.emit('dve', lambda e: e.tensor_copy(out=A_[0:64, :], in_=posi[:]), reads=[posi.r], writes=[A_.r])
                        kb.emit('dve', lambda e: e.tensor_scalar(out=A_[0:64, :], in0=A_[0:64, :], scalar1=invf[:, 0:1], scalar2=1.0 / (2 * np.pi),
                                                                 op0=ALU.mult, op1=ALU.mult), reads=[A_.r, invf.r], writes=[A_.r])
                        kb.emit('dve', lambda e: e.tensor_copy(out=posi[:], in_=A_[0:64, :]), reads=[A_.r], writes=[posi.r])
                        kb.emit('dve', lambda e: e.tensor_copy(out=B_[0:64, :], in_=posi[:]), reads=[posi.r], writes=[B_.r])
                        kb.emit('dve', lambda e: e.tensor_tensor(out=A_[0:64, :], in0=A_[0:64, :], in1=B_[0:64, :], op=ALU.subtract),
                                reads=[A_.r, B_.r], writes=[A_.r])
                        kb.emit('act', lambda e: e.activation(out=B_[0:64, :], in_=A_[0:64, :], func=AF.Sin, scale=float(np.pi)), reads=[A_.r], writes=[B_.r])
                        kb.emit('act', lambda e: e.activation(out=C_[0:64, :], in_=A_[0:64, :], func=AF.Sin, scale=float(np.pi / 2)), reads=[A_.r], writes=[C_.r])
                        kb.emit('dve', lambda e: e.tensor_tensor(out=cosT[0:64, :], in0=B_[0:64, :], in1=B_[0:64, :], op=ALU.mult), reads=[B_.r], writes=[cosT.r])
                        kb.emit('dve', lambda e: e.tensor_scalar(out=cosT[0:64, :], in0=cosT[0:64, :], scalar1=-2.0, scalar2=1.0, op0=ALU.mult, op1=ALU.add),
                                reads=[cosT.r], writes=[cosT.r])
                        kb.emit('dve', lambda e: e.tensor_tensor(out=C_[0:64, :], in0=C_[0:64, :], in1=C_[0:64, :], op=ALU.mult), reads=[C_.r], writes=[C_.r])
                        kb.emit('dve', lambda e: e.tensor_scalar(out=C_[0:64, :], in0=C_[0:64, :], scalar1=-4.0, scalar2=2.0, op0=ALU.mult, op1=ALU.add),
                                reads=[C_.r], writes=[C_.r])
                        kb.emit('dve', lambda e: e.tensor_tensor(out=sinT[0:64, :], in0=C_[0:64, :], in1=B_[0:64, :], op=ALU.mult), reads=[C_.r, B_.r], writes=[sinT.r])
                    def _dec(h):
                        pq = ps()
                        fm_proj(pq, wqk, h * 64, 64, t0)
                        pk = ps()
                        fm_proj(pk, wqk, 256 + h * 64, 64, t0)
                        if isg:
                            plf = ps()
                            kb.emit('pe', lambda e, plf=plf, h=h: e.matmul(plf[0:64, :], lhsT=fup_b[0:16, h * 64:(h + 1) * 64], rhs=flo_b[0:16, :], start=True, stop=True),
                                    reads=[fup_b.r, flo_b.r], writes=[plf.r])
                            cs, eb, enb = (WK[0], WK[1], WK[2]) if h % 2 == 0 else (WK[5], WK[6], WK[7])
                            kb.emit('act', lambda e, plf=plf, h=h: e.activation(out=cs[0:64, :], in_=plf[0:64, :], func=AF.Exp, scale=-1.0, bias=nfb[:, h:h + 1]),
                                    reads=[plf.r, nfb.r], writes=[cs.r])
                            yield
                            kb.emit('act', lambda e: e.activation(out=cs[0:64, :], in_=cs[0:64, :], func=AF.Ln, bias=eps_t[0:64, 2:3]), reads=[cs.r, eps_t.r], writes=[cs.r])
                            yield
                            for c in range(NCH):
                                kb.emit('dve', lambda e, c=c: e.tensor_tensor_scan(out=eb[0:64, c * 128:(c + 1) * 128], data0=ones_f[0:64, :], data1=cs[0:64, c * 128:(c + 1) * 128],
                                                                                   initial=0.0, op0=ALU.mult, op1=ALU.add), reads=[cs.r, ones_f.r], writes=[eb.r])
                                yield
                            kb.emit('act', lambda e: e.activation(out=enb[0:64, :], in_=eb[0:64, :], func=AF.Exp, scale=1.0 / 16.0), reads=[eb.r], writes=[enb.r])
                            yield
                            kb.emit('act', lambda e: e.activation(out=eb[0:64, :], in_=eb[0:64, :], func=AF.Exp, scale=-1.0 / 16.0), reads=[eb.r], writes=[eb.r])
                            yield
                            kb.emit('dve', lambda e, pq=pq, h=h: e.scalar_tensor_tensor(out=qd[:, h, :], in0=pq[0:64, :], scalar=0.125, in1=eb[0:64, :], op0=ALU.mult, op1=ALU.mult),
                                    reads=[pq.r, eb.r], writes=[qd.r])
                            yield
                            kb.emit('dve', lambda e, pk=pk, h=h: e.tensor_tensor(out=kd32[:, h, :], in0=pk[0:64, :], in1=enb[0:64, :], op=ALU.mult),
                                    reads=[pk.r, enb.r], writes=[kd32.r])
                            yield
                            for c in range(NCH):
                                kb.emit('act', lambda e, c=c, h=h: e.activation(out=ebl[:, h, c:c + 1], in_=eb[0:64, c * 128 + 127:c * 128 + 128], func=AF.Copy),
                                        reads=[eb.r], writes=[ebl.r])
                                yield
                        else:
                            pqs = ps()
                            fm_proj(pqs, wsw, h * 64, 64, t0)
                            pks = ps()
                            fm_proj(pks, wsw, 256 + h * 64, 64, t0)
                            q1, q2 = (WK[3], WK[4]) if h % 2 == 0 else (WK[8], WK[9])
                            for (pa, pb, dst, tab) in ((pq, pqs, qd, ebR), (pk, pks, kd32, enbR)):
                                kb.emit('dve', lambda e, pa=pa: e.tensor_tensor(out=q1[0:64, :], in0=pa[0:64, :], in1=cosT[0:64, :], op=ALU.mult), reads=[pa.r, cosT.r], writes=[q1.r])
                                yield
                                kb.emit('dve', lambda e, pb=pb: e.tensor_tensor(out=q2[0:64, :], in0=pb[0:64, :], in1=sinT[0:64, :], op=ALU.mult), reads=[pb.r, sinT.r], writes=[q2.r])
                                yield
                                kb.emit('dve', lambda e: e.tensor_tensor(out=q1[0:64, :], in0=q1[0:64, :], in1=q2[0:64, :], op=ALU.add), reads=[q1.r, q2.r], writes=[q1.r])
                                yield
                                kb.emit('dve', lambda e, dst=dst, tab=tab, h=h: e.tensor_tensor(
                                    out=dst[:, h, :].rearrange("p (c t) -> p c t", t=128), in0=q1[0:64, :].rearrange("p (c t) -> p c t", t=128),
                                    in1=tab[:, h:h + 1, :].to_broadcast([64, NCH, 128]), op=ALU.mult), reads=[q1.r, tab.r], writes=[dst.r])
                                yield
                    for h0 in (0, 2):
                        _interleave([_dec(h0), _dec(h0 + 1)])
                    kb.emit('act', lambda e: e.activation(out=kd[:], in_=kd32[:], func=AF.Copy), reads=[kd32.r], writes=[kd.r])
                    for c in range(NCH):
                        cs_ = slice(c * 128, (c + 1) * 128)
                        pa = ps()
                        for h in range(4):
                            kb.emit('pe', lambda e, pa=pa, h=h, cs_=cs_: e.matmul(pa[:, h * 128:(h + 1) * 128], lhsT=kd[:, h, cs_], rhs=qd[:, h, cs_], start=True, stop=True),
                                    reads=[kd.r, qd.r], writes=[pa.r])
                        kb.emit('dve', lambda e, pa=pa: e.tensor_tensor(out=attb[:], in0=pa[:], in1=mask4[:], op=ALU.mult), reads=[pa.r, mask4.r], writes=[attb.r])
                        po = ps()
                        for h in range(4):
                            kb.emit('pe', lambda e, po=po, h=h, c=c: e.matmul(po[:, h * 128:(h + 1) * 128], lhsT=vT[c][:, h * 128:(h + 1) * 128], rhs=attb[:, h * 128:(h + 1) * 128],
                                                                              start=True, stop=False), reads=[vT[c].r, attb.r], writes=[po.r])
                            kb.emit('pe', lambda e, po=po, h=h, cs_=cs_: e.matmul(po[:, h * 128:(h + 1) * 128], lhsT=Sb[:, h, :], rhs=qd[:, h, cs_], start=False, stop=True),
                                    reads=[Sb.r, qd.r], writes=[po.r])
                        kb.emit('act', lambda e, po=po, cs_=cs_: e.activation(out=oall[:, :, cs_], in_=po[:].rearrange("p (h t) -> p h t", h=4), func=AF.Copy),
                                reads=[po.r], writes=[oall.r])
                        pt = ps()
                        for h in range(4):
                            kb.emit('pe', lambda e, pt=pt, h=h, cs_=cs_: e.transpose(out=pt[:, h * 64:(h + 1) * 64], in_=kd32[:, h, cs_], identity=ident[0:64, 0:64]),
                                    reads=[kd32.r, ident.r], writes=[pt.r])
                        kb.emit('dve', lambda e, pt=pt: e.tensor_copy(out=kdT[:], in_=pt[:, 0:256]), reads=[pt.r], writes=[kdT.r])
                        pss = ps()
                        for h in range(4):
                            kb.emit('pe', lambda e, pss=pss, h=h, c=c: e.matmul(pss[0:64, h * 128:(h + 1) * 128], lhsT=kdT[:, h * 64:(h + 1) * 64], rhs=vT[c][:, h * 128:(h + 1) * 128],
                                                                                start=True, stop=True), reads=[kdT.r, vT[c].r], writes=[pss.r])
                        kb.emit('dve', lambda e, pss=pss: e.tensor_tensor(out=S32[:].rearrange("p h v -> p (h v)"), in0=S32[:].rearrange("p h v -> p (h v)"), in1=pss[0:64, :], op=ALU.add),
                                reads=[pss.r, S32.r], writes=[S32.r])
                        for h in range(4):
                            if isg:
                                kb.emit('dve', lambda e, h=h, c=c: e.tensor_scalar(out=S32[:, h, :], in0=S32[:, h, :], scalar1=ebl[:, h, c:c + 1], scalar2=None, op0=ALU.mult),
                                        reads=[S32.r, ebl.r], writes=[S32.r])
                            else:
                                kb.emit('dve', lambda e, h=h: e.tensor_scalar(out=S32[:, h, :], in0=S32[:, h, :], scalar1=float(RET_G[h] ** 128), scalar2=None, op0=ALU.mult),
                                        reads=[S32.r], writes=[S32.r])
                        kb.emit('act', lambda e: e.activation(out=Sb[:], in_=S32[:], func=AF.Copy), reads=[S32.r], writes=[Sb.r])
                    if not isg:
                        wg = wtile()
                        load_w(wg, Wb[l][:, AUGOFF[nm + '_g']:AUGOFF[nm + '_g'] + 512], 512)
                    def _nrm(h):
                        o_h, sqq, rs_, gg = (WK[5], WK[6], WK[7], WK[8]) if h % 2 == 0 else (WK[0], WK[1], WK[2], WK[3])
                        if isg:
                            kb.emit('act', lambda e, h=h: e.activation(out=sqq[:], in_=oall[:, h, :], func=AF.Square), reads=[oall.r], writes=[sqq.r])
                            yield
                            pv_ = ps()
                            kb.emit('pe', lambda e, pv_=pv_: e.matmul(pv_[:], lhsT=ones_f[:], rhs=sqq[:], start=True, stop=True), reads=[ones_f.r, sqq.r], writes=[pv_.r])
                            src_o = None
                        else:
                            pm_ = ps()
                            kb.emit('pe', lambda e, pm_=pm_, h=h: e.matmul(pm_[:], lhsT=ones_f[:], rhs=oall[:, h, :], start=True, stop=True), reads=[ones_f.r, oall.r], writes=[pm_.r])
                            kb.emit('dve', lambda e, pm_=pm_, h=h: e.scalar_tensor_tensor(out=o_h[:], in0=pm_[:], scalar=-1.0 / 128.0, in1=oall[:, h, :], op0=ALU.mult, op1=ALU.add),
                                    reads=[pm_.r, oall.r], writes=[o_h.r])
                            yield
                            kb.emit('act', lambda e: e.activation(out=sqq[:], in_=o_h[:], func=AF.Square), reads=[o_h.r], writes=[sqq.r])
                            yield
                            pv_ = ps()
                            kb.emit('pe', lambda e, pv_=pv_: e.matmul(pv_[:], lhsT=ones_f[:], rhs=sqq[:], start=True, stop=True), reads=[ones_f.r, sqq.r], writes=[pv_.r])
                        kb.emit('act', lambda e, pv_=pv_: e.activation(out=rs_[:], in_=pv_[:], func=AF.Ln, scale=1.0 / 128.0, bias=eps_t[:, 0:1]), reads=[pv_.r, eps_t.r], writes=[rs_.r])
                        yield
                        kb.emit('act', lambda e: e.activation(out=rs_[:], in_=rs_[:], func=AF.Exp, scale=-0.5), reads=[rs_.r], writes=[rs_.r])
                        yield
                        if isg:
                            kb.emit('dve', lambda e, h=h: e.scalar_tensor_tensor(out=rs_[:], in0=oall[:, h, :], scalar=gn[:, h:h + 1], in1=rs_[:], op0=ALU.mult, op1=ALU.mult),
                                    reads=[oall.r, gn.r, rs_.r], writes=[rs_.r])
                            yield
                        else:
                            kb.emit('dve', lambda e, h=h: e.scalar_tensor_tensor(out=rs_[:], in0=o_h[:], scalar=gn[:, h:h + 1], in1=rs_[:], op0=ALU.mult, op1=ALU.mult),
                                    reads=[o_h.r, gn.r, rs_.r], writes=[rs_.r])
                            yield
                        pg = ps()
                        fm_proj(pg, wg, h * 128, 128, t0)
                        kb.emit('act', lambda e, pg=pg: e.activation(out=gg[:], in_=pg[:], func=AF.Silu), reads=[pg.r], writes=[gg.r])
                        yield
                        kb.emit('dve', lambda e, h=h, n_=n_: e.tensor_tensor(out=yg[n_][:, h, t0:t0 + TT], in0=rs_[:], in1=gg[:], op=ALU.mult),
                                reads=[rs_.r, gg.r], writes=[yg[n_].r])
                        yield

                    for h0 in (0, 2):
                        _interleave([_nrm(h0), _nrm(h0 + 1)])
            for j in range(KC):
                wgt = wtile()
                for n in range(4):
                    c = AUGOFF['merge%d' % (2 * n + j // 4)] + (j % 4) * 128
                    load_w(wgt, Wb[l][:, c:c + 128], 128, col0=n * 128, batch=(n > 0))
                wbr = wtile()
                for n in range(4):
                    load_w(wbr, WBR[l][n * 512:(n + 1) * 512, j * 128:(j + 1) * 128], 128, col0=n * 128, kc=4, batch=(n > 0))
                for tt in range(NT):
                    t0 = tt * TT
                    acc = WK[6]
                    for n in range(4):
                        gt = WK[7] if n % 2 == 0 else WK[8]
                        pg = ps()
                        fm_proj(pg, wgt, n * 128, 128, t0)
                        pb = ps()
                        for k in range(4):
                            kb.emit('pe', lambda e, k=k, n=n, pb=pb, t0=t0: e.matmul(pb[:], lhsT=wbr[:, k, n * 128:(n + 1) * 128],
                                                                                      rhs=yg[n][:, k, t0:t0 + TT], start=(k == 0), stop=(k == 3)),
                                    reads=[wbr.r, yg[n].r], writes=[pb.r])
                        kb.emit('act', lambda e, pg=pg: e.activation(out=gt[:], in_=pg[:], func=AF.Sigmoid), reads=[pg.r], writes=[gt.r])
                        if n == 0:
                            kb.emit('dve', lambda e, pb=pb: e.tensor_tensor(out=acc[:], in0=gt[:], in1=pb[:], op=ALU.mult),
                                    reads=[gt.r, pb.r], writes=[acc.r])
                        else:
                            kb.emit('dve', lambda e, pb=pb: e.tensor_tensor(out=gt[:], in0=gt[:], in1=pb[:], op=ALU.mult),
                                    reads=[gt.r, pb.r], writes=[gt.r])
                            if n < 3:
                                kb.emit('dve', lambda e: e.tensor_tensor(out=acc[:], in0=acc[:], in1=gt[:], op=ALU.add),
                                        reads=[acc.r, gt.r], writes=[acc.r])
                            else:
                                kb.emit('dve', lambda e, j=j, t0=t0: e.tensor_tensor(out=merged[:, j, t0:t0 + TT], in0=acc[:], in1=gt[:], op=ALU.add),
                                        reads=[acc.r, gt.r], writes=[merged.r])
            for j in range(KC):
                w = wtile()
                load_w(w, WO[l][:, j * 128:(j + 1) * 128], 128)
                for tt in range(NT):
                    t0 = tt * TT
                    p = ps()
                    for k in range(KC):
                        kb.emit('pe', lambda e, k=k, p=p, w=w, t0=t0: e.matmul(p[:], lhsT=w[:, k, 0:128], rhs=merged[:, k, t0:t0 + TT],
                                                                               start=(k == 0), stop=(k == KC - 1)),
                                reads=[w.r, merged.r], writes=[p.r])
                    kb.emit('dve', lambda e, p=p, j=j, t0=t0: e.tensor_tensor(out=hseg[:, j, t0:t0 + TT], in0=hseg[:, j, t0:t0 + TT], in1=p[:], op=ALU.add),
                            reads=[p.r, hseg.r], writes=[hseg.r])

            if XA:
                for tt in range(NT):
                    hv = _View(hseg, slice(tt * TT, (tt + 1) * TT))
                    rmsnorm_fm(hv, xng, lambda k, tt=tt: un[:, k, 3 + tt * TT:3 + (tt + 1) * TT], un.r)
                for j in range(KC):
                    w = wtile()
                    load_w(w, WQ[l][:, j * 128:(j + 1) * 128], 128)
                    for tt in range(NT):
                        t0 = tt * TT
                        p = ps()
                        fm_proj(p, w, 0, 128, t0)
                        kb.emit('act', lambda e, p=p, j=j, t0=t0: e.activation(out=merged[:, j, t0:t0 + TT], in_=p[:], func=AF.Copy),
                                reads=[p.r], writes=[merged.r])
                for tt in range(NT):
                    t0 = tt * TT
                    for hh_ in range(4):
                        pT = [WB16[0], WB16[1]] if hh_ % 2 == 0 else [WB16[2], WB16[3]]
                        psum_ = ps()
                        for mb in range(2):
                            p = ps()
                            for k2 in range(2):
                                kb.emit('pe', lambda e, p=p, k2=k2, mb=mb, hh_=hh_, t0=t0: e.matmul(
                                    p[:], lhsT=kT[:, hh_ * 2 + k2, mb * 128:(mb + 1) * 128], rhs=merged[:, hh_ * 2 + k2, t0:t0 + TT],
                                    start=(k2 == 0), stop=(k2 == 1)), reads=[kT.r, merged.r], writes=[p.r])
                            kb.emit('act', lambda e, p=p, mb=mb: e.activation(out=pT[mb][:], in_=p[:], func=AF.Exp, scale=1.0 / 16.0),
                                    reads=[p.r], writes=[pT[mb].r])
                        for mb in range(2):
                            kb.emit('pe', lambda e, mb=mb, psum_=psum_: e.matmul(psum_[:], lhsT=ones_b[:], rhs=pT[mb][:], start=(mb == 0), stop=(mb == 1)),
                                    reads=[ones_b.r, pT[mb].r], writes=[psum_.r])
                        rs = WK[8] if hh_ % 2 == 0 else WK[9]
                        kb.emit('act', lambda e, psum_=psum_, rs=rs: e.activation(out=rs[:], in_=psum_[:], func=AF.Ln), reads=[psum_.r], writes=[rs.r])
                        kb.emit('act', lambda e, rs=rs: e.activation(out=rs[:], in_=rs[:], func=AF.Exp, scale=-1.0), reads=[rs.r], writes=[rs.r])
                        for dh in range(2):
                            po = ps()
                            for mb in range(2):
                                kb.emit('pe', lambda e, po=po, mb=mb, dh=dh, hh_=hh_: e.matmul(
                                    po[:], lhsT=vtm[:, mb, hh_ * 256 + dh * 128:hh_ * 256 + (dh + 1) * 128], rhs=pT[mb][:],
                                    start=(mb == 0), stop=(mb == 1)), reads=[vtm.r, pT[mb].r], writes=[po.r])
                            dst = yg[hh_ // 2]
                            kb.emit('dve', lambda e, po=po, dst=dst, hh_=hh_, dh=dh, t0=t0: e.tensor_tensor(
                                out=dst[:, (hh_ % 2) * 2 + dh, t0:t0 + TT], in0=po[:], in1=rs[:], op=ALU.mult),
                                    reads=[po.r, rs.r], writes=[dst.r])
                for j in range(KC):
                    w = wtile()
                    load_w(w, WXO[l][:, j * 128:(j + 1) * 128], 128)
                    for tt in range(NT):
                        t0 = tt * TT
                        p = ps()
                        for k in range(KC):
                            src = yg[k // 4]
                            kb.emit('pe', lambda e, k=k, p=p, w=w, t0=t0, src=src: e.matmul(p[:], lhsT=w[:, k, 0:128], rhs=src[:, k % 4, t0:t0 + TT],
                                                                                            start=(k == 0), stop=(k == KC - 1)),
                                    reads=[w.r, src.r], writes=[p.r])
                        kb.emit('dve', lambda e, p=p, j=j, t0=t0: e.tensor_tensor(out=hseg[:, j, t0:t0 + TT], in0=hseg[:, j, t0:t0 + TT], in1=p[:], op=ALU.add),
                                reads=[p.r, hseg.r], writes=[hseg.r])
            kb.dma('pool', hT_v[:, :, s0:s0 + SEG], hseg[:], hseg.r, reads=[hseg.r], writes=[hT_r])

    kb.new_scope()
    _tl.clear()
    hs_t = [T(kb, "hs%d" % i, [128, KC, TT], F32) for i in range(2)]
    yn = T(kb, "yn", [128, KC, TT], F32)
    otm = [T(kb, "otm%d" % i, [128, D], F32) for i in range(2)]
    oi = 0
    for tt in range(S // TT):
        hs = hs_t[tt % 2]
        kb.dma('sp', hs[:], hT_v[:, :, tt * TT:(tt + 1) * TT], hs.r, reads=[hT_r], writes=[hs.r])
        rmsnorm_fm(hs, fng, lambda k: yn[:, k, :], yn.r)
        for tb in range(TT // 128):
            ot = otm[oi % 2]
            oi += 1
            for half in range(2):
                p = ps()
                for kk in range(4):
                    k = half * 4 + kk
                    kb.emit('pe', lambda e, p=p, kk=kk, k=k, tb=tb: e.transpose(
                        out=p[:, kk * 128:(kk + 1) * 128], in_=yn[:, k, tb * 128:(tb + 1) * 128], identity=ident[:]),
                            reads=[yn.r, ident.r], writes=[p.r])
                if half:
                    kb.emit('act', lambda e, p=p, ot=ot: e.activation(out=ot[:, 512:1024], in_=p[:], func=AF.Copy),
                            reads=[p.r], writes=[ot.r])
                else:
                    kb.emit('dve', lambda e, p=p, ot=ot: e.tensor_copy(out=ot[:, 0:512], in_=p[:]),
                            reads=[p.r], writes=[ot.r])
            r0 = tt * TT + tb * 128
            kb.dma('pool', out[r0:r0 + 128, :], ot[:], ot.r, reads=[ot.r])
    kb.wait_all('pool', [t.r for t in otm])
    kb.new_scope()
    kb.scope.close()
    build.ninst = kb.ninst
    return nc, es


PARAM_NAMES = ('mix_norm_g', 'w_in', 'rw_mu', 'rw_w0', 'rw_w2', 'rw_a0', 'rw_a2', 'rw_k_k', 'rw_k_a', 'rw_r_k', 'rw_ln_g',
               'rw_ln_b', 'gla_f_up', 'gla_f_b', 'gla_norm_g', 'ret_gn_g', 'lru_conv_w', 'lru_conv_b', 'lru_wa', 'lru_ba',
               'lru_wx', 'lru_bx', 'lru_lambda', 'w_branch', 'w_out', 'xa_norm_g', 'xa_mem_norm_g', 'xa_wq', 'xa_wkv',
               'xa_wo', 'final_norm_g')


def core_inputs(inputs, b, S):
    m = {"x": np.ascontiguousarray(inputs['x'][b, :S]), "mem": np.ascontiguousarray(inputs['mem'][b]),
         "positions": np.ascontiguousarray(inputs['positions'][b, :S]).astype(np.int32)}
    for n in PARAM_NAMES:
        m[n] = np.ascontiguousarray(inputs[n])
    return m


def kernel(**inputs):
    S = inputs['x'].shape[1]
    nc, es = build(S, 2)
    in_maps = [core_inputs(inputs, c % 2, S) for c in range(8)]
    res = run_bass_kernel_spmd(nc, in_maps, core_ids=list(range(8)))
    es.close()
    return np.stack([res.results[0]["out"], res.results[1]["out"]], axis=0)
```

```python
import numpy as np
from contextlib import ExitStack
import concourse.bass as bass
import concourse.mybir as mybir
from concourse.bass_utils import run_bass_kernel_spmd

F32 = mybir.dt.float32
BF16 = mybir.dt.bfloat16
I32 = mybir.dt.int32
ALU = mybir.AluOpType
AF = mybir.ActivationFunctionType
AX = mybir.AxisListType

D = 1024
KC = 8
NMEM = 256
DBR = 512
NORM_EPS = 1e-6
RW_LN_EPS = 64e-5
RET_G = [1.0 - 2.0 ** (-5.0 - h) for h in range(4)]

_src = {}
_o = 0
for _n, _w in (('rw_r', 512), ('rw_k', 512), ('rw_v', 512), ('rw_wlo', 64), ('rw_alo', 64), ('rw_g', 512),
               ('gla_q', 256), ('gla_k', 256), ('gla_v', 512), ('gla_flo', 16), ('gla_g', 512),
               ('ret_q', 256), ('ret_k', 256), ('ret_v', 512), ('ret_g', 512),
               ('lru_x', 512), ('lru_g', 512), ('merge', 4096)):
    _src[_n] = (_o, _w)
    _o += _w
N_IN = _o
MU_OFF = {'rw_r': 0, 'rw_k': 512, 'rw_v': 1024, 'rw_wlo': 1536, 'rw_alo': 1600}
AUG = []
for _n in ('rw_r', 'rw_k', 'rw_v', 'rw_wlo', 'rw_alo'):
    AUG.append((_n + '_c', _src[_n][1], _src[_n][0], 'mu1m', MU_OFF[_n]))
    AUG.append((_n + '_p', _src[_n][1], _src[_n][0], 'mu', MU_OFF[_n]))
for _j in range(4):
    AUG.append(('lru_x%d' % _j, 512, _src['lru_x'][0], 'conv', _j))
for _n in ('rw_g', 'gla_q', 'gla_k', 'gla_v', 'gla_flo', 'gla_g', 'ret_q', 'ret_k', 'ret_v', 'ret_g', 'lru_g'):
    AUG.append((_n, _src[_n][1], _src[_n][0], 'plain', 0))
AUG.append(('ret_qs', 256, _src['ret_q'][0], 'swap', 0))
AUG.append(('ret_ks', 256, _src['ret_k'][0], 'swap', 0))
for _j in range(8):
    AUG.append(('merge%d' % _j, 512, _src['merge'][0] + 512 * _j, 'plain', 0))
AUGOFF = {}
_o = 0
for _a in AUG:
    AUGOFF[_a[0]] = _o
    _o += _a[1]
NAUG = _o


class Res:
    __slots__ = ('w', 'r', 'ds')

    def __init__(self):
        self.w = None
        self.r = {}
        self.ds = None


class KB:
    def __init__(self, nc, es):
        self.nc = nc
        self.es = es
        self.eng = dict(pe=nc.tensor, dve=nc.vector, act=nc.scalar, pool=nc.gpsimd, sp=nc.sync)
        self.semh = {}
        self.cnt = {}
        for e in self.eng:
            self.semh[e] = es.enter_context(nc.semaphore('c_' + e))
            self.cnt[e] = 0
        self.seen = {e: {} for e in self.eng}
        self.nd = 0
        self.strict = True
        self.ninst = 0
        self.scope = es

    def new_scope(self):
        keys = list(self.cnt.keys())
        for e in self.eng:
            need = {k: (self.cnt[k], self.cnt[k]) for k in keys if k != e and self.cnt[k] > 0}
            self._waits(e, need)
        for e in self.eng:
            self.emit(e, lambda en: en.engine_nop() if hasattr(en, 'engine_nop') else en.nop())
        for e in self.eng:
            need = {k: (self.cnt[k], self.cnt[k]) for k in self.eng if k != e}
            self._waits(e, need)
        if self.scope is not self.es:
            self.scope.close()
        self.scope = ExitStack()

    def _need(self, reads, writes):
        need = {}

        def add(k, v, raw):
            a, b = need.get(k, (0, 0))
            need[k] = (max(a, v), max(b, v) if raw else b)
        for r in reads:
            if r.w is not None:
                add(r.w[0], r.w[1], True)
        for w in writes:
            if w.w is not None:
                add(w.w[0], w.w[1], False)
            for k, v in w.r.items():
                add(k, v, False)
        return need

    def _waits(self, e, need):
        eng = self.eng[e]
        seen = self.seen[e]
        for k, vv in need.items():
            v, vraw = vv if isinstance(vv, tuple) else (vv, vv)
            if k == e:
                if e == 'pe' or not self.strict:
                    continue
                pass
            if k[0] == 'd' and k[1:].isdigit():
                v = self.cnt[k]
            if seen.get(k, 0) >= v:
                continue
            eng.wait_ge(self.semh[k], v)
            seen[k] = v
            self.ninst += 1

    def emit(self, e, fn, reads=(), writes=()):
        self._waits(e, self._need(reads, writes))
        ins = fn(self.eng[e])
        self.cnt[e] += 1
        ins.then_inc(self.semh[e], 1)
        t = (e, self.cnt[e])
        for r in reads:
            if r.r.get(e, 0) < t[1]:
                r.r[e] = t[1]
        for w in writes:
            w.w = t
            w.r = {}
        self.ninst += 1
        return ins

    def _dsem(self, res, q):
        if res.ds is None:
            res.ds = {}
        if q not in res.ds:
            k = 'd%d' % self.nd
            self.nd += 1
            self.semh[k] = self.es.enter_context(self.nc.semaphore(k))
            self.cnt[k] = 0
            res.ds[q] = k
        return res.ds[q]

    def dma(self, q, out, in_, sb, reads=(), writes=(), batch=False):
        k = self._dsem(sb, q)
        need = self._need(reads, writes)
        if not batch and self.cnt[k] > 0:
            need[k] = (self.cnt[k], self.cnt[k])
        self._waits(q, need)
        ins = self.eng[q].dma_start(out=out, in_=in_)
        self.cnt[k] += 16
        ins.then_inc(self.semh[k], 16)
        t = (k, self.cnt[k])
        for r in reads:
            if r.r.get(k, 0) < t[1]:
                r.r[k] = t[1]
        for w in writes:
            w.w = t
            w.r = {}
        self.ninst += 1
        return ins

    def wait_all(self, e, ress):
        need = {}
        for r in ress:
            if r.w is not None:
                k, v = r.w
                need[k] = max(need.get(k, 0), v)
            for k, v in r.r.items():
                need[k] = max(need.get(k, 0), v)
        self._waits(e, {k: (v, v) for k, v in need.items()})


class T:
    def __init__(self, kb, name, shape, dt, psum=False):
        nc = kb.nc
        if psum:
            self.t = kb.es.enter_context(nc.psum_tensor(name, shape, dt))
        else:
            self.t = kb.scope.enter_context(nc.sbuf_tensor(name, shape, dt))
        self.r = Res()

    def __getitem__(self, k):
        return self.t[k]


class _View3:
    def __init__(self, t):
        self.t, self.r = t, t.r

    def __getitem__(self, k):
        if k == slice(None):
            return self.t[:, 0:2, :]
        return self.t[k]


class _ViewC:
    def __init__(self, t, fn):
        self.t, self.r, self.v = t, t.r, fn(t)

    def __getitem__(self, k):
        return self.v[k]


class _View:
    def __init__(self, t, sl):
        self.t, self.sl, self.r = t, sl, t.r

    def __getitem__(self, k):
        a, b, c = k
        assert c == slice(None)
        return self.t[a, b, self.sl]


def build(S, DEPTH, SEG=512, TT=512, MIX=('rw', 'gla', 'ret', 'lru'), XA=True):
    nc = bass.Bass("TRN2", target_bir_lowering=False)
    es = ExitStack()
    es.enter_context(nc.allow_non_contiguous_dma(reason="small per-channel vectors / strided layouts"))
    kb = KB(nc, es)
    NSEG = S // SEG
    NT = SEG // TT
    L = DEPTH

    def din(name, shape, dt=F32):
        return nc.dram_tensor(name, list(shape), dt, kind="ExternalInput").ap()

    x = din("x", [S, D])
    out = nc.dram_tensor("out", [S, D], F32, kind="ExternalOutput").ap()
    final_norm_g = din("final_norm_g", [D])
    hT = nc.dram_tensor("hT", [D, S], F32).ap()
    hT_r = Res()

    ident = T(kb, "ident", [128, 128], F32)
    ones_f = T(kb, "ones_f", [128, 128], F32)
    kb.emit('pool', lambda e: e.memset(ones_f[:], 1.0), writes=[ones_f.r])
    kb.emit('pool', lambda e: e.memset(ident[:], 1.0), writes=[ident.r])
    kb.emit('pool', lambda e: e.affine_select(out=ident[:], in_=ident[:], pattern=[[1, 128]], base=0,
                                              channel_multiplier=-1, compare_op=ALU.is_equal, fill=0.0),
            reads=[ident.r], writes=[ident.r])

    eps_t = T(kb, "eps_t", [128, 4], F32)
    kb.emit('pool', lambda e: e.memset(eps_t[:, 0:1], NORM_EPS), writes=[eps_t.r])
    kb.emit('pool', lambda e: e.memset(eps_t[:, 1:2], RW_LN_EPS), writes=[eps_t.r])
    kb.emit('pool', lambda e: e.memset(eps_t[:, 2:3], 1.0), writes=[eps_t.r])
    kb.emit('pool', lambda e: e.memset(eps_t[:, 3:4], 0.0), writes=[eps_t.r])
    ones_b = T(kb, "ones_b", [128, 128], BF16)
    kb.emit('pool', lambda e: e.memset(ones_b[:], 1.0), writes=[ones_b.r])
    sq = T(kb, "sq", [128, TT], F32)
    rstd = T(kb, "rstd", [128, TT], F32)
    PS = [T(kb, "ps%d" % i, [128, 512], F32, psum=True) for i in range(8)]
    kb.PS = PS
    psi = [0]

    def ps():
        p = PS[psi[0] % 8]
        psi[0] += 1
        return p

    _tl = {}

    def TL(name, shape, dt):
        if name not in _tl:
            _tl[name] = T(kb, name, shape, dt)
        return _tl[name]

    def load_vec_fm(name, ap1d, n=D):
        t = TL(name, [128, n // 128], F32)
        kb.dma('sp', t[:], ap1d.rearrange("(k p) -> p k", p=128), t.r, writes=[t.r])
        return t

    fng = load_vec_fm("fng", final_norm_g)

    kb.new_scope()
    xin = [T(kb, "xin%d" % i, [128, D], F32) for i in range(2)]
    xtr = [T(kb, "xtr%d" % i, [128, KC, 128], F32) for i in range(2)]
    hT_v = hT.rearrange("(k p) s -> p k s", p=128)
    for tb in range(S // 128):
        xi = xin[tb % 2]
        xo = xtr[tb % 2]
        kb.dma('sp', xi[:], x[tb * 128:(tb + 1) * 128, :], xi.r, writes=[xi.r])
        for half in range(2):
            p = ps()
            for kk in range(4):
                k = half * 4 + kk
                kb.emit('pe', lambda e, p=p, kk=kk, k=k: e.transpose(out=p[:, kk * 128:(kk + 1) * 128],
                                                                     in_=xi[:, k * 128:(k + 1) * 128], identity=ident[:]),
                        reads=[xi.r, ident.r], writes=[p.r])
            eng = 'act' if half else 'dve'
            if eng == 'act':
                kb.emit('act', lambda e, p=p, half=half: e.activation(
                    out=xo[:, half * 4:(half + 1) * 4, :].rearrange("p k s -> p (k s)"), in_=p[:], func=AF.Copy),
                        reads=[p.r], writes=[xo.r])
            else:
                kb.emit('dve', lambda e, p=p, half=half: e.tensor_copy(
                    out=xo[:, half * 4:(half + 1) * 4, :].rearrange("p k s -> p (k s)"), in_=p[:]),
                        reads=[p.r], writes=[xo.r])
        kb.dma('pool', hT_v[:, :, tb * 128:(tb + 1) * 128], xo[:], xo.r, reads=[xo.r], writes=[hT_r])


    def rmsnorm_fm(hs, g, dst_fn, dst_res):
        p = ps()
        for k in range(KC):
            kb.emit('act', lambda e, k=k: e.activation(out=sq[:], in_=hs[:, k, :], func=AF.Square),
                    reads=[hs.r], writes=[sq.r])
            kb.emit('pe', lambda e, k=k, p=p: e.matmul(p[:], lhsT=ones_f[:], rhs=sq[:], start=(k == 0), stop=(k == KC - 1)),
                    reads=[sq.r, ones_f.r], writes=[p.r])
        kb.emit('act', lambda e, p=p: e.activation(out=rstd[:], in_=p[:], func=AF.Ln, scale=1.0 / D, bias=eps_t[:, 0:1]),
                reads=[p.r, eps_t.r], writes=[rstd.r])
        kb.emit('act', lambda e: e.activation(out=rstd[:], in_=rstd[:], func=AF.Exp, scale=-0.5),
                reads=[rstd.r], writes=[rstd.r])
        for k in range(KC):
            kb.emit('dve', lambda e, k=k: e.scalar_tensor_tensor(out=dst_fn(k), in0=hs[:, k, :], scalar=g[:, k:k + 1],
                                                                 in1=rstd[:], op0=ALU.mult, op1=ALU.mult),
                    reads=[hs.r, g.r, rstd.r], writes=[dst_res])

    kb.new_scope()
    mem = din("mem", [NMEM, D])
    positions = din("positions", [S], I32)
    P = {}
    for nm, shp in (('mix_norm_g', [2, D]), ('w_in', [2, D, N_IN]), ('rw_mu', [2, 1664]), ('rw_w0', [2, 512]),
                    ('rw_w2', [2, 64, 512]), ('rw_a0', [2, 512]), ('rw_a2', [2, 64, 512]), ('rw_k_k', [2, 512]),
                    ('rw_k_a', [2, 512]), ('rw_r_k', [2, 8, 64]), ('rw_ln_g', [2, 512]), ('rw_ln_b', [2, 512]),
                    ('gla_f_up', [2, 16, 256]), ('gla_f_b', [2, 256]), ('gla_norm_g', [2, 512]), ('ret_gn_g', [2, 512]),
                    ('lru_conv_w', [2, 4, 512]), ('lru_conv_b', [2, 512]), ('lru_wa', [2, 8, 64, 64]), ('lru_ba', [2, 512]),
                    ('lru_wx', [2, 8, 64, 64]), ('lru_bx', [2, 512]), ('lru_lambda', [2, 512]),
                    ('w_branch', [2, 4, 512, D]), ('w_out', [2, D, D]), ('xa_norm_g', [2, D]), ('xa_mem_norm_g', [2, D]),
                    ('xa_wq', [2, D, D]), ('xa_wkv', [2, D, 2 * D]), ('xa_wo', [2, D, D])):
        P[nm] = din(nm, shp)

    def dscr(name, shape, dt=BF16):
        return nc.dram_tensor(name, list(shape), dt).ap()

    Wb = [dscr("Wb%d" % l, [D, NAUG]) for l in range(L)]
    WBR = [dscr("WBR%d" % l, [4 * 512, D]) for l in range(L)]
    WO = [dscr("WO%d" % l, [D, D]) for l in range(L)]
    WQ = [dscr("WQ%d" % l, [D, D]) for l in range(L)]
    WKV = [dscr("WKV%d" % l, [D, 2 * D]) for l in range(L)]
    WXO = [dscr("WXO%d" % l, [D, D]) for l in range(L)]
    wdram_r = Res()

    PW = 2048
    wld = [T(kb, "wld%d" % i, [128, PW], F32) for i in range(2)]
    wst = [T(kb, "wst%d" % i, [128, PW], BF16) for i in range(2)]
    scl = T(kb, "scl", [128, 512], F32)
    pi = [0]

    def prep(src, dst, rows, n, scale=None, swap=False):
        for rc in range(rows // 128):
            a = wld[pi[0] % 2]
            b = wst[pi[0] % 2]
            pi[0] += 1
            kb.dma('sp', a[:, 0:n], src[rc * 128:(rc + 1) * 128, :], a.r, writes=[a.r])
            if swap:
                av = a[:, 0:n].rearrange("p (h two d) -> p h two d", two=2, d=32)
                bv = b[:, 0:n].rearrange("p (h two d) -> p h two d", two=2, d=32)
                kb.emit('dve', lambda e: e.tensor_scalar(out=bv[:, :, 0, :], in0=av[:, :, 1, :], scalar1=-1.0, scalar2=None, op0=ALU.mult),
                        reads=[a.r], writes=[b.r])
                kb.emit('dve', lambda e: e.tensor_copy(out=bv[:, :, 1, :], in_=av[:, :, 0, :]), reads=[a.r, b.r], writes=[b.r])
            elif scale is None:
                if pi[0] % 2:
                    kb.emit('act', lambda e: e.activation(out=b[:, 0:n], in_=a[:, 0:n], func=AF.Copy),
                            reads=[a.r], writes=[b.r])
                else:
                    kb.emit('dve', lambda e: e.tensor_copy(out=b[:, 0:n], in_=a[:, 0:n]), reads=[a.r], writes=[b.r])
            else:
                kb.emit('dve', lambda e: e.tensor_tensor(out=b[:, 0:n], in0=a[:, 0:n], in1=scale[:, 0:n], op=ALU.mult),
                        reads=[a.r, scale.r], writes=[b.r])
            kb.dma('pool', dst[rc * 128:(rc + 1) * 128, :], b[:, 0:n], b.r, reads=[b.r], writes=[wdram_r])

    memt = T(kb, "memt", [128, 2, D], F32)
    memT32 = T(kb, "memT32", [128, KC, NMEM], F32)
    memT_d = nc.dram_tensor("memT_d", [D, NMEM], F32).ap()
    memT_r = Res()
    kb.dma('sp', memt[:], mem.rearrange("(b p) d -> p b d", p=128), memt.r, writes=[memt.r])
    for bk in range(2):
        for half in range(2):
            p = ps()
            for kk in range(4):
                k = half * 4 + kk
                kb.emit('pe', lambda e, p=p, kk=kk, k=k, bk=bk: e.transpose(out=p[:, kk * 128:(kk + 1) * 128],
                                                                            in_=memt[:, bk, k * 128:(k + 1) * 128], identity=ident[:]),
                        reads=[memt.r, ident.r], writes=[p.r])
            kb.emit('dve', lambda e, p=p, half=half, bk=bk: e.tensor_copy(
                out=memT32[:, half * 4:(half + 1) * 4, bk * 128:(bk + 1) * 128],
                in_=p[:].rearrange("p (k s) -> p k s", k=4)), reads=[p.r], writes=[memT32.r])


    kb.dma('pool', memT_d.rearrange("(k p) m -> p k m", p=128), memT32[:], memT32.r, reads=[memT32.r], writes=[memT_r])
    for l in range(L):
        for (an, w, so, kind, arg) in AUG:
            src = P['w_in'][l, :, so:so + w]
            dst = Wb[l][:, AUGOFF[an]:AUGOFF[an] + w]
            if kind == 'plain':
                prep(src, dst, D, w)
            elif kind == 'swap':
                prep(src, dst, D, w, swap=True)
            else:
                if kind in ('mu', 'mu1m'):
                    row = P['rw_mu'][l, arg:arg + w]
                else:
                    row = P['lru_conv_w'][l, arg, :]
                kb.dma('sp', scl[:, 0:w], row.partition_broadcast(128), scl.r, writes=[scl.r])
                if kind == 'mu1m':
                    kb.emit('dve', lambda e, w=w: e.tensor_scalar(out=scl[:, 0:w], in0=scl[:, 0:w], scalar1=-1.0, scalar2=1.0,
                                                                   op0=ALU.mult, op1=ALU.add), reads=[scl.r], writes=[scl.r])
                prep(src, dst, D, w, scale=scl)
        prep(P['w_branch'][l].rearrange("n r c -> (n r) c"), WBR[l], 4 * 512, D)
        prep(P['w_out'][l], WO[l], D, D)
        prep(P['xa_wq'][l], WQ[l], D, D)
        prep(P['xa_wkv'][l], WKV[l], D, 2 * D)
        prep(P['xa_wo'][l], WXO[l], D, D)

    kb.new_scope()
    _tl.clear()
    kT = T(kb, "kT", [128, KC, NMEM], BF16)
    vtm = T(kb, "vtm", [128, 2, D], BF16)
    hseg = T(kb, "hseg", [128, KC, SEG], F32)
    un = T(kb, "un", [128, KC, 3 + SEG], BF16)
    halo = T(kb, "halo", [128, KC, 3], BF16)
    yg = [T(kb, "yg%d" % n, [128, 4, SEG], BF16) for n in range(4)]
    merged = T(kb, "merged", [128, KC, SEG], BF16)
    NW = 3
    NC8_ = TT // 64
    twb = T(kb, "twb", [64, TT], BF16)
    alb = T(kb, "alb", [64, TT], BF16)
    AR = T(kb, "AR", [128, NC8_, 128], BF16)
    BK32 = T(kb, "BK32", [128, 2, TT], F32)
    BKb = T(kb, "BKb", [128, 2, TT], BF16)
    VT = T(kb, "VT", [128, NC8_, 64], BF16)
    BTm = T(kb, "BTm", [128, NC8_, 64], BF16)
    KTm = T(kb, "KTm", [128, NC8_, 64], BF16)
    Mall = T(kb, "Mall", [128, NC8_, 320], BF16)
    AN = [T(kb, "AN%d" % i, [128, NC8_, 128], BF16) for i in range(2)]
    TTb = T(kb, "TTb", [128, NC8_, 64], BF16)
    vb = T(kb, "vb", [128, TT], BF16)
    Xb = T(kb, "Xb", [128, 64], BF16)
    Ub = T(kb, "Ub", [128, 64], BF16)
    gC = T(kb, "gC", [128, NC8_], F32)
    def _interleave(gens):
        gens = [g for g in gens if g is not None]
        while gens:
            for g in list(gens):
                try:
                    next(g)
                except StopIteration:
                    gens.remove(g)
    identb = T(kb, "identb", [128, 1, 64], BF16)
    kb.emit('dve', lambda e: e.tensor_copy(out=identb[0:64, 0, :], in_=ident[0:64, 0:64]), reads=[ident.r], writes=[identb.r])
    kb.emit('dve', lambda e: e.tensor_copy(out=identb[64:128, 0, :], in_=ident[64:128, 64:128]), reads=[ident.r], writes=[identb.r])
    ones_bd = T(kb, "ones_bd", [128, 128], F32)
    kb.emit('pool', lambda e: e.memset(ones_bd[:], 0.0), writes=[ones_bd.r])
    kb.emit('pool', lambda e: e.memset(ones_bd[0:64, 0:64], 1.0), writes=[ones_bd.r])
    kb.emit('pool', lambda e: e.memset(ones_bd[64:128, 64:128], 1.0), writes=[ones_bd.r])
    maskR = T(kb, "maskR", [128, 320], F32)
    kb.emit('pool', lambda e: e.memset(maskR[:], 1.0), writes=[maskR.r])
    for hv_ in (slice(0, 64), slice(64, 128)):
        for (c0_, strict, transposed) in ((0, True, False), (64, False, False), (128, True, False), (192, False, False), (256, True, True)):
            kb.emit('pool', lambda e, c0_=c0_, strict=strict, transposed=transposed, hv_=hv_: e.affine_select(
                out=maskR[hv_, c0_:c0_ + 64], in_=maskR[hv_, c0_:c0_ + 64], pattern=[[-1 if transposed else 1, 64]], base=0,
                channel_multiplier=(1 if transposed else -1), compare_op=(ALU.is_gt if strict else ALU.is_ge), fill=0.0),
                    reads=[maskR.r], writes=[maskR.r])
    vT = [T(kb, "vT%d" % i, [128, 512], BF16) for i in range(TT // 128)]
    qd = T(kb, "qd", [64, 4, TT], BF16)
    kd = T(kb, "kd", [64, 4, TT], BF16)
    kd32 = T(kb, "kd32", [64, 4, TT], F32)
    kdT = T(kb, "kdT", [128, 256], BF16)
    attb = T(kb, "attb", [128, 512], BF16)
    oall = T(kb, "oall", [128, 4, TT], F32)
    ebl = T(kb, "ebl", [64, 4, TT // 128], F32)
    flo_b = T(kb, "flo_b", [16, TT], BF16)
    wflo = T(kb, "wflo", [128, KC, 16], BF16)
    posi = T(kb, "posi", [64, TT], I32)
    cosT = T(kb, "cosT", [128, TT], F32)
    sinT = T(kb, "sinT", [128, TT], F32)
    yT, bonus = cosT, sinT
    tmp2 = T(kb, "tmp2", [128, TT], F32)
    sgate = T(kb, "sgate", [128, TT], BF16)
    RWB = [(AR, BKb, VT, BTm, KTm, gC, sinT, sgate),
           (T(kb, "AR_b", [128, NC8_, 128], BF16), T(kb, "BKb_b", [128, 2, TT], BF16), T(kb, "VT_b", [128, NC8_, 64], BF16),
            T(kb, "BTm_b", [128, NC8_, 64], BF16), T(kb, "KTm_b", [128, NC8_, 64], BF16), T(kb, "gC_b", [128, NC8_], F32),
            T(kb, "bonus_b", [128, TT], F32), T(kb, "sgate_b", [128, TT], BF16))]

    mask4 = T(kb, "mask4", [128, 4, 128], F32)
    kb.emit('pool', lambda e: e.memset(mask4[:], 1.0), writes=[mask4.r])
    kb.emit('pool', lambda e: e.affine_select(out=mask4[:], in_=mask4[:], pattern=[[0, 4], [1, 128]], base=0, channel_multiplier=-1,
                                              compare_op=ALU.is_ge, fill=0.0), reads=[mask4.r], writes=[mask4.r])
    ebR = T(kb, "ebR", [64, 4, 128], F32)
    enbR = T(kb, "enbR", [64, 4, 128], F32)
    iot = T(kb, "iot", [64, 128], F32)
    invf = T(kb, "invf", [64, 1], F32)
    kb.emit('pool', lambda e: e.iota(iot[:], pattern=[[1, 128]], base=1, channel_multiplier=0, allow_small_or_imprecise_dtypes=True), writes=[iot.r])
    for h in range(4):
        lg = float(np.log(RET_G[h]))
        kb.emit('act', lambda e, h=h, lg=lg: e.activation(out=ebR[:, h, :], in_=iot[:], func=AF.Exp, scale=lg), reads=[iot.r], writes=[ebR.r])
        kb.emit('act', lambda e, h=h, lg=lg: e.activation(out=enbR[:, h, :], in_=iot[:], func=AF.Exp, scale=-lg), reads=[iot.r], writes=[enbR.r])
    kb.emit('dve', lambda e: e.tensor_scalar(out=enbR[:], in0=enbR[:], scalar1=0.125, scalar2=None, op0=ALU.mult), reads=[enbR.r], writes=[enbR.r])
    pidx = T(kb, "pidx", [64, 2], F32)
    kb.emit('pool', lambda e: e.iota(pidx[:, 0:1], pattern=[[0, 1]], base=0, channel_multiplier=1, allow_small_or_imprecise_dtypes=True), writes=[pidx.r])
    kb.emit('dve', lambda e: e.tensor_scalar(out=pidx[:, 1:2], in0=pidx[:, 0:1], scalar1=32.0, scalar2=-32.0, op0=ALU.is_ge, op1=ALU.mult), reads=[pidx.r], writes=[pidx.r])
    kb.emit('dve', lambda e: e.tensor_tensor(out=pidx[:, 0:1], in0=pidx[:, 0:1], in1=pidx[:, 1:2], op=ALU.add), reads=[pidx.r], writes=[pidx.r])
    kb.emit('act', lambda e: e.activation(out=invf[:], in_=pidx[:, 0:1], func=AF.Exp, scale=float(-np.log(10000.0) / 32.0)), reads=[pidx.r], writes=[invf.r])
    wt = [T(kb, "wt%d" % i, [128, KC, 512], BF16) for i in range(NW)]
    wi = [0]

    def wtile():
        t = wt[wi[0] % NW]
        wi[0] += 1
        return t

    def load_w(t, src2d, ncols, col0=0, kc=KC, batch=False):
        kb.dma('sp', t[:, 0:kc, col0:col0 + ncols], src2d.rearrange("(k p) c -> p k c", p=128), t.r,
               reads=[wdram_r], writes=[t.r], batch=batch)

    WK = [T(kb, "wk%d" % i, [128, TT], F32) for i in range(10)]
    WB16 = [T(kb, "wb%d" % i, [128, TT], BF16) for i in range(4)]

    def fm_proj(dst_ps, wtile_, c0, ncols, t0, shift=0, start=True, stop=True, kc=KC):
        for k in range(kc):
            kb.emit('pe', lambda e, k=k: e.matmul(dst_ps[0:ncols, :], lhsT=wtile_[:, k, c0:c0 + ncols],
                                                  rhs=un[:, k, 3 + t0 + shift:3 + t0 + shift + TT],
                                                  start=(start and k == 0), stop=(stop and k == kc - 1)),
                    reads=[wtile_.r, un.r], writes=[dst_ps.r])

    def rmsnorm_cols(src, g, dst, ncol):
        p = ps()
        for k in range(KC):
            kb.emit('act', lambda e, k=k: e.activation(out=sq[:, 0:ncol], in_=src[:, k, 0:ncol], func=AF.Square),
                    reads=[src.r], writes=[sq.r])
            kb.emit('pe', lambda e, k=k, p=p: e.matmul(p[:, 0:ncol], lhsT=ones_f[:], rhs=sq[:, 0:ncol], start=(k == 0), stop=(k == KC - 1)),
                    reads=[sq.r, ones_f.r], writes=[p.r])
        kb.emit('act', lambda e, p=p: e.activation(out=rstd[:, 0:ncol], in_=p[:, 0:ncol], func=AF.Ln, scale=1.0 / D, bias=eps_t[:, 0:1]),
                reads=[p.r, eps_t.r], writes=[rstd.r])
        kb.emit('act', lambda e: e.activation(out=rstd[:, 0:ncol], in_=rstd[:, 0:ncol], func=AF.Exp, scale=-0.5),
                reads=[rstd.r], writes=[rstd.r])
        for k in range(KC):
            kb.emit('dve', lambda e, k=k: e.scalar_tensor_tensor(out=dst[:, k, 0:ncol], in0=src[:, k, 0:ncol], scalar=g[:, k:k + 1],
                                                                 in1=rstd[:, 0:ncol], op0=ALU.mult, op1=ALU.mult),
                    reads=[src.r, g.r, rstd.r], writes=[dst.r])

    for l in range(L):
        mng = load_vec_fm("mng", P['mix_norm_g'][l])
        xng = load_vec_fm("xng", P['xa_norm_g'][l])
        xmg = load_vec_fm("xmg", P['xa_mem_norm_g'][l])
        mnT = _ViewC(merged, lambda t: t[:, :, 0:NMEM])
        mem32 = _ViewC(oall, lambda t: t[:].rearrange("p h (a t) -> p (h a) t", a=2))
        kb.dma('sp', mem32[:, :, :], memT_d.rearrange("(k p) m -> p k m", p=128), oall.r, reads=[memT_r], writes=[oall.r])
        rmsnorm_cols(mem32, xmg, mnT, NMEM)
        for j in range(KC):
            w = wtile()
            load_w(w, WKV[l][:, j * 128:(j + 1) * 128], 128)
            p = ps()
            for k in range(KC):
                kb.emit('pe', lambda e, k=k, p=p, w=w: e.matmul(p[:, 0:NMEM], lhsT=w[:, k, 0:128], rhs=mnT[:, k, :],
                                                                start=(k == 0), stop=(k == KC - 1)),
                        reads=[w.r, mnT.r], writes=[p.r])
            kb.emit('act', lambda e, p=p, j=j: e.activation(out=kT[:, j, :], in_=p[:, 0:NMEM], func=AF.Copy),
                    reads=[p.r], writes=[kT.r])
        for c2 in range(2):
            w = wtile()
            load_w(w, WKV[l][:, D + c2 * 512:D + (c2 + 1) * 512], 512)
            for bk in range(2):
                p = ps()
                for k in range(KC):
                    kb.emit('pe', lambda e, k=k, p=p, w=w, bk=bk: e.matmul(p[:], lhsT=mnT[:, k, bk * 128:(bk + 1) * 128], rhs=w[:, k, :],
                                                                           start=(k == 0), stop=(k == KC - 1)),
                            reads=[w.r, mnT.r], writes=[p.r])
                kb.emit('act', lambda e, p=p, bk=bk, c2=c2: e.activation(out=vtm[:, bk, c2 * 512:(c2 + 1) * 512], in_=p[:], func=AF.Copy),
                        reads=[p.r], writes=[vtm.r])

        if 'rw' in MIX:
            def hv(name, ap1d):
                t_ = TL(name, [128, 4], F32)
                kb.dma('sp', t_[:], ap1d.rearrange("(hp p) -> p hp", p=128), t_.r, writes=[t_.r])
                return t_
            w0t = hv("w0t", P['rw_w0'][l])
            a0t = hv("a0t", P['rw_a0'][l])
            kkt = hv("kkt", P['rw_k_k'][l])
            kat = hv("kat", P['rw_k_a'][l])
            rkt = hv("rkt", P['rw_r_k'][l].rearrange("h k -> (h k)"))
            lngt = hv("lngt", P['rw_ln_g'][l])
            lnbt = hv("lnbt", P['rw_ln_b'][l])
            w2b = TL("w2b", [64, 512], BF16)
            a2b = TL("a2b", [64, 512], BF16)
            for (nmw, dstb, stg) in (('rw_w2', w2b, WK[2]), ('rw_a2', a2b, WK[3])):
                kb.dma('sp', stg[0:64, :], P[nmw][l], stg.r, writes=[stg.r])
                kb.emit('dve', lambda e, dstb=dstb, stg=stg: e.tensor_copy(out=dstb[:], in_=stg[0:64, :]), reads=[stg.r], writes=[dstb.r])
            STs = [TL("ST_%d" % h_, [128, 64], BF16) for h_ in range(4)]
            for t_ in STs:
                kb.emit('pool', lambda e, t_=t_: e.memset(t_[:], 0.0), writes=[t_.r])
        if 'gla' in MIX or 'ret' in MIX:
            fup32 = TL("fup32", [16, 256], F32)
            fup_b = T(kb, "fup_b%d" % l, [16, 256], BF16)
            kb.dma('sp', fup32[:], P['gla_f_up'][l], fup32.r, writes=[fup32.r])
            kb.emit('dve', lambda e: e.tensor_copy(out=fup_b[:], in_=fup32[:]), reads=[fup32.r], writes=[fup_b.r])
            nfb = TL("nfb", [64, 4], F32)
            kb.dma('sp', nfb[:], P['gla_f_b'][l].rearrange("(h d) -> d h", d=64), nfb.r, writes=[nfb.r])
            kb.emit('dve', lambda e: e.tensor_scalar(out=nfb[:], in0=nfb[:], scalar1=-1.0, scalar2=None, op0=ALU.mult), reads=[nfb.r], writes=[nfb.r])
            gng = load_vec_fm("gng", P['gla_norm_g'][l], 512)
            rgg = load_vec_fm("rgg", P['ret_gn_g'][l], 512)
            Sg32 = TL("Sg32", [64, 4, 128], F32)
            Sgb = TL("Sgb", [64, 4, 128], BF16)
            Sr32 = TL("Sr32", [64, 4, 128], F32)
            Srb = TL("Srb", [64, 4, 128], BF16)
            for t_ in (Sg32, Sgb, Sr32, Srb):
                kb.emit('pool', lambda e, t_=t_: e.memset(t_[:], 0.0), writes=[t_.r])
        if 'lru' in MIX:
            lcb = load_vec_fm("lcb", P['lru_conv_b'][l], 512)
            lba = load_vec_fm("lba", P['lru_ba'][l], 512)
            lbx = load_vec_fm("lbx", P['lru_bx'][l], 512)
            llam = load_vec_fm("llam", P['lru_lambda'][l], 512)
            lc = TL("lc", [128, 4], F32)
            lc2 = TL("lc2", [128, 4], F32)
            kb.emit('act', lambda e: e.activation(out=lc[:], in_=llam[:], func=AF.Exp, scale=-1.0), reads=[llam.r], writes=[lc.r])
            kb.emit('act', lambda e: e.activation(out=lc[:], in_=lc[:], func=AF.Ln, bias=eps_t[:, 2:3]), reads=[lc.r, eps_t.r], writes=[lc.r])
            kb.emit('dve', lambda e: e.tensor_scalar(out=lc2[:], in0=lc[:], scalar1=-16.0, scalar2=None, op0=ALU.mult), reads=[lc.r], writes=[lc2.r])
            kb.emit('dve', lambda e: e.tensor_scalar(out=lc[:], in0=lc[:], scalar1=-8.0, scalar2=None, op0=ALU.mult), reads=[lc.r], writes=[lc.r])
            wab = TL("wab", [128, 2, 4, 128], BF16)
            for wi_, nmw in enumerate(('lru_wa', 'lru_wx')):
                stg = WK[wi_]
                sv = stg[:].rearrange("p (j o) -> p j o", j=4)
                kb.emit('pool', lambda e, stg=stg: e.memset(stg[:], 0.0), writes=[stg.r])
                for bk in range(8):
                    j, hb = bk // 2, bk % 2
                    kb.dma('sp', sv[hb * 64:(hb + 1) * 64, j, hb * 64:(hb + 1) * 64], P[nmw][l, bk], stg.r,
                           writes=[stg.r], batch=(bk > 0))
                kb.emit('dve', lambda e, sv=sv, wi_=wi_: e.tensor_copy(out=wab[:, wi_, :, :], in_=sv), reads=[stg.r], writes=[wab.r])
            lcarry = TL("lcarry", [128, 4], F32)
            kb.emit('pool', lambda e: e.memset(lcarry[:], 0.0), writes=[lcarry.r])

        for sg in range(NSEG):
            s0 = sg * SEG
            kb.dma('sp', hseg[:], hT_v[:, :, s0:s0 + SEG], hseg.r, reads=[hT_r], writes=[hseg.r])
            if sg == 0:
                kb.emit('dve', lambda e: e.memset(un[:, :, 0:3], 0.0), writes=[un.r])
            else:
                kb.emit('dve', lambda e: e.tensor_copy(out=un[:, :, 0:3], in_=halo[:]), reads=[halo.r], writes=[un.r])
            for tt in range(NT):
                hv = _View(hseg, slice(tt * TT, (tt + 1) * TT))
                rmsnorm_fm(hv, mng, lambda k, tt=tt: un[:, k, 3 + tt * TT:3 + (tt + 1) * TT], un.r)
            kb.emit('dve', lambda e: e.tensor_copy(out=halo[:], in_=un[:, :, SEG:SEG + 3]), reads=[un.r], writes=[halo.r])

            for n in range(4):
                if ('rw', 'gla', 'ret', 'lru')[n] not in MIX:
                    kb.emit('pool', lambda e, n=n: e.memset(yg[n][:], 0.0), writes=[yg[n].r])

            def LRUgen():
                for j in range(4):
                    w = wtile()
                    for v in range(4):
                        c = AUGOFF['lru_x%d' % v] + j * 128
                        load_w(w, Wb[l][:, c:c + 128], 128, col0=v * 128, batch=(v > 0))
                    wg = wtile()
                    c = AUGOFF['lru_g'] + j * 128
                    load_w(wg, Wb[l][:, c:c + 128], 128)
                    for tt in range(NT):
                        t0 = tt * TT
                        _o = 5 * ((j * NT + tt) % 2)
                        xc, xcb, rr, ii, aa, uu = WK[_o], WB16[(j * NT + tt) % 2], WK[_o + 1], WK[_o + 2], WK[_o + 3], WK[_o + 4]
                        hh = rr
                        p = ps()
                        for v in range(4):
                            fm_proj(p, w, v * 128, 128, t0, shift=v - 3, start=(v == 0), stop=(v == 3))
                        kb.emit('act', lambda e, p=p, j=j: e.activation(out=xc[:], in_=p[:], func=AF.Identity, bias=lcb[:, j:j + 1]),
                                reads=[p.r, lcb.r], writes=[xc.r])
                        yield
                        kb.emit('dve', lambda e: e.tensor_copy(out=xcb[:], in_=xc[:]), reads=[xc.r], writes=[xcb.r])
                        yield
                        p1 = ps()
                        kb.emit('pe', lambda e, p1=p1, j=j: e.matmul(p1[:], lhsT=wab[:, 0, j, :], rhs=xcb[:], start=True, stop=True),
                                reads=[wab.r, xcb.r], writes=[p1.r])
                        p2 = ps()
                        kb.emit('pe', lambda e, p2=p2, j=j: e.matmul(p2[:], lhsT=wab[:, 1, j, :], rhs=xcb[:], start=True, stop=True),
                                reads=[wab.r, xcb.r], writes=[p2.r])
                        kb.emit('act', lambda e, p1=p1, j=j: e.activation(out=rr[:], in_=p1[:], func=AF.Sigmoid, bias=lba[:, j:j + 1]),
                                reads=[p1.r, lba.r], writes=[rr.r])
                        yield
                        kb.emit('act', lambda e, p2=p2, j=j: e.activation(out=ii[:], in_=p2[:], func=AF.Sigmoid, bias=lbx[:, j:j + 1]),
                                reads=[p2.r, lbx.r], writes=[ii.r])
                        yield
                        kb.emit('act', lambda e, j=j: e.activation(out=aa[:], in_=rr[:], func=AF.Exp, scale=lc[:, j:j + 1]),
                                reads=[rr.r, lc.r], writes=[aa.r])
                        yield
                        kb.emit('act', lambda e, j=j: e.activation(out=uu[:], in_=rr[:], func=AF.Exp, scale=lc2[:, j:j + 1]),
                                reads=[rr.r, lc2.r], writes=[uu.r])
                        yield
                        kb.emit('act', lambda e: e.activation(out=uu[:], in_=uu[:], func=AF.Ln, scale=-1.0, bias=eps_t[:, 2:3]),
                                reads=[uu.r, eps_t.r], writes=[uu.r])
                        yield
                        kb.emit('act', lambda e: e.activation(out=uu[:], in_=uu[:], func=AF.Exp, scale=0.5), reads=[uu.r], writes=[uu.r])
                        yield
                        kb.emit('dve', lambda e: e.tensor_tensor(out=ii[:], in0=ii[:], in1=xc[:], op=ALU.mult), reads=[ii.r, xc.r], writes=[ii.r])
                        yield
                        kb.emit('dve', lambda e: e.tensor_tensor(out=uu[:], in0=uu[:], in1=ii[:], op=ALU.mult), reads=[uu.r, ii.r], writes=[uu.r])
                        yield
                        kb.emit('dve', lambda e, j=j: e.tensor_tensor_scan(out=hh[:], data0=aa[:], data1=uu[:], initial=lcarry[:, j:j + 1],
                                                                           op0=ALU.mult, op1=ALU.add),
                                reads=[aa.r, uu.r, lcarry.r], writes=[hh.r])
                        yield
                        kb.emit('act', lambda e, j=j: e.activation(out=lcarry[:, j:j + 1], in_=hh[:, TT - 1:TT], func=AF.Copy),
                                reads=[hh.r], writes=[lcarry.r])
                        yield
                        p3 = ps()
                        fm_proj(p3, wg, 0, 128, t0)
                        kb.emit('act', lambda e, p3=p3: e.activation(out=ii[:], in_=p3[:], func=AF.Silu), reads=[p3.r], writes=[ii.r])
                        yield
                        kb.emit('dve', lambda e, j=j, t0=t0: e.tensor_tensor(out=yg[3][:, j, t0:t0 + TT], in0=hh[:], in1=ii[:], op=ALU.mult),
                                reads=[hh.r, ii.r], writes=[yg[3].r])
                        yield


                return
                yield
            if 'rw' in MIX:
                NC8 = TT // 64
                C0 = float(np.exp(-0.5))
                HV = (slice(0, 64), slice(64, 128))
                for tt in range(NT):
                    t0 = tt * TT
                    wc = wtile()
                    for i_, nm_ in enumerate(('rw_wlo_c', 'rw_wlo_p', 'rw_alo_c', 'rw_alo_p')):
                        load_w(wc, Wb[l][:, AUGOFF[nm_]:AUGOFF[nm_] + 64], 64, col0=i_ * 64, batch=(i_ > 0))
                    p = ps()
                    fm_proj(p, wc, 0, 64, t0, start=True, stop=False)
                    fm_proj(p, wc, 64, 64, t0, shift=-1, start=False, stop=True)
                    kb.emit('act', lambda e, p=p: e.activation(out=twb[:], in_=p[0:64, :], func=AF.Tanh), reads=[p.r], writes=[twb.r])
                    p = ps()
                    fm_proj(p, wc, 128, 64, t0, start=True, stop=False)
                    fm_proj(p, wc, 192, 64, t0, shift=-1, start=False, stop=True)
                    kb.emit('act', lambda e, p=p: e.activation(out=alb[:], in_=p[0:64, :], func=AF.Copy), reads=[p.r], writes=[alb.r])
                    def P1(hp, B):
                        AR, BKb, VT, BTm, KTm, gC, bonus, sgate = B
                        wa_ = wtile()
                        for i_, nm_ in enumerate(('rw_r_c', 'rw_r_p', 'rw_k_c', 'rw_k_p')):
                            c = AUGOFF[nm_] + hp * 128
                            load_w(wa_, Wb[l][:, c:c + 128], 128, col0=i_ * 128, batch=(i_ > 0))
                        wb_ = wtile()
                        for i_, nm_ in enumerate(('rw_v_c', 'rw_v_p', 'rw_g')):
                            c = AUGOFF[nm_] + hp * 128
                            load_w(wb_, Wb[l][:, c:c + 128], 128, col0=i_ * 128, batch=(i_ > 0))
                        r32, k32, v32, sg, asg, kkn, kmod, cs, eG, tmp = [WK[i] for i in range(10)]

                        def proj2(wt_, ccur, cprev):
                            pp = ps()
                            fm_proj(pp, wt_, ccur, 128, t0, start=True, stop=False)
                            fm_proj(pp, wt_, cprev, 128, t0, shift=-1, start=False, stop=True)
                            return pp
                        pr = proj2(wa_, 0, 128)
                        kb.emit('act', lambda e, pr=pr: e.activation(out=r32[:], in_=pr[:], func=AF.Copy), reads=[pr.r], writes=[r32.r])
                        yield
                        pk = proj2(wa_, 256, 384)
                        kb.emit('act', lambda e, pk=pk: e.activation(out=k32[:], in_=pk[:], func=AF.Copy), reads=[pk.r], writes=[k32.r])
                        yield
                        pv = proj2(wb_, 0, 128)
                        kb.emit('act', lambda e, pv=pv: e.activation(out=v32[:], in_=pv[:], func=AF.Copy), reads=[pv.r], writes=[v32.r])
                        yield
                        pw = ps()
                        kb.emit('pe', lambda e, pw=pw, hp=hp: e.matmul(pw[:], lhsT=w2b[:, hp * 128:(hp + 1) * 128], rhs=twb[:], start=True, stop=True),
                                reads=[w2b.r, twb.r], writes=[pw.r])
                        kb.emit('act', lambda e, pw=pw, hp=hp: e.activation(out=sg[:], in_=pw[:], func=AF.Sigmoid, bias=w0t[:, hp:hp + 1]),
                                reads=[pw.r, w0t.r], writes=[sg.r])
                        yield
                        pa_ = ps()
                        kb.emit('pe', lambda e, pa_=pa_, hp=hp: e.matmul(pa_[:], lhsT=a2b[:, hp * 128:(hp + 1) * 128], rhs=alb[:], start=True, stop=True),
                                reads=[a2b.r, alb.r], writes=[pa_.r])
                        kb.emit('act', lambda e, pa_=pa_, hp=hp: e.activation(out=asg[:], in_=pa_[:], func=AF.Sigmoid, bias=a0t[:, hp:hp + 1]),
                                reads=[pa_.r, a0t.r], writes=[asg.r])
                        yield
                        kb.emit('dve', lambda e, hp=hp: e.tensor_scalar(out=kkn[:], in0=k32[:], scalar1=kkt[:, hp:hp + 1], scalar2=None, op0=ALU.mult),
                                reads=[k32.r, kkt.r], writes=[kkn.r])
                        yield
                        kb.emit('act', lambda e: e.activation(out=tmp[:], in_=kkn[:], func=AF.Square), reads=[kkn.r], writes=[tmp.r])
                        yield
                        pn = ps()
                        kb.emit('pe', lambda e, pn=pn: e.matmul(pn[:], lhsT=ones_bd[:], rhs=tmp[:], start=True, stop=True), reads=[ones_bd.r, tmp.r], writes=[pn.r])
                        kb.emit('act', lambda e, pn=pn: e.activation(out=tmp[:], in_=pn[:], func=AF.Ln, bias=eps_t[:, 3:4]), reads=[pn.r, eps_t.r], writes=[tmp.r])
                        yield
                        kb.emit('act', lambda e: e.activation(out=tmp[:], in_=tmp[:], func=AF.Exp, scale=-0.5), reads=[tmp.r], writes=[tmp.r])
                        yield
                        kb.emit('dve', lambda e: e.tensor_tensor(out=kkn[:], in0=kkn[:], in1=tmp[:], op=ALU.mult), reads=[kkn.r, tmp.r], writes=[kkn.r])
                        yield
                        kb.emit('dve', lambda e, hp=hp: e.tensor_scalar(out=tmp[:], in0=asg[:], scalar1=-1.0, scalar2=kat[:, hp:hp + 1], op0=ALU.add, op1=ALU.mult),
                                reads=[asg.r, kat.r], writes=[tmp.r])
                        yield
                        kb.emit('dve', lambda e: e.scalar_tensor_tensor(out=kmod[:], in0=tmp[:], scalar=1.0, in1=k32[:], op0=ALU.add, op1=ALU.mult),
                                reads=[tmp.r, k32.r], writes=[kmod.r])
                        yield
                        kb.emit('dve', lambda e, hp=hp: e.scalar_tensor_tensor(out=tmp[:], in0=r32[:], scalar=rkt[:, hp:hp + 1], in1=kmod[:], op0=ALU.mult, op1=ALU.mult),
                                reads=[r32.r, rkt.r, kmod.r], writes=[tmp.r])
                        yield
                        pbn = ps()
                        kb.emit('pe', lambda e, pbn=pbn: e.matmul(pbn[:], lhsT=ones_bd[:], rhs=tmp[:], start=True, stop=True), reads=[ones_bd.r, tmp.r], writes=[pbn.r])
                        kb.emit('dve', lambda e, pbn=pbn: e.tensor_tensor(out=bonus[:], in0=pbn[:], in1=v32[:], op=ALU.mult), reads=[pbn.r, v32.r], writes=[bonus.r])
                        yield
                        for c in range(NC8):
                            kb.emit('dve', lambda e, c=c: e.tensor_tensor_scan(out=cs[:, c * 64:(c + 1) * 64], data0=ones_f[:, 0:64], data1=sg[:, c * 64:(c + 1) * 64],
                                                                               initial=0.0, op0=ALU.mult, op1=ALU.add), reads=[sg.r, ones_f.r], writes=[cs.r])
                            yield
                        kb.emit('act', lambda e: e.activation(out=eG[:], in_=cs[:], func=AF.Exp, scale=-C0), reads=[cs.r], writes=[eG.r])
                        yield
                        kb.emit('act', lambda e: e.activation(out=gC[:], in_=eG[:].rearrange("p (c t) -> p c t", t=64)[:, :, 63], func=AF.Copy), reads=[eG.r], writes=[gC.r])
                        yield
                        kb.emit('dve', lambda e: e.tensor_tensor(out=AR[:, :, 64:128], in0=r32[:].rearrange("p (c t) -> p c t", t=64),
                                                                 in1=eG[:].rearrange("p (c t) -> p c t", t=64), op=ALU.mult), reads=[r32.r, eG.r], writes=[AR.r])
                        yield
                        kb.emit('dve', lambda e: e.tensor_tensor(out=tmp[:], in0=cs[:], in1=sg[:], op=ALU.subtract), reads=[cs.r, sg.r], writes=[tmp.r])
                        yield
                        kb.emit('act', lambda e: e.activation(out=tmp[:], in_=tmp[:], func=AF.Exp, scale=-C0), reads=[tmp.r], writes=[tmp.r])
                        yield
                        kb.emit('dve', lambda e: e.scalar_tensor_tensor(out=AR[:, :, 0:64], in0=kkn[:].rearrange("p (c t) -> p c t", t=64), scalar=-1.0,
                                                                        in1=tmp[:].rearrange("p (c t) -> p c t", t=64), op0=ALU.mult, op1=ALU.mult),
                                reads=[kkn.r, tmp.r], writes=[AR.r])
                        yield
                        kb.emit('act', lambda e: e.activation(out=eG[:], in_=cs[:], func=AF.Exp, scale=C0), reads=[cs.r], writes=[eG.r])
                        yield
                        kb.emit('dve', lambda e: e.tensor_tensor(out=tmp[:], in0=kkn[:], in1=asg[:], op=ALU.mult), reads=[kkn.r, asg.r], writes=[tmp.r])
                        yield
                        kb.emit('dve', lambda e: e.tensor_tensor(out=BK32[:, 0, :], in0=tmp[:], in1=eG[:], op=ALU.mult), reads=[tmp.r, eG.r], writes=[BK32.r])
                        yield
                        kb.emit('dve', lambda e: e.tensor_tensor(out=BK32[:, 1, :], in0=kmod[:], in1=eG[:], op=ALU.mult), reads=[kmod.r, eG.r], writes=[BK32.r])
                        yield
                        kb.emit('act', lambda e: e.activation(out=BKb[:], in_=BK32[:], func=AF.Copy), reads=[BK32.r], writes=[BKb.r])
                        yield
                        kb.emit('act', lambda e: e.activation(out=vb[:], in_=v32[:], func=AF.Copy), reads=[v32.r], writes=[vb.r])
                        yield
                        for (srcfn, srcres, dstt) in ((lambda c, hv: vb[hv, c * 64:(c + 1) * 64], vb.r, VT), (lambda c, hv: BKb[hv, 0, c * 64:(c + 1) * 64], BKb.r, BTm),
                                                      (lambda c, hv: BKb[hv, 1, c * 64:(c + 1) * 64], BKb.r, KTm)):
                            ptp = ps()
                            for c in range(NC8):
                                for hv in HV:
                                    kb.emit('pe', lambda e, ptp=ptp, c=c, hv=hv, srcfn=srcfn: e.matmul(ptp[hv, c * 64:(c + 1) * 64], lhsT=srcfn(c, hv), rhs=identb[hv, 0, :], start=True, stop=True),
                                            reads=[srcres, identb.r], writes=[ptp.r])
                            kb.emit('act', lambda e, ptp=ptp, dstt=dstt: e.activation(out=dstt[:].rearrange("p c v -> p (c v)"), in_=ptp[:], func=AF.Copy),
                                    reads=[ptp.r], writes=[dstt.r])
                            yield
                        pg = ps()
                        fm_proj(pg, wb_, 256, 128, t0)
                        kb.emit('act', lambda e, pg=pg: e.activation(out=sgate[:], in_=pg[:], func=AF.Silu), reads=[pg.r], writes=[sgate.r])
                        yield
                        yield
                    def P2S(hp, B):
                        AR, BKb, VT, BTm, KTm, gC, bonus, sgate = B
                        tmp = tmp2
                        for c in range(NC8):
                            pm_ = ps()
                            for hv in HV:
                                kb.emit('pe', lambda e, pm_=pm_, c=c, hv=hv: e.matmul(pm_[hv, 0:128], lhsT=BKb[hv, 0, c * 64:(c + 1) * 64], rhs=AR[hv, c, :], start=True, stop=True),
                                        reads=[BKb.r, AR.r], writes=[pm_.r])
                                kb.emit('pe', lambda e, pm_=pm_, c=c, hv=hv: e.matmul(pm_[hv, 128:256], lhsT=BKb[hv, 1, c * 64:(c + 1) * 64], rhs=AR[hv, c, :], start=True, stop=True),
                                        reads=[BKb.r, AR.r], writes=[pm_.r])
                                kb.emit('pe', lambda e, pm_=pm_, c=c, hv=hv: e.matmul(pm_[hv, 256:320], lhsT=AR[hv, c, 0:64], rhs=BKb[hv, 0, c * 64:(c + 1) * 64], start=True, stop=True),
                                        reads=[BKb.r, AR.r], writes=[pm_.r])
                            kb.emit('dve', lambda e, pm_=pm_, c=c: e.tensor_tensor(out=Mall[:, c, :], in0=pm_[:, 0:320], in1=maskR[:], op=ALU.mult),
                                    reads=[pm_.r, maskR.r], writes=[Mall.r])
                            yield
                        kb.emit('dve', lambda e: e.tensor_copy(out=AN[0][:, :, 0:64], in_=Mall[:, :, 256:320]), reads=[Mall.r], writes=[AN[0].r])
                        yield
                        kb.emit('dve', lambda e: e.tensor_copy(out=AN[0][:, :, 64:128], in_=Mall[:, :, 0:64]), reads=[Mall.r], writes=[AN[0].r])
                        yield
                        kb.emit('dve', lambda e: e.tensor_tensor(out=TTb[:], in0=Mall[:, :, 0:64], in1=identb[:, 0:1, :].to_broadcast([128, NC8, 64]), op=ALU.add),
                                reads=[Mall.r, identb.r], writes=[TTb.r])
                        yield
                        for lev in range(5):
                            src_, dst_ = AN[lev % 2], AN[(lev + 1) % 2]
                            for half in range(2):
                                pd = ps()
                                for cc in range(4):
                                    c = half * 4 + cc
                                    for hv in HV:
                                        kb.emit('pe', lambda e, pd=pd, c=c, cc=cc, src_=src_, hv=hv: e.matmul(pd[hv, cc * 128:cc * 128 + 64], lhsT=src_[hv, c, 64:128], rhs=src_[hv, c, 0:64], start=True, stop=True),
                                                reads=[src_.r], writes=[pd.r])
                                        if lev < 4:
                                            kb.emit('pe', lambda e, pd=pd, c=c, cc=cc, src_=src_, hv=hv: e.matmul(pd[hv, cc * 128 + 64:cc * 128 + 128], lhsT=src_[hv, c, 0:64], rhs=src_[hv, c, 64:128], start=True, stop=True),
                                                    reads=[src_.r], writes=[pd.r])
                                kb.emit('act', lambda e, pd=pd, half=half, dst_=dst_: e.activation(out=dst_[:, half * 4:(half + 1) * 4, :].rearrange("p c x -> p (c x)"), in_=pd[:], func=AF.Copy),
                                        reads=[pd.r], writes=[dst_.r])
                                yield
                            pt_ = ps()
                            for c in range(NC8):
                                for hv in HV:
                                    kb.emit('pe', lambda e, pt_=pt_, c=c, dst_=dst_, hv=hv: e.matmul(pt_[hv, c * 64:(c + 1) * 64], lhsT=dst_[hv, c, 0:64], rhs=TTb[hv, c, :], start=True, stop=True),
                                            reads=[dst_.r, TTb.r], writes=[pt_.r])
                            kb.emit('dve', lambda e, pt_=pt_: e.tensor_tensor(out=TTb[:].rearrange("p c x -> p (c x)"), in0=TTb[:].rearrange("p c x -> p (c x)"), in1=pt_[:], op=ALU.add),
                                    reads=[pt_.r, TTb.r], writes=[TTb.r])
                            yield
                        ST = STs[hp]
                        for c in range(NC8):
                            px = ps()
                            for hv in HV:
                                kb.emit('pe', lambda e, px=px, c=c, hv=hv: e.matmul(px[hv, 0:64], lhsT=AR[hv, c, 0:64], rhs=ST[hv, :], start=True, stop=False), reads=[AR.r, ST.r], writes=[px.r])
                                kb.emit('pe', lambda e, px=px, c=c, hv=hv: e.matmul(px[hv, 0:64], lhsT=Mall[hv, c, 128:192], rhs=VT[hv, c, :], start=False, stop=True), reads=[Mall.r, VT.r], writes=[px.r])
                            kb.emit('act', lambda e, px=px: e.activation(out=Xb[:], in_=px[:, 0:64], func=AF.Copy), reads=[px.r], writes=[Xb.r])
                            yield
                            pu = ps()
                            for hv in HV:
                                kb.emit('pe', lambda e, pu=pu, c=c, hv=hv: e.matmul(pu[hv, 0:64], lhsT=TTb[hv, c, :], rhs=Xb[hv, :], start=True, stop=True), reads=[TTb.r, Xb.r], writes=[pu.r])
                            kb.emit('dve', lambda e, pu=pu: e.tensor_copy(out=Ub[:], in_=pu[:, 0:64]), reads=[pu.r], writes=[Ub.r])
                            yield
                            py = ps()
                            pS = ps()
                            for hv in HV:
                                kb.emit('pe', lambda e, py=py, c=c, hv=hv: e.matmul(py[hv, 0:64], lhsT=ST[hv, :], rhs=AR[hv, c, 64:128], start=True, stop=False), reads=[AR.r, ST.r], writes=[py.r])
                                kb.emit('pe', lambda e, py=py, c=c, hv=hv: e.matmul(py[hv, 0:64], lhsT=Ub[hv, :], rhs=Mall[hv, c, 64:128], start=False, stop=False), reads=[Ub.r, Mall.r], writes=[py.r])
                                kb.emit('pe', lambda e, py=py, c=c, hv=hv: e.matmul(py[hv, 0:64], lhsT=VT[hv, c, :], rhs=Mall[hv, c, 192:256], start=False, stop=True), reads=[VT.r, Mall.r], writes=[py.r])
                            for hv in HV:
                                kb.emit('pe', lambda e, pS=pS, c=c, hv=hv: e.matmul(pS[hv, 0:64], lhsT=BTm[hv, c, :], rhs=Ub[hv, :], start=True, stop=False), reads=[BTm.r, Ub.r], writes=[pS.r])
                                kb.emit('pe', lambda e, pS=pS, c=c, hv=hv: e.matmul(pS[hv, 0:64], lhsT=KTm[hv, c, :], rhs=VT[hv, c, :], start=False, stop=False), reads=[KTm.r, VT.r], writes=[pS.r])
                                kb.emit('pe', lambda e, pS=pS, hv=hv: e.matmul(pS[hv, 0:64], lhsT=identb[hv, 0, :], rhs=ST[hv, :], start=False, stop=True), reads=[identb.r, ST.r], writes=[pS.r])
                            kb.emit('act', lambda e, py=py, c=c: e.activation(out=yT[:, c * 64:(c + 1) * 64], in_=py[:, 0:64], func=AF.Copy), reads=[py.r], writes=[yT.r])
                            yield
                            kb.emit('dve', lambda e, pS=pS, c=c: e.tensor_scalar(out=ST[:], in0=pS[:, 0:64], scalar1=gC[:, c:c + 1], scalar2=None, op0=ALU.mult),
                                    reads=[pS.r, gC.r], writes=[ST.r])
                            yield
                        pm2 = ps()
                        kb.emit('pe', lambda e, pm2=pm2: e.matmul(pm2[:], lhsT=ones_bd[:], rhs=yT[:], start=True, stop=True), reads=[ones_bd.r, yT.r], writes=[pm2.r])
                        kb.emit('dve', lambda e, pm2=pm2: e.scalar_tensor_tensor(out=yT[:], in0=pm2[:], scalar=-1.0 / 64.0, in1=yT[:], op0=ALU.mult, op1=ALU.add),
                                reads=[pm2.r, yT.r], writes=[yT.r])
                        yield
                        kb.emit('act', lambda e: e.activation(out=tmp[:], in_=yT[:], func=AF.Square), reads=[yT.r], writes=[tmp.r])
                        yield
                        pv2 = ps()
                        kb.emit('pe', lambda e, pv2=pv2: e.matmul(pv2[:], lhsT=ones_bd[:], rhs=tmp[:], start=True, stop=True), reads=[ones_bd.r, tmp.r], writes=[pv2.r])
                        kb.emit('act', lambda e, pv2=pv2: e.activation(out=tmp[:], in_=pv2[:], func=AF.Ln, scale=1.0 / 64.0, bias=eps_t[:, 1:2]), reads=[pv2.r, eps_t.r], writes=[tmp.r])
                        yield
                        kb.emit('act', lambda e: e.activation(out=tmp[:], in_=tmp[:], func=AF.Exp, scale=-0.5), reads=[tmp.r], writes=[tmp.r])
                        yield
                        kb.emit('dve', lambda e: e.tensor_tensor(out=yT[:], in0=yT[:], in1=tmp[:], op=ALU.mult), reads=[yT.r, tmp.r], writes=[yT.r])
                        yield
                        kb.emit('dve', lambda e, hp=hp: e.tensor_scalar(out=yT[:], in0=yT[:], scalar1=lngt[:, hp:hp + 1], scalar2=lnbt[:, hp:hp + 1], op0=ALU.mult, op1=ALU.add),
                                reads=[yT.r, lngt.r, lnbt.r], writes=[yT.r])
                        yield
                        kb.emit('dve', lambda e: e.tensor_tensor(out=yT[:], in0=yT[:], in1=bonus[:], op=ALU.add), reads=[yT.r, bonus.r], writes=[yT.r])
                        yield
                        kb.emit('dve', lambda e, hp=hp: e.tensor_tensor(out=yg[0][:, hp, t0:t0 + TT], in0=yT[:], in1=sgate[:], op=ALU.mult), reads=[yT.r, sgate.r], writes=[yg[0].r])
                        yield
                        yield
                    prev = None
                    for hp in range(4):
                        gens = [P1(hp, RWB[hp % 2])] + ([prev] if prev is not None else [])
                        _interleave(gens)
                        prev = P2S(hp, RWB[hp % 2])
                    _interleave([prev] + ([LRUgen()] if ('lru' in MIX and tt == NT - 1) else []))

            if 'rw' not in MIX and 'lru' in MIX:
                _interleave([LRUgen()])
            for n_, nm in ((1, 'gla'), (2, 'ret')):
                if nm not in MIX:
                    continue
                isg = (nm == 'gla')
                S32, Sb = (Sg32, Sgb) if isg else (Sr32, Srb)
                gn = gng if isg else rgg
                wqk = wtile()
                load_w(wqk, Wb[l][:, AUGOFF[nm + '_q']:AUGOFF[nm + '_q'] + 256], 256, col0=0)
                load_w(wqk, Wb[l][:, AUGOFF[nm + '_k']:AUGOFF[nm + '_k'] + 256], 256, col0=256, batch=True)
                wv = wtile()
                load_w(wv, Wb[l][:, AUGOFF[nm + '_v']:AUGOFF[nm + '_v'] + 512], 512)
                if isg:
                    wg = wtile()
                    load_w(wg, Wb[l][:, AUGOFF[nm + '_g']:AUGOFF[nm + '_g'] + 512], 512)
                if isg:
                    kb.dma('sp', wflo[:], Wb[l][:, AUGOFF['gla_flo']:AUGOFF['gla_flo'] + 16].rearrange("(k p) c -> p k c", p=128),
                           wflo.r, reads=[wdram_r], writes=[wflo.r])
                else:
                    wsw = wtile()
                    load_w(wsw, Wb[l][:, AUGOFF['ret_qs']:AUGOFF['ret_qs'] + 256], 256, col0=0)
                    load_w(wsw, Wb[l][:, AUGOFF['ret_ks']:AUGOFF['ret_ks'] + 256], 256, col0=256, batch=True)
                for tt in range(NT):
                    t0 = tt * TT
                    NCH = TT // 128
                    for c in range(NCH):
                        p = ps()
                        for k in range(KC):
                            kb.emit('pe', lambda e, k=k, p=p, c=c: e.matmul(p[:], lhsT=un[:, k, 3 + t0 + c * 128:3 + t0 + (c + 1) * 128], rhs=wv[:, k, :],
                                                                            start=(k == 0), stop=(k == KC - 1)), reads=[un.r, wv.r], writes=[p.r])
                        kb.emit('act', lambda e, p=p, c=c: e.activation(out=vT[c][:], in_=p[:], func=AF.Copy), reads=[p.r], writes=[vT[c].r])
                    if isg:
                        pf = ps()
                        fm_proj(pf, wflo, 0, 16, t0)
                        kb.emit('act', lambda e, pf=pf: e.activation(out=flo_b[:], in_=pf[0:16, :], func=AF.Copy), reads=[pf.r], writes=[flo_b.r])
                    else:
                        kb.dma('sp', posi[:], positions[s0 + t0:s0 + t0 + TT].partition_broadcast(64), posi.r, writes=[posi.r])
                        A_, B_, C_ = WK[0], WK[1], WK[2]
                        kb.emit('dve', lambda e: e.tensor_copy(out=A_[0:64, :], in_=posi[:]), reads=[posi.r], writes=[A_.r])
                        kb.emit('dve', lambda e: e.tensor_scalar(out=A_[0:64, :], in0=A_[0:64, :], scalar1=invf[:, 0:1], scalar2=1.0 / (2 * np.pi),
                                                                 op0=ALU.mult, op1=ALU.mult), reads=[A_.r, invf.r], writes=[A_.r])
                        kb.emit('dve', lambda e: e.tensor_copy(out=posi[:], in_=A_[0:64, :]), reads=[A_.r], writes=[posi.r])
                        kb.emit('dve', lambda e: e.tensor_copy(out=B_[0:64, :], in_=posi[:]), reads=[posi.r], writes=[B_.r])
                        kb.emit('dve', lambda e: e.tensor_tensor(out=A_[0:64, :], in0=A_[0:64, :], in1=B_[0:64, :], op=ALU.subtract),
                                reads=[A_.r, B_.r], writes=[A_.r])
                        kb.emit('act', lambda e: e.activation(out=B_[0:64, :], in_=A_[0:64, :], func=AF.Sin, scale=float(np.pi)), reads=[A_.r], writes=[B_.r])
                        kb.emit('act', lambda e: e.activation(out=C_[0:64, :], in_=A_[0:64, :], func=AF.Sin, scale=float(np.pi / 2)), reads=[A_.r], writes=[C_.r])
                        kb.emit('dve', lambda e: e.tensor_tensor(out=cosT[0:64, :], in0=B_[0:64, :], in1=B_[0:64, :], op=ALU.mult), reads=[B_.r], writes=[cosT.r])
                        kb.emit('dve', lambda e: e.tensor_scalar(out=cosT[0:64, :], in0=cosT[0:64, :], scalar1=-2.0, scalar2=1.0, op0=ALU.mult, op1=ALU.add),
                                reads=[cosT.r], writes=[cosT.r])
                        kb.emit('dve', lambda e: e.tensor_tensor(out=C_[0:64, :], in0=C_[0:64, :], in1=C_[0:64, :], op=ALU.mult), reads=[C_.r], writes=[C_.r])
                        kb.emit('dve', lambda e: e.tensor_scalar(out=C_[0:64, :], in0=C_[0:64, :], scalar1=-4.0, scalar2=2.0, op0=ALU.mult, op1=ALU.add),
                                reads=[C_.r], writes=[C_.r])
                        kb.emit('dve', lambda e: e.tensor_tensor(out=sinT[0:64, :], in0=C_[0:64, :], in1=B_[0:64, :], op=ALU.mult), reads=[C_.r, B_.r], writes=[sinT.r])
                    def _dec(h):
                        pq = ps()
                        fm_proj(pq, wqk, h * 64, 64, t0)
                        pk = ps()
                        fm_proj(pk, wqk, 256 + h * 64, 64, t0)
                        if isg:
                            plf = ps()
                            kb.emit('pe', lambda e, plf=plf, h=h: e.matmul(plf[0:64, :], lhsT=fup_b[0:16, h * 64:(h + 1) * 64], rhs=flo_b[0:16, :], start=True, stop=True),
                                    reads=[fup_b.r, flo_b.r], writes=[plf.r])
                            cs, eb, enb = (WK[0], WK[1], WK[2]) if h % 2 == 0 else (WK[5], WK[6], WK[7])
                            kb.emit('act', lambda e, plf=plf, h=h: e.activation(out=cs[0:64, :], in_=plf[0:64, :], func=AF.Exp, scale=-1.0, bias=nfb[:, h:h + 1]),
                                    reads=[plf.r, nfb.r], writes=[cs.r])
                            yield
                            kb.emit('act', lambda e: e.activation(out=cs[0:64, :], in_=cs[0:64, :], func=AF.Ln, bias=eps_t[0:64, 2:3]), reads=[cs.r, eps_t.r], writes=[cs.r])
                            yield
                            for c in range(NCH):
                                kb.emit('dve', lambda e, c=c: e.tensor_tensor_scan(out=eb[0:64, c * 128:(c + 1) * 128], data0=ones_f[0:64, :], data1=cs[0:64, c * 128:(c + 1) * 128],
                                                                                   initial=0.0, op0=ALU.mult, op1=ALU.add), reads=[cs.r, ones_f.r], writes=[eb.r])
                                yield
                            kb.emit('act', lambda e: e.activation(out=enb[0:64, :], in_=eb[0:64, :], func=AF.Exp, scale=1.0 / 16.0), reads=[eb.r], writes=[enb.r])
                            yield
                            kb.emit('act', lambda e: e.activation(out=eb[0:64, :], in_=eb[0:64, :], func=AF.Exp, scale=-1.0 / 16.0), reads=[eb.r], writes=[eb.r])
                            yield
                            kb.emit('dve', lambda e, pq=pq, h=h: e.scalar_tensor_tensor(out=qd[:, h, :], in0=pq[0:64, :], scalar=0.125, in1=eb[0:64, :], op0=ALU.mult, op1=ALU.mult),
                                    reads=[pq.r, eb.r], writes=[qd.r])
                            yield
                            kb.emit('dve', lambda e, pk=pk, h=h: e.tensor_tensor(out=kd32[:, h, :], in0=pk[0:64, :], in1=enb[0:64, :], op=ALU.mult),
                                    reads=[pk.r, enb.r], writes=[kd32.r])
                            yield
                            for c in range(NCH):
                                kb.emit('act', lambda e, c=c, h=h: e.activation(out=ebl[:, h, c:c + 1], in_=eb[0:64, c * 128 + 127:c * 128 + 128], func=AF.Copy),
                                        reads=[eb.r], writes=[ebl.r])
                                yield
                        else:
                            pqs = ps()
                            fm_proj(pqs, wsw, h * 64, 64, t0)
                            pks = ps()
                            fm_proj(pks, wsw, 256 + h * 64, 64, t0)
                            q1, q2 = (WK[3], WK[4]) if h % 2 == 0 else (WK[8], WK[9])
                            for (pa, pb, dst, tab) in ((pq, pqs, qd, ebR), (pk, pks, kd32, enbR)):
                                kb.emit('dve', lambda e, pa=pa: e.tensor_tensor(out=q1[0:64, :], in0=pa[0:64, :], in1=cosT[0:64, :], op=ALU.mult), reads=[pa.r, cosT.r], writes=[q1.r])
                                yield
                                kb.emit('dve', lambda e, pb=pb: e.tensor_tensor(out=q2[0:64, :], in0=pb[0:64, :], in1=sinT[0:64, :], op=ALU.mult), reads=[pb.r, sinT.r], writes=[q2.r])
                                yield
                                kb.emit('dve', lambda e: e.tensor_tensor(out=q1[0:64, :], in0=q1[0:64, :], in1=q2[0:64, :], op=ALU.add), reads=[q1.r, q2.r], writes=[q1.r])
                                yield
                                kb.emit('dve', lambda e, dst=dst, tab=tab, h=h: e.tensor_tensor(
                                    out=dst[:, h, :].rearrange("p (c t) -> p c t", t=128), in0=q1[0:64, :].rearrange("p (c t) -> p c t", t=128),
                                    in1=tab[:, h:h + 1, :].to_broadcast([64, NCH, 128]), op=ALU.mult), reads=[q1.r, tab.r], writes=[dst.r])
                                yield
                    for h0 in (0, 2):
                        _interleave([_dec(h0), _dec(h0 + 1)])
                    kb.emit('act', lambda e: e.activation(out=kd[:], in_=kd32[:], func=AF.Copy), reads=[kd32.r], writes=[kd.r])
                    for c in range(NCH):
                        cs_ = slice(c * 128, (c + 1) * 128)
                        pa = ps()
                        for h in range(4):
                            kb.emit('pe', lambda e, pa=pa, h=h, cs_=cs_: e.matmul(pa[:, h * 128:(h + 1) * 128], lhsT=kd[:, h, cs_], rhs=qd[:, h, cs_], start=True, stop=True),
                                    reads=[kd.r, qd.r], writes=[pa.r])
                        kb.emit('dve', lambda e, pa=pa: e.tensor_tensor(out=attb[:], in0=pa[:], in1=mask4[:], op=ALU.mult), reads=[pa.r, mask4.r], writes=[attb.r])
                        po = ps()
                        for h in range(4):
                            kb.emit('pe', lambda e, po=po, h=h, c=c: e.matmul(po[:, h * 128:(h + 1) * 128], lhsT=vT[c][:, h * 128:(h + 1) * 128], rhs=attb[:, h * 128:(h + 1) * 128],
                                                                              start=True, stop=False), reads=[vT[c].r, attb.r], writes=[po.r])
                            kb.emit('pe', lambda e, po=po, h=h, cs_=cs_: e.matmul(po[:, h * 128:(h + 1) * 128], lhsT=Sb[:, h, :], rhs=qd[:, h, cs_], start=False, stop=True),
                                    reads=[Sb.r, qd.r], writes=[po.r])
                        kb.emit('act', lambda e, po=po, cs_=cs_: e.activation(out=oall[:, :, cs_], in_=po[:].rearrange("p (h t) -> p h t", h=4), func=AF.Copy),
                                reads=[po.r], writes=[oall.r])
                        pt = ps()
                        for h in range(4):
                            kb.emit('pe', lambda e, pt=pt, h=h, cs_=cs_: e.transpose(out=pt[:, h * 64:(h + 1) * 64], in_=kd32[:, h, cs_], identity=ident[0:64, 0:64]),
                                    reads=[kd32.r, ident.r], writes=[pt.r])
                        kb.emit('dve', lambda e, pt=pt: e.tensor_copy(out=kdT[:], in_=pt[:, 0:256]), reads=[pt.r], writes=[kdT.r])
                        pss = ps()
                        for h in range(4):
                            kb.emit('pe', lambda e, pss=pss, h=h, c=c: e.matmul(pss[0:64, h * 128:(h + 1) * 128], lhsT=kdT[:, h * 64:(h + 1) * 64], rhs=vT[c][:, h * 128:(h + 1) * 128],
                                                                                start=True, stop=True), reads=[kdT.r, vT[c].r], writes=[pss.r])
                        kb.emit('dve', lambda e, pss=pss: e.tensor_tensor(out=S32[:].rearrange("p h v -> p (h v)"), in0=S32[:].rearrange("p h v -> p (h v)"), in1=pss[0:64, :], op=ALU.add),
                                reads=[pss.r, S32.r], writes=[S32.r])
                        for h in range(4):
                            if isg:
                                kb.emit('dve', lambda e, h=h, c=c: e.tensor_scalar(out=S32[:, h, :], in0=S32[:, h, :], scalar1=ebl[:, h, c:c + 1], scalar2=None, op0=ALU.mult),
                                        reads=[S32.r, ebl.r], writes=[S32.r])
                            else:
                                kb.emit('dve', lambda e, h=h: e.tensor_scalar(out=S32[:, h, :], in0=S32[:, h, :], scalar1=float(RET_G[h] ** 128), scalar2=None, op0=ALU.mult),
                                        reads=[S32.r], writes=[S32.r])
                        kb.emit('act', lambda e: e.activation(out=Sb[:], in_=S32[:], func=AF.Copy), reads=[S32.r], writes=[Sb.r])
                    if not isg:
                        wg = wtile()
                        load_w(wg, Wb[l][:, AUGOFF[nm + '_g']:AUGOFF[nm + '_g'] + 512], 512)
                    def _nrm(h):
                        o_h, sqq, rs_, gg = (WK[5], WK[6], WK[7], WK[8]) if h % 2 == 0 else (WK[0], WK[1], WK[2], WK[3])
                        if isg:
                            kb.emit('act', lambda e, h=h: e.activation(out=sqq[:], in_=oall[:, h, :], func=AF.Square), reads=[oall.r], writes=[sqq.r])
                            yield
                            pv_ = ps()
                            kb.emit('pe', lambda e, pv_=pv_: e.matmul(pv_[:], lhsT=ones_f[:], rhs=sqq[:], start=True, stop=True), reads=[ones_f.r, sqq.r], writes=[pv_.r])
                            src_o = None
                        else:
                            pm_ = ps()
                            kb.emit('pe', lambda e, pm_=pm_, h=h: e.matmul(pm_[:], lhsT=ones_f[:], rhs=oall[:, h, :], start=True, stop=True), reads=[ones_f.r, oall.r], writes=[pm_.r])
                            kb.emit('dve', lambda e, pm_=pm_, h=h: e.scalar_tensor_tensor(out=o_h[:], in0=pm_[:], scalar=-1.0 / 128.0, in1=oall[:, h, :], op0=ALU.mult, op1=ALU.add),
                                    reads=[pm_.r, oall.r], writes=[o_h.r])
                            yield
                            kb.emit('act', lambda e: e.activation(out=sqq[:], in_=o_h[:], func=AF.Square), reads=[o_h.r], writes=[sqq.r])
                            yield
                            pv_ = ps()
                            kb.emit('pe', lambda e, pv_=pv_: e.matmul(pv_[:], lhsT=ones_f[:], rhs=sqq[:], start=True, stop=True), reads=[ones_f.r, sqq.r], writes=[pv_.r])
                        kb.emit('act', lambda e, pv_=pv_: e.activation(out=rs_[:], in_=pv_[:], func=AF.Ln, scale=1.0 / 128.0, bias=eps_t[:, 0:1]), reads=[pv_.r, eps_t.r], writes=[rs_.r])
                        yield
                        kb.emit('act', lambda e: e.activation(out=rs_[:], in_=rs_[:], func=AF.Exp, scale=-0.5), reads=[rs_.r], writes=[rs_.r])
                        yield
                        if isg:
                            kb.emit('dve', lambda e, h=h: e.scalar_tensor_tensor(out=rs_[:], in0=oall[:, h, :], scalar=gn[:, h:h + 1], in1=rs_[:], op0=ALU.mult, op1=ALU.mult),
                                    reads=[oall.r, gn.r, rs_.r], writes=[rs_.r])
                            yield
                        else:
                            kb.emit('dve', lambda e, h=h: e.scalar_tensor_tensor(out=rs_[:], in0=o_h[:], scalar=gn[:, h:h + 1], in1=rs_[:], op0=ALU.mult, op1=ALU.mult),
                                    reads=[o_h.r, gn.r, rs_.r], writes=[rs_.r])
                            yield
                        pg = ps()
                        fm_proj(pg, wg, h * 128, 128, t0)
                        kb.emit('act', lambda e, pg=pg: e.activation(out=gg[:], in_=pg[:], func=AF.Silu), reads=[pg.r], writes=[gg.r])
                        yield
                        kb.emit('dve', lambda e, h=h, n_=n_: e.tensor_tensor(out=yg[n_][:, h, t0:t0 + TT], in0=rs_[:], in1=gg[:], op=ALU.mult),
                                reads=[rs_.r, gg.r], writes=[yg[n_].r])
                        yield

                    for h0 in (0, 2):
                        _interleave([_nrm(h0), _nrm(h0 + 1)])
            for j in range(KC):
                wgt = wtile()
                for n in range(4):
                    c = AUGOFF['merge%d' % (2 * n + j // 4)] + (j % 4) * 128
                    load_w(wgt, Wb[l][:, c:c + 128], 128, col0=n * 128, batch=(n > 0))
                wbr = wtile()
                for n in range(4):
                    load_w(wbr, WBR[l][n * 512:(n + 1) * 512, j * 128:(j + 1) * 128], 128, col0=n * 128, kc=4, batch=(n > 0))
                for tt in range(NT):
                    t0 = tt * TT
                    acc = WK[6]
                    for n in range(4):
                        gt = WK[7] if n % 2 == 0 else WK[8]
                        pg = ps()
                        fm_proj(pg, wgt, n * 128, 128, t0)
                        pb = ps()
                        for k in range(4):
                            kb.emit('pe', lambda e, k=k, n=n, pb=pb, t0=t0: e.matmul(pb[:], lhsT=wbr[:, k, n * 128:(n + 1) * 128],
                                                                                      rhs=yg[n][:, k, t0:t0 + TT], start=(k == 0), stop=(k == 3)),
                                    reads=[wbr.r, yg[n].r], writes=[pb.r])
                        kb.emit('act', lambda e, pg=pg: e.activation(out=gt[:], in_=pg[:], func=AF.Sigmoid), reads=[pg.r], writes=[gt.r])
                        if n == 0:
                            kb.emit('dve', lambda e, pb=pb: e.tensor_tensor(out=acc[:], in0=gt[:], in1=pb[:], op=ALU.mult),
                                    reads=[gt.r, pb.r], writes=[acc.r])
                        else:
                            kb.emit('dve', lambda e, pb=pb: e.tensor_tensor(out=gt[:], in0=gt[:], in1=pb[:], op=ALU.mult),
                                    reads=[gt.r, pb.r], writes=[gt.r])
                            if n < 3:
                                kb.emit('dve', lambda e: e.tensor_tensor(out=acc[:], in0=acc[:], in1=gt[:], op=ALU.add),
                                        reads=[acc.r, gt.r], writes=[acc.r])
                            else:
                                kb.emit('dve', lambda e, j=j, t0=t0: e.tensor_tensor(out=merged[:, j, t0:t0 + TT], in0=acc[:], in1=gt[:], op=ALU.add),
                                        reads=[acc.r, gt.r], writes=[merged.r])
            for j in range(KC):
                w = wtile()
                load_w(w, WO[l][:, j * 128:(j + 1) * 128], 128)
                for tt in range(NT):
                    t0 = tt * TT
                    p = ps()
                    for k in range(KC):
                        kb.emit('pe', lambda e, k=k, p=p, w=w, t0=t0: e.matmul(p[:], lhsT=w[:, k, 0:128], rhs=merged[:, k, t0:t0 + TT],
                                                                               start=(k == 0), stop=(k == KC - 1)),
                                reads=[w.r, merged.r], writes=[p.r])
                    kb.emit('dve', lambda e, p=p, j=j, t0=t0: e.tensor_tensor(out=hseg[:, j, t0:t0 + TT], in0=hseg[:, j, t0:t0 + TT], in1=p[:], op=ALU.add),
                            reads=[p.r, hseg.r], writes=[hseg.r])

            if XA:
                for tt in range(NT):
                    hv = _View(hseg, slice(tt * TT, (tt + 1) * TT))
                    rmsnorm_fm(hv, xng, lambda k, tt=tt: un[:, k, 3 + tt * TT:3 + (tt + 1) * TT], un.r)
                for j in range(KC):
                    w = wtile()
                    load_w(w, WQ[l][:, j * 128:(j + 1) * 128], 128)
                    for tt in range(NT):
                        t0 = tt * TT
                        p = ps()
                        fm_proj(p, w, 0, 128, t0)
                        kb.emit('act', lambda e, p=p, j=j, t0=t0: e.activation(out=merged[:, j, t0:t0 + TT], in_=p[:], func=AF.Copy),
                                reads=[p.r], writes=[merged.r])
                for tt in range(NT):
                    t0 = tt * TT
                    for hh_ in range(4):
                        pT = [WB16[0], WB16[1]] if hh_ % 2 == 0 else [WB16[2], WB16[3]]
                        psum_ = ps()
                        for mb in range(2):
                            p = ps()
                            for k2 in range(2):
                                kb.emit('pe', lambda e, p=p, k2=k2, mb=mb, hh_=hh_, t0=t0: e.matmul(
                                    p[:], lhsT=kT[:, hh_ * 2 + k2, mb * 128:(mb + 1) * 128], rhs=merged[:, hh_ * 2 + k2, t0:t0 + TT],
                                    start=(k2 == 0), stop=(k2 == 1)), reads=[kT.r, merged.r], writes=[p.r])
                            kb.emit('act', lambda e, p=p, mb=mb: e.activation(out=pT[mb][:], in_=p[:], func=AF.Exp, scale=1.0 / 16.0),
                                    reads=[p.r], writes=[pT[mb].r])
                        for mb in range(2):
                            kb.emit('pe', lambda e, mb=mb, psum_=psum_: e.matmul(psum_[:], lhsT=ones_b[:], rhs=pT[mb][:], start=(mb == 0), stop=(mb == 1)),
                                    reads=[ones_b.r, pT[mb].r], writes=[psum_.r])
                        rs = WK[8] if hh_ % 2 == 0 else WK[9]
                        kb.emit('act', lambda e, psum_=psum_, rs=rs: e.activation(out=rs[:], in_=psum_[:], func=AF.Ln), reads=[psum_.r], writes=[rs.r])
                        kb.emit('act', lambda e, rs=rs: e.activation(out=rs[:], in_=rs[:], func=AF.Exp, scale=-1.0), reads=[rs.r], writes=[rs.r])
                        for dh in range(2):
                            po = ps()
                            for mb in range(2):
                                kb.emit('pe', lambda e, po=po, mb=mb, dh=dh, hh_=hh_: e.matmul(
                                    po[:], lhsT=vtm[:, mb, hh_ * 256 + dh * 128:hh_ * 256 + (dh + 1) * 128], rhs=pT[mb][:],
                                    start=(mb == 0), stop=(mb == 1)), reads=[vtm.r, pT[mb].r], writes=[po.r])
                            dst = yg[hh_ // 2]
                            kb.emit('dve', lambda e, po=po, dst=dst, hh_=hh_, dh=dh, t0=t0: e.tensor_tensor(
                                out=dst[:, (hh_ % 2) * 2 + dh, t0:t0 + TT], in0=po[:], in1=rs[:], op=ALU.mult),
                                    reads=[po.r, rs.r], writes=[dst.r])
                for j in range(KC):
                    w = wtile()
                    load_w(w, WXO[l][:, j * 128:(j + 1) * 128], 128)
                    for tt in range(NT):
                        t0 = tt * TT
                        p = ps()
                        for k in range(KC):
                            src = yg[k // 4]
                            kb.emit('pe', lambda e, k=k, p=p, w=w, t0=t0, src=src: e.matmul(p[:], lhsT=w[:, k, 0:128], rhs=src[:, k % 4, t0:t0 + TT],
                                                                                            start=(k == 0), stop=(k == KC - 1)),
                                    reads=[w.r, src.r], writes=[p.r])
                        kb.emit('dve', lambda e, p=p, j=j, t0=t0: e.tensor_tensor(out=hseg[:, j, t0:t0 + TT], in0=hseg[:, j, t0:t0 + TT], in1=p[:], op=ALU.add),
                                reads=[p.r, hseg.r], writes=[hseg.r])
            kb.dma('pool', hT_v[:, :, s0:s0 + SEG], hseg[:], hseg.r, reads=[hseg.r], writes=[hT_r])

    kb.new_scope()
    _tl.clear()
    hs_t = [T(kb, "hs%d" % i, [128, KC, TT], F32) for i in range(2)]
    yn = T(kb, "yn", [128, KC, TT], F32)
    otm = [T(kb, "otm%d" % i, [128, D], F32) for i in range(2)]
    oi = 0
    for tt in range(S // TT):
        hs = hs_t[tt % 2]
        kb.dma('sp', hs[:], hT_v[:, :, tt * TT:(tt + 1) * TT], hs.r, reads=[hT_r], writes=[hs.r])
        rmsnorm_fm(hs, fng, lambda k: yn[:, k, :], yn.r)
        for tb in range(TT // 128):
            ot = otm[oi % 2]
            oi += 1
            for half in range(2):
                p = ps()
                for kk in range(4):
                    k = half * 4 + kk
                    kb.emit('pe', lambda e, p=p, kk=kk, k=k, tb=tb: e.transpose(
                        out=p[:, kk * 128:(kk + 1) * 128], in_=yn[:, k, tb * 128:(tb + 1) * 128], identity=ident[:]),
                            reads=[yn.r, ident.r], writes=[p.r])
                if half:
                    kb.emit('act', lambda e, p=p, ot=ot: e.activation(out=ot[:, 512:1024], in_=p[:], func=AF.Copy),
                            reads=[p.r], writes=[ot.r])
                else:
                    kb.emit('dve', lambda e, p=p, ot=ot: e.tensor_copy(out=ot[:, 0:512], in_=p[:]),
                            reads=[p.r], writes=[ot.r])
            r0 = tt * TT + tb * 128
            kb.dma('pool', out[r0:r0 + 128, :], ot[:], ot.r, reads=[ot.r])
    kb.wait_all('pool', [t.r for t in otm])
    kb.new_scope()
    kb.scope.close()
    build.ninst = kb.ninst
    return nc, es


PARAM_NAMES = ('mix_norm_g', 'w_in', 'rw_mu', 'rw_w0', 'rw_w2', 'rw_a0', 'rw_a2', 'rw_k_k', 'rw_k_a', 'rw_r_k', 'rw_ln_g',
               'rw_ln_b', 'gla_f_up', 'gla_f_b', 'gla_norm_g', 'ret_gn_g', 'lru_conv_w', 'lru_conv_b', 'lru_wa', 'lru_ba',
               'lru_wx', 'lru_bx', 'lru_lambda', 'w_branch', 'w_out', 'xa_norm_g', 'xa_mem_norm_g', 'xa_wq', 'xa_wkv',
               'xa_wo', 'final_norm_g')


def core_inputs(inputs, b, S):
    m = {"x": np.ascontiguousarray(inputs['x'][b, :S]), "mem": np.ascontiguousarray(inputs['mem'][b]),
         "positions": np.ascontiguousarray(inputs['positions'][b, :S]).astype(np.int32)}
    for n in PARAM_NAMES:
        m[n] = np.ascontiguousarray(inputs[n])
    return m


def kernel(**inputs):
    S = inputs['x'].shape[1]
    nc, es = build(S, 2)
    in_maps = [core_inputs(inputs, c % 2, S) for c in range(8)]
    res = run_bass_kernel_spmd(nc, in_maps, core_ids=list(range(8)))
    es.close()
    return np.stack([res.results[0]["out"], res.results[1]["out"]], axis=0)
```

```python
import numpy as np
from contextlib import ExitStack
import concourse.bass as bass
import concourse.mybir as mybir
from concourse.bass_utils import run_bass_kernel_spmd

F32 = mybir.dt.float32
BF16 = mybir.dt.bfloat16
I32 = mybir.dt.int32
ALU = mybir.AluOpType
AF = mybir.ActivationFunctionType
AX = mybir.AxisListType

D = 1024
KC = 8
NMEM = 256
DBR = 512
NORM_EPS = 1e-6
RW_LN_EPS = 64e-5
RET_G = [1.0 - 2.0 ** (-5.0 - h) for h in range(4)]

_src = {}
_o = 0
for _n, _w in (('rw_r', 512), ('rw_k', 512), ('rw_v', 512), ('rw_wlo', 64), ('rw_alo', 64), ('rw_g', 512),
               ('gla_q', 256), ('gla_k', 256), ('gla_v', 512), ('gla_flo', 16), ('gla_g', 512),
               ('ret_q', 256), ('ret_k', 256), ('ret_v', 512), ('ret_g', 512),
               ('lru_x', 512), ('lru_g', 512), ('merge', 4096)):
    _src[_n] = (_o, _w)
    _o += _w
N_IN = _o
MU_OFF = {'rw_r': 0, 'rw_k': 512, 'rw_v': 1024, 'rw_wlo': 1536, 'rw_alo': 1600}
AUG = []
for _n in ('rw_r', 'rw_k', 'rw_v', 'rw_wlo', 'rw_alo'):
    AUG.append((_n + '_c', _src[_n][1], _src[_n][0], 'mu1m', MU_OFF[_n]))
    AUG.append((_n + '_p', _src[_n][1], _src[_n][0], 'mu', MU_OFF[_n]))
for _j in range(4):
    AUG.append(('lru_x%d' % _j, 512, _src['lru_x'][0], 'conv', _j))
for _n in ('rw_g', 'gla_q', 'gla_k', 'gla_v', 'gla_flo', 'gla_g', 'ret_q', 'ret_k', 'ret_v', 'ret_g', 'lru_g'):
    AUG.append((_n, _src[_n][1], _src[_n][0], 'plain', 0))
AUG.append(('ret_qs', 256, _src['ret_q'][0], 'swap', 0))
AUG.append(('ret_ks', 256, _src['ret_k'][0], 'swap', 0))
for _j in range(8):
    AUG.append(('merge%d' % _j, 512, _src['merge'][0] + 512 * _j, 'plain', 0))
AUGOFF = {}
_o = 0
for _a in AUG:
    AUGOFF[_a[0]] = _o
    _o += _a[1]
NAUG = _o


class Res:
    __slots__ = ('w', 'r', 'ds')

    def __init__(self):
        self.w = None
        self.r = {}
        self.ds = None


class KB:
    def __init__(self, nc, es):
        self.nc = nc
        self.es = es
        self.eng = dict(pe=nc.tensor, dve=nc.vector, act=nc.scalar, pool=nc.gpsimd, sp=nc.sync)
        self.semh = {}
        self.cnt = {}
        for e in self.eng:
            self.semh[e] = es.enter_context(nc.semaphore('c_' + e))
            self.cnt[e] = 0
        self.seen = {e: {} for e in self.eng}
        self.nd = 0
        self.strict = True
        self.ninst = 0
        self.scope = es

    def new_scope(self):
        keys = list(self.cnt.keys())
        for e in self.eng:
            need = {k: (self.cnt[k], self.cnt[k]) for k in keys if k != e and self.cnt[k] > 0}
            self._waits(e, need)
        for e in self.eng:
            self.emit(e, lambda en: en.engine_nop() if hasattr(en, 'engine_nop') else en.nop())
        for e in self.eng:
            need = {k: (self.cnt[k], self.cnt[k]) for k in self.eng if k != e}
            self._waits(e, need)
        if self.scope is not self.es:
            self.scope.close()
        self.scope = ExitStack()

    def _need(self, reads, writes):
        need = {}

        def add(k, v, raw):
            a, b = need.get(k, (0, 0))
            need[k] = (max(a, v), max(b, v) if raw else b)
        for r in reads:
            if r.w is not None:
                add(r.w[0], r.w[1], True)
        for w in writes:
            if w.w is not None:
                add(w.w[0], w.w[1], False)
            for k, v in w.r.items():
                add(k, v, False)
        return need

    def _waits(self, e, need):
        eng = self.eng[e]
        seen = self.seen[e]
        for k, vv in need.items():
            v, vraw = vv if isinstance(vv, tuple) else (vv, vv)
            if k == e:
                if e == 'pe' or not self.strict:
                    continue
                pass
            if k[0] == 'd' and k[1:].isdigit():
                v = self.cnt[k]
            if seen.get(k, 0) >= v:
                continue
            eng.wait_ge(self.semh[k], v)
            seen[k] = v
            self.ninst += 1

    def emit(self, e, fn, reads=(), writes=()):
        self._waits(e, self._need(reads, writes))
        ins = fn(self.eng[e])
        self.cnt[e] += 1
        ins.then_inc(self.semh[e], 1)
        t = (e, self.cnt[e])
        for r in reads:
            if r.r.get(e, 0) < t[1]:
                r.r[e] = t[1]
        for w in writes:
            w.w = t
            w.r = {}
        self.ninst += 1
        return ins

    def _dsem(self, res, q):
        if res.ds is None:
            res.ds = {}
        if q not in res.ds:
            k = 'd%d' % self.nd
            self.nd += 1
            self.semh[k] = self.es.enter_context(self.nc.semaphore(k))
            self.cnt[k] = 0
            res.ds[q] = k
        return res.ds[q]

    def dma(self, q, out, in_, sb, reads=(), writes=(), batch=False):
        k = self._dsem(sb, q)
        need = self._need(reads, writes)
        if not batch and self.cnt[k] > 0:
            need[k] = (self.cnt[k], self.cnt[k])
        self._waits(q, need)
        ins = self.eng[q].dma_start(out=out, in_=in_)
        self.cnt[k] += 16
        ins.then_inc(self.semh[k], 16)
        t = (k, self.cnt[k])
        for r in reads:
            if r.r.get(k, 0) < t[1]:
                r.r[k] = t[1]
        for w in writes:
            w.w = t
            w.r = {}
        self.ninst += 1
        return ins

    def wait_all(self, e, ress):
        need = {}
        for r in ress:
            if r.w is not None:
                k, v = r.w
                need[k] = max(need.get(k, 0), v)
            for k, v in r.r.items():
                need[k] = max(need.get(k, 0), v)
        self._waits(e, {k: (v, v) for k, v in need.items()})


class T:
    def __init__(self, kb, name, shape, dt, psum=False):
        nc = kb.nc
        if psum:
            self.t = kb.es.enter_context(nc.psum_tensor(name, shape, dt))
        else:
            self.t = kb.scope.enter_context(nc.sbuf_tensor(name, shape, dt))
        self.r = Res()

    def __getitem__(self, k):
        return self.t[k]


class _View3:
    def __init__(self, t):
        self.t, self.r = t, t.r

    def __getitem__(self, k):
        if k == slice(None):
            return self.t[:, 0:2, :]
        return self.t[k]


class _ViewC:
    def __init__(self, t, fn):
        self.t, self.r, self.v = t, t.r, fn(t)

    def __getitem__(self, k):
        return self.v[k]


class _View:
    def __init__(self, t, sl):
        self.t, self.sl, self.r = t, sl, t.r

    def __getitem__(self, k):
        a, b, c = k
        assert c == slice(None)
        return self.t[a, b, self.sl]


def build(S, DEPTH, SEG=512, TT=512, MIX=('rw', 'gla', 'ret', 'lru'), XA=True):
    nc = bass.Bass("TRN2", target_bir_lowering=False)
    es = ExitStack()
    es.enter_context(nc.allow_non_contiguous_dma(reason="small per-channel vectors / strided layouts"))
    kb = KB(nc, es)
    NSEG = S // SEG
    NT = SEG // TT
    L = DEPTH

    def din(name, shape, dt=F32):
        return nc.dram_tensor(name, list(shape), dt, kind="ExternalInput").ap()

    x = din("x", [S, D])
    out = nc.dram_tensor("out", [S, D], F32, kind="ExternalOutput").ap()
    final_norm_g = din("final_norm_g", [D])
    hT = nc.dram_tensor("hT", [D, S], F32).ap()
    hT_r = Res()

    ident = T(kb, "ident", [128, 128], F32)
    ones_f = T(kb, "ones_f", [128, 128], F32)
    kb.emit('pool', lambda e: e.memset(ones_f[:], 1.0), writes=[ones_f.r])
    kb.emit('pool', lambda e: e.memset(ident[:], 1.0), writes=[ident.r])
    kb.emit('pool', lambda e: e.affine_select(out=ident[:], in_=ident[:], pattern=[[1, 128]], base=0,
                                              channel_multiplier=-1, compare_op=ALU.is_equal, fill=0.0),
            reads=[ident.r], writes=[ident.r])

    eps_t = T(kb, "eps_t", [128, 4], F32)
    kb.emit('pool', lambda e: e.memset(eps_t[:, 0:1], NORM_EPS), writes=[eps_t.r])
    kb.emit('pool', lambda e: e.memset(eps_t[:, 1:2], RW_LN_EPS), writes=[eps_t.r])
    kb.emit('pool', lambda e: e.memset(eps_t[:, 2:3], 1.0), writes=[eps_t.r])
    kb.emit('pool', lambda e: e.memset(eps_t[:, 3:4], 0.0), writes=[eps_t.r])
    ones_b = T(kb, "ones_b", [128, 128], BF16)
    kb.emit('pool', lambda e: e.memset(ones_b[:], 1.0), writes=[ones_b.r])
    sqs = [T(kb, "sq%d" % i, [128, TT], BF16) for i in range(2)]
    rstd = T(kb, "rstd", [128, TT], F32)
    PS = [T(kb, "ps%d" % i, [128, 512], F32, psum=True) for i in range(8)]
    kb.PS = PS
    psi = [0]

    def ps():
        p = PS[psi[0] % 8]
        psi[0] += 1
        return p

    _tl = {}

    def TL(name, shape, dt):
        if name not in _tl:
            _tl[name] = T(kb, name, shape, dt)
        return _tl[name]

    def load_vec_fm(name, ap1d, n=D):
        t = TL(name, [128, n // 128], F32)
        kb.dma('sp', t[:], ap1d.rearrange("(k p) -> p k", p=128), t.r, writes=[t.r])
        return t

    fng = load_vec_fm("fng", final_norm_g)

    kb.new_scope()
    xin = [T(kb, "xin%d" % i, [128, D], F32) for i in range(2)]
    xtr = [T(kb, "xtr%d" % i, [128, KC, 128], F32) for i in range(2)]
    hT_v = hT.rearrange("(k p) s -> p k s", p=128)
    for tb in range(S // 128):
        xi = xin[tb % 2]
        xo = xtr[tb % 2]
        kb.dma('sp', xi[:], x[tb * 128:(tb + 1) * 128, :], xi.r, writes=[xi.r])
        for half in range(2):
            p = ps()
            for kk in range(4):
                k = half * 4 + kk
                kb.emit('pe', lambda e, p=p, kk=kk, k=k: e.transpose(out=p[:, kk * 128:(kk + 1) * 128],
                                                                     in_=xi[:, k * 128:(k + 1) * 128], identity=ident[:]),
                        reads=[xi.r, ident.r], writes=[p.r])
            eng = 'act' if half else 'dve'
            if eng == 'act':
                kb.emit('act', lambda e, p=p, half=half: e.activation(
                    out=xo[:, half * 4:(half + 1) * 4, :].rearrange("p k s -> p (k s)"), in_=p[:], func=AF.Copy),
                        reads=[p.r], writes=[xo.r])
            else:
                kb.emit('dve', lambda e, p=p, half=half: e.tensor_copy(
                    out=xo[:, half * 4:(half + 1) * 4, :].rearrange("p k s -> p (k s)"), in_=p[:]),
                        reads=[p.r], writes=[xo.r])
        kb.dma('pool', hT_v[:, :, tb * 128:(tb + 1) * 128], xo[:], xo.r, reads=[xo.r], writes=[hT_r])


    def rmsnorm_fm(hs, g, dst_fn, dst_res):
        p = ps()
        for k in range(KC):
            sq = sqs[k % 2]
            kb.emit('act', lambda e, k=k, sq=sq: e.activation(out=sq[:], in_=hs[:, k, :], func=AF.Square),
                    reads=[hs.r], writes=[sq.r])
            kb.emit('pe', lambda e, k=k, p=p, sq=sq: e.matmul(p[:], lhsT=ones_b[:], rhs=sq[:], start=(k == 0), stop=(k == KC - 1)),
                    reads=[sq.r, ones_b.r], writes=[p.r])
        kb.emit('act', lambda e, p=p: e.activation(out=rstd[:], in_=p[:], func=AF.Ln, scale=1.0 / D, bias=eps_t[:, 0:1]),
                reads=[p.r, eps_t.r], writes=[rstd.r])
        kb.emit('act', lambda e: e.activation(out=rstd[:], in_=rstd[:], func=AF.Exp, scale=-0.5),
                reads=[rstd.r], writes=[rstd.r])
        for k in range(KC):
            kb.emit('dve', lambda e, k=k: e.scalar_tensor_tensor(out=dst_fn(k), in0=hs[:, k, :], scalar=g[:, k:k + 1],
                                                                 in1=rstd[:], op0=ALU.mult, op1=ALU.mult),
                    reads=[hs.r, g.r, rstd.r], writes=[dst_res])

    kb.new_scope()
    mem = din("mem", [NMEM, D])
    positions = din("positions", [S], I32)
    P = {}
    for nm, shp in (('mix_norm_g', [2, D]), ('w_in', [2, D, N_IN]), ('rw_mu', [2, 1664]), ('rw_w0', [2, 512]),
                    ('rw_w2', [2, 64, 512]), ('rw_a0', [2, 512]), ('rw_a2', [2, 64, 512]), ('rw_k_k', [2, 512]),
                    ('rw_k_a', [2, 512]), ('rw_r_k', [2, 8, 64]), ('rw_ln_g', [2, 512]), ('rw_ln_b', [2, 512]),
                    ('gla_f_up', [2, 16, 256]), ('gla_f_b', [2, 256]), ('gla_norm_g', [2, 512]), ('ret_gn_g', [2, 512]),
                    ('lru_conv_w', [2, 4, 512]), ('lru_conv_b', [2, 512]), ('lru_wa', [2, 8, 64, 64]), ('lru_ba', [2, 512]),
                    ('lru_wx', [2, 8, 64, 64]), ('lru_bx', [2, 512]), ('lru_lambda', [2, 512]),
                    ('w_branch', [2, 4, 512, D]), ('w_out', [2, D, D]), ('xa_norm_g', [2, D]), ('xa_mem_norm_g', [2, D]),
                    ('xa_wq', [2, D, D]), ('xa_wkv', [2, D, 2 * D]), ('xa_wo', [2, D, D])):
        P[nm] = din(nm, shp)

    def dscr(name, shape, dt=BF16):
        return nc.dram_tensor(name, list(shape), dt).ap()

    Wb = [dscr("Wb%d" % l, [D, NAUG]) for l in range(L)]
    WBR = [dscr("WBR%d" % l, [4 * 512, D]) for l in range(L)]
    WO = [dscr("WO%d" % l, [D, D]) for l in range(L)]
    WQ = [dscr("WQ%d" % l, [D, D]) for l in range(L)]
    WKV = [dscr("WKV%d" % l, [D, 2 * D]) for l in range(L)]
    WXO = [dscr("WXO%d" % l, [D, D]) for l in range(L)]
    wdram_r = Res()

    PW = 2048
    wld = [T(kb, "wld%d" % i, [128, PW], F32) for i in range(2)]
    wst = [T(kb, "wst%d" % i, [128, PW], BF16) for i in range(2)]
    scl = T(kb, "scl", [128, 512], F32)
    pi = [0]

    def prep(src, dst, rows, n, scale=None, swap=False):
        for rc in range(rows // 128):
            a = wld[pi[0] % 2]
            b = wst[pi[0] % 2]
            pi[0] += 1
            kb.dma('sp', a[:, 0:n], src[rc * 128:(rc + 1) * 128, :], a.r, writes=[a.r])
            if swap:
                av = a[:, 0:n].rearrange("p (h two d) -> p h two d", two=2, d=32)
                bv = b[:, 0:n].rearrange("p (h two d) -> p h two d", two=2, d=32)
                kb.emit('dve', lambda e: e.tensor_scalar(out=bv[:, :, 0, :], in0=av[:, :, 1, :], scalar1=-1.0, scalar2=None, op0=ALU.mult),
                        reads=[a.r], writes=[b.r])
                kb.emit('dve', lambda e: e.tensor_copy(out=bv[:, :, 1, :], in_=av[:, :, 0, :]), reads=[a.r, b.r], writes=[b.r])
            elif scale is None:
                if pi[0] % 2:
                    kb.emit('act', lambda e: e.activation(out=b[:, 0:n], in_=a[:, 0:n], func=AF.Copy),
                            reads=[a.r], writes=[b.r])
                else:
                    kb.emit('dve', lambda e: e.tensor_copy(out=b[:, 0:n], in_=a[:, 0:n]), reads=[a.r], writes=[b.r])
            else:
                kb.emit('dve', lambda e: e.tensor_tensor(out=b[:, 0:n], in0=a[:, 0:n], in1=scale[:, 0:n], op=ALU.mult),
                        reads=[a.r, scale.r], writes=[b.r])
            kb.dma('pool', dst[rc * 128:(rc + 1) * 128, :], b[:, 0:n], b.r, reads=[b.r], writes=[wdram_r])

    memt = T(kb, "memt", [128, 2, D], F32)
    memT32 = T(kb, "memT32", [128, KC, NMEM], F32)
    memT_d = nc.dram_tensor("memT_d", [D, NMEM], F32).ap()
    memT_r = Res()
    kb.dma('sp', memt[:], mem.rearrange("(b p) d -> p b d", p=128), memt.r, writes=[memt.r])
    for bk in range(2):
        for half in range(2):
            p = ps()
            for kk in range(4):
                k = half * 4 + kk
                kb.emit('pe', lambda e, p=p, kk=kk, k=k, bk=bk: e.transpose(out=p[:, kk * 128:(kk + 1) * 128],
                                                                            in_=memt[:, bk, k * 128:(k + 1) * 128], identity=ident[:]),
                        reads=[memt.r, ident.r], writes=[p.r])
            kb.emit('dve', lambda e, p=p, half=half, bk=bk: e.tensor_copy(
                out=memT32[:, half * 4:(half + 1) * 4, bk * 128:(bk + 1) * 128],
                in_=p[:].rearrange("p (k s) -> p k s", k=4)), reads=[p.r], writes=[memT32.r])


    kb.dma('pool', memT_d.rearrange("(k p) m -> p k m", p=128), memT32[:], memT32.r, reads=[memT32.r], writes=[memT_r])
    for l in range(L):
        for (an, w, so, kind, arg) in AUG:
            src = P['w_in'][l, :, so:so + w]
            dst = Wb[l][:, AUGOFF[an]:AUGOFF[an] + w]
            if kind == 'plain':
                prep(src, dst, D, w)
            elif kind == 'swap':
                prep(src, dst, D, w, swap=True)
            else:
                if kind in ('mu', 'mu1m'):
                    row = P['rw_mu'][l, arg:arg + w]
                else:
                    row = P['lru_conv_w'][l, arg, :]
                kb.dma('sp', scl[:, 0:w], row.partition_broadcast(128), scl.r, writes=[scl.r])
                if kind == 'mu1m':
                    kb.emit('dve', lambda e, w=w: e.tensor_scalar(out=scl[:, 0:w], in0=scl[:, 0:w], scalar1=-1.0, scalar2=1.0,
                                                                   op0=ALU.mult, op1=ALU.add), reads=[scl.r], writes=[scl.r])
                prep(src, dst, D, w, scale=scl)
        prep(P['w_branch'][l].rearrange("n r c -> (n r) c"), WBR[l], 4 * 512, D)
        prep(P['w_out'][l], WO[l], D, D)
        prep(P['xa_wq'][l], WQ[l], D, D)
        prep(P['xa_wkv'][l], WKV[l], D, 2 * D)
        prep(P['xa_wo'][l], WXO[l], D, D)

    kb.new_scope()
    _tl.clear()
    kT = T(kb, "kT", [128, KC, NMEM], BF16)
    vtm = T(kb, "vtm", [128, 2, D], BF16)
    hseg = T(kb, "hseg", [128, KC, SEG], F32)
    un = T(kb, "un", [128, KC, 3 + SEG], BF16)
    halo = T(kb, "halo", [128, KC, 3], BF16)
    yg = [T(kb, "yg%d" % n, [128, 4, SEG], BF16) for n in range(4)]
    merged = T(kb, "merged", [128, KC, SEG], BF16)
    NW = 3
    NC8_ = TT // 64
    twb = T(kb, "twb", [64, TT], BF16)
    alb = T(kb, "alb", [64, TT], BF16)
    AR = T(kb, "AR", [128, NC8_, 128], BF16)
    BK32 = T(kb, "BK32", [128, 2, TT], F32)
    BKb = T(kb, "BKb", [128, 2, TT], BF16)
    VT = T(kb, "VT", [128, NC8_, 64], BF16)
    BTm = T(kb, "BTm", [128, NC8_, 64], BF16)
    KTm = T(kb, "KTm", [128, NC8_, 64], BF16)
    Mall = T(kb, "Mall", [128, NC8_, 320], BF16)
    AN = [T(kb, "AN%d" % i, [128, NC8_, 128], BF16) for i in range(2)]
    TTb = T(kb, "TTb", [128, NC8_, 64], BF16)
    vb = T(kb, "vb", [128, TT], BF16)
    Xb = T(kb, "Xb", [128, 64], BF16)
    Ub = T(kb, "Ub", [128, 64], BF16)
    gC = T(kb, "gC", [128, NC8_], F32)
    def _interleave(gens):
        gens = [g for g in gens if g is not None]
        while gens:
            for g in list(gens):
                try:
                    next(g)
                except StopIteration:
                    gens.remove(g)
    identb = T(kb, "identb", [128, 1, 64], BF16)
    kb.emit('dve', lambda e: e.tensor_copy(out=identb[0:64, 0, :], in_=ident[0:64, 0:64]), reads=[ident.r], writes=[identb.r])
    kb.emit('dve', lambda e: e.tensor_copy(out=identb[64:128, 0, :], in_=ident[64:128, 64:128]), reads=[ident.r], writes=[identb.r])
    ones_bd = T(kb, "ones_bd", [128, 128], F32)
    kb.emit('pool', lambda e: e.memset(ones_bd[:], 0.0), writes=[ones_bd.r])
    kb.emit('pool', lambda e: e.memset(ones_bd[0:64, 0:64], 1.0), writes=[ones_bd.r])
    kb.emit('pool', lambda e: e.memset(ones_bd[64:128, 64:128], 1.0), writes=[ones_bd.r])
    maskR = T(kb, "maskR", [128, 320], F32)
    kb.emit('pool', lambda e: e.memset(maskR[:], 1.0), writes=[maskR.r])
    for hv_ in (slice(0, 64), slice(64, 128)):
        for (c0_, strict, transposed) in ((0, True, False), (64, False, False), (128, True, False), (192, False, False), (256, True, True)):
            kb.emit('pool', lambda e, c0_=c0_, strict=strict, transposed=transposed, hv_=hv_: e.affine_select(
                out=maskR[hv_, c0_:c0_ + 64], in_=maskR[hv_, c0_:c0_ + 64], pattern=[[-1 if transposed else 1, 64]], base=0,
                channel_multiplier=(1 if transposed else -1), compare_op=(ALU.is_gt if strict else ALU.is_ge), fill=0.0),
                    reads=[maskR.r], writes=[maskR.r])
    vT = [T(kb, "vT%d" % i, [128, 512], BF16) for i in range(TT // 128)]
    qd = T(kb, "qd", [64, 4, TT], BF16)
    kd = T(kb, "kd", [64, 4, TT], BF16)
    kd32 = T(kb, "kd32", [64, 4, TT], F32)
    kdT = T(kb, "kdT", [128, 256], BF16)
    attb = T(kb, "attb", [128, 512], BF16)
    oall = T(kb, "oall", [128, 4, TT], F32)
    ebl = T(kb, "ebl", [64, 4, TT // 128], F32)
    flo_b = T(kb, "flo_b", [16, TT], BF16)
    wflo = T(kb, "wflo", [128, KC, 16], BF16)
    posi = T(kb, "posi", [64, TT], I32)
    cosT = T(kb, "cosT", [128, TT], F32)
    sinT = T(kb, "sinT", [128, TT], F32)
    yT, bonus = cosT, sinT
    tmp2 = T(kb, "tmp2", [128, TT], F32)
    sgate = T(kb, "sgate", [128, TT], BF16)
    RWB = [(AR, BKb, VT, BTm, KTm, gC, sinT, sgate),
           (T(kb, "AR_b", [128, NC8_, 128], BF16), T(kb, "BKb_b", [128, 2, TT], BF16), T(kb, "VT_b", [128, NC8_, 64], BF16),
            T(kb, "BTm_b", [128, NC8_, 64], BF16), T(kb, "KTm_b", [128, NC8_, 64], BF16), T(kb, "gC_b", [128, NC8_], F32),
            T(kb, "bonus_b", [128, TT], F32), T(kb, "sgate_b", [128, TT], BF16))]

    mask4 = T(kb, "mask4", [128, 4, 128], F32)
    kb.emit('pool', lambda e: e.memset(mask4[:], 1.0), writes=[mask4.r])
    kb.emit('pool', lambda e: e.affine_select(out=mask4[:], in_=mask4[:], pattern=[[0, 4], [1, 128]], base=0, channel_multiplier=-1,
                                              compare_op=ALU.is_ge, fill=0.0), reads=[mask4.r], writes=[mask4.r])
    ebR = T(kb, "ebR", [64, 4, 128], F32)
    enbR = T(kb, "enbR", [64, 4, 128], F32)
    iot = T(kb, "iot", [64, 128], F32)
    invf = T(kb, "invf", [64, 1], F32)
    kb.emit('pool', lambda e: e.iota(iot[:], pattern=[[1, 128]], base=1, channel_multiplier=0, allow_small_or_imprecise_dtypes=True), writes=[iot.r])
    for h in range(4):
        lg = float(np.log(RET_G[h]))
        kb.emit('act', lambda e, h=h, lg=lg: e.activation(out=ebR[:, h, :], in_=iot[:], func=AF.Exp, scale=lg), reads=[iot.r], writes=[ebR.r])
        kb.emit('act', lambda e, h=h, lg=lg: e.activation(out=enbR[:, h, :], in_=iot[:], func=AF.Exp, scale=-lg), reads=[iot.r], writes=[enbR.r])
    kb.emit('dve', lambda e: e.tensor_scalar(out=enbR[:], in0=enbR[:], scalar1=0.125, scalar2=None, op0=ALU.mult), reads=[enbR.r], writes=[enbR.r])
    pidx = T(kb, "pidx", [64, 2], F32)
    kb.emit('pool', lambda e: e.iota(pidx[:, 0:1], pattern=[[0, 1]], base=0, channel_multiplier=1, allow_small_or_imprecise_dtypes=True), writes=[pidx.r])
    kb.emit('dve', lambda e: e.tensor_scalar(out=pidx[:, 1:2], in0=pidx[:, 0:1], scalar1=32.0, scalar2=-32.0, op0=ALU.is_ge, op1=ALU.mult), reads=[pidx.r], writes=[pidx.r])
    kb.emit('dve', lambda e: e.tensor_tensor(out=pidx[:, 0:1], in0=pidx[:, 0:1], in1=pidx[:, 1:2], op=ALU.add), reads=[pidx.r], writes=[pidx.r])
    kb.emit('act', lambda e: e.activation(out=invf[:], in_=pidx[:, 0:1], func=AF.Exp, scale=float(-np.log(10000.0) / 32.0)), reads=[pidx.r], writes=[invf.r])
    wt = [T(kb, "wt%d" % i, [128, KC, 512], BF16) for i in range(NW)]
    wi = [0]

    def wtile():
        t = wt[wi[0] % NW]
        wi[0] += 1
        return t

    def load_w(t, src2d, ncols, col0=0, kc=KC, batch=False):
        kb.dma('sp', t[:, 0:kc, col0:col0 + ncols], src2d.rearrange("(k p) c -> p k c", p=128), t.r,
               reads=[wdram_r], writes=[t.r], batch=batch)

    WK = [T(kb, "wk%d" % i, [128, TT], F32) for i in range(10)]
    WB16 = [T(kb, "wb%d" % i, [128, TT], BF16) for i in range(4)]

    def fm_proj(dst_ps, wtile_, c0, ncols, t0, shift=0, start=True, stop=True, kc=KC):
        for k in range(kc):
            kb.emit('pe', lambda e, k=k: e.matmul(dst_ps[0:ncols, :], lhsT=wtile_[:, k, c0:c0 + ncols],
                                                  rhs=un[:, k, 3 + t0 + shift:3 + t0 + shift + TT],
                                                  start=(start and k == 0), stop=(stop and k == kc - 1)),
                    reads=[wtile_.r, un.r], writes=[dst_ps.r])

    def rmsnorm_cols(src, g, dst, ncol):
        p = ps()
        for k in range(KC):
            sq = sqs[k % 2]
            kb.emit('act', lambda e, k=k, sq=sq: e.activation(out=sq[:, 0:ncol], in_=src[:, k, 0:ncol], func=AF.Square),
                    reads=[src.r], writes=[sq.r])
            kb.emit('pe', lambda e, k=k, p=p, sq=sq: e.matmul(p[:, 0:ncol], lhsT=ones_b[:], rhs=sq[:, 0:ncol], start=(k == 0), stop=(k == KC - 1)),
                    reads=[sq.r, ones_b.r], writes=[p.r])
        kb.emit('act', lambda e, p=p: e.activation(out=rstd[:, 0:ncol], in_=p[:, 0:ncol], func=AF.Ln, scale=1.0 / D, bias=eps_t[:, 0:1]),
                reads=[p.r, eps_t.r], writes=[rstd.r])
        kb.emit('act', lambda e: e.activation(out=rstd[:, 0:ncol], in_=rstd[:, 0:ncol], func=AF.Exp, scale=-0.5),
                reads=[rstd.r], writes=[rstd.r])
        for k in range(KC):
            kb.emit('dve', lambda e, k=k: e.scalar_tensor_tensor(out=dst[:, k, 0:ncol], in0=src[:, k, 0:ncol], scalar=g[:, k:k + 1],
                                                                 in1=rstd[:, 0:ncol], op0=ALU.mult, op1=ALU.mult),
                    reads=[src.r, g.r, rstd.r], writes=[dst.r])

    for l in range(L):
        mng = load_vec_fm("mng", P['mix_norm_g'][l])
        xng = load_vec_fm("xng", P['xa_norm_g'][l])
        xmg = load_vec_fm("xmg", P['xa_mem_norm_g'][l])
        mnT = _ViewC(merged, lambda t: t[:, :, 0:NMEM])
        mem32 = _ViewC(oall, lambda t: t[:].rearrange("p h (a t) -> p (h a) t", a=2))
        kb.dma('sp', mem32[:, :, :], memT_d.rearrange("(k p) m -> p k m", p=128), oall.r, reads=[memT_r], writes=[oall.r])
        rmsnorm_cols(mem32, xmg, mnT, NMEM)
        for j in range(KC):
            w = wtile()
            load_w(w, WKV[l][:, j * 128:(j + 1) * 128], 128)
            p = ps()
            for k in range(KC):
                kb.emit('pe', lambda e, k=k, p=p, w=w: e.matmul(p[:, 0:NMEM], lhsT=w[:, k, 0:128], rhs=mnT[:, k, :],
                                                                start=(k == 0), stop=(k == KC - 1)),
                        reads=[w.r, mnT.r], writes=[p.r])
            kb.emit('act', lambda e, p=p, j=j: e.activation(out=kT[:, j, :], in_=p[:, 0:NMEM], func=AF.Copy),
                    reads=[p.r], writes=[kT.r])
        for c2 in range(2):
            w = wtile()
            load_w(w, WKV[l][:, D + c2 * 512:D + (c2 + 1) * 512], 512)
            for bk in range(2):
                p = ps()
                for k in range(KC):
                    kb.emit('pe', lambda e, k=k, p=p, w=w, bk=bk: e.matmul(p[:], lhsT=mnT[:, k, bk * 128:(bk + 1) * 128], rhs=w[:, k, :],
                                                                           start=(k == 0), stop=(k == KC - 1)),
                            reads=[w.r, mnT.r], writes=[p.r])
                kb.emit('act', lambda e, p=p, bk=bk, c2=c2: e.activation(out=vtm[:, bk, c2 * 512:(c2 + 1) * 512], in_=p[:], func=AF.Copy),
                        reads=[p.r], writes=[vtm.r])

        if 'rw' in MIX:
            def hv(name, ap1d):
                t_ = TL(name, [128, 4], F32)
                kb.dma('sp', t_[:], ap1d.rearrange("(hp p) -> p hp", p=128), t_.r, writes=[t_.r])
                return t_
            w0t = hv("w0t", P['rw_w0'][l])
            a0t = hv("a0t", P['rw_a0'][l])
            kkt = hv("kkt", P['rw_k_k'][l])
            kat = hv("kat", P['rw_k_a'][l])
            rkt = hv("rkt", P['rw_r_k'][l].rearrange("h k -> (h k)"))
            lngt = hv("lngt", P['rw_ln_g'][l])
            lnbt = hv("lnbt", P['rw_ln_b'][l])
            w2b = TL("w2b", [64, 512], BF16)
            a2b = TL("a2b", [64, 512], BF16)
            for (nmw, dstb, stg) in (('rw_w2', w2b, WK[2]), ('rw_a2', a2b, WK[3])):
                kb.dma('sp', stg[0:64, :], P[nmw][l], stg.r, writes=[stg.r])
                kb.emit('dve', lambda e, dstb=dstb, stg=stg: e.tensor_copy(out=dstb[:], in_=stg[0:64, :]), reads=[stg.r], writes=[dstb.r])
            STs = [TL("ST_%d" % h_, [128, 64], BF16) for h_ in range(4)]
            for t_ in STs:
                kb.emit('pool', lambda e, t_=t_: e.memset(t_[:], 0.0), writes=[t_.r])
        if 'gla' in MIX or 'ret' in MIX:
            fup32 = TL("fup32", [16, 256], F32)
            fup_b = T(kb, "fup_b%d" % l, [16, 256], BF16)
            kb.dma('sp', fup32[:], P['gla_f_up'][l], fup32.r, writes=[fup32.r])
            kb.emit('dve', lambda e: e.tensor_copy(out=fup_b[:], in_=fup32[:]), reads=[fup32.r], writes=[fup_b.r])
            nfb = TL("nfb", [64, 4], F32)
            kb.dma('sp', nfb[:], P['gla_f_b'][l].rearrange("(h d) -> d h", d=64), nfb.r, writes=[nfb.r])
            kb.emit('dve', lambda e: e.tensor_scalar(out=nfb[:], in0=nfb[:], scalar1=-1.0, scalar2=None, op0=ALU.mult), reads=[nfb.r], writes=[nfb.r])
            gng = load_vec_fm("gng", P['gla_norm_g'][l], 512)
            rgg = load_vec_fm("rgg", P['ret_gn_g'][l], 512)
            Sg32 = TL("Sg32", [64, 4, 128], F32)
            Sgb = TL("Sgb", [64, 4, 128], BF16)
            Sr32 = TL("Sr32", [64, 4, 128], F32)
            Srb = TL("Srb", [64, 4, 128], BF16)
            for t_ in (Sg32, Sgb, Sr32, Srb):
                kb.emit('pool', lambda e, t_=t_: e.memset(t_[:], 0.0), writes=[t_.r])
        if 'lru' in MIX:
            lcb = load_vec_fm("lcb", P['lru_conv_b'][l], 512)
            lba = load_vec_fm("lba", P['lru_ba'][l], 512)
            lbx = load_vec_fm("lbx", P['lru_bx'][l], 512)
            llam = load_vec_fm("llam", P['lru_lambda'][l], 512)
            lc = TL("lc", [128, 4], F32)
            lc2 = TL("lc2", [128, 4], F32)
            kb.emit('act', lambda e: e.activation(out=lc[:], in_=llam[:], func=AF.Exp, scale=-1.0), reads=[llam.r], writes=[lc.r])
            kb.emit('act', lambda e: e.activation(out=lc[:], in_=lc[:], func=AF.Ln, bias=eps_t[:, 2:3]), reads=[lc.r, eps_t.r], writes=[lc.r])
            kb.emit('dve', lambda e: e.tensor_scalar(out=lc2[:], in0=lc[:], scalar1=-16.0, scalar2=None, op0=ALU.mult), reads=[lc.r], writes=[lc2.r])
            kb.emit('dve', lambda e: e.tensor_scalar(out=lc[:], in0=lc[:], scalar1=-8.0, scalar2=None, op0=ALU.mult), reads=[lc.r], writes=[lc.r])
            wab = TL("wab", [128, 2, 4, 128], BF16)
            for wi_, nmw in enumerate(('lru_wa', 'lru_wx')):
                stg = WK[wi_]
                sv = stg[:].rearrange("p (j o) -> p j o", j=4)
                kb.emit('pool', lambda e, stg=stg: e.memset(stg[:], 0.0), writes=[stg.r])
                for bk in range(8):
                    j, hb = bk // 2, bk % 2
                    kb.dma('sp', sv[hb * 64:(hb + 1) * 64, j, hb * 64:(hb + 1) * 64], P[nmw][l, bk], stg.r,
                           writes=[stg.r], batch=(bk > 0))
                kb.emit('dve', lambda e, sv=sv, wi_=wi_: e.tensor_copy(out=wab[:, wi_, :, :], in_=sv), reads=[stg.r], writes=[wab.r])
            lcarry = TL("lcarry", [128, 4], F32)
            kb.emit('pool', lambda e: e.memset(lcarry[:], 0.0), writes=[lcarry.r])

        for sg in range(NSEG):
            s0 = sg * SEG
            kb.dma('sp', hseg[:], hT_v[:, :, s0:s0 + SEG], hseg.r, reads=[hT_r], writes=[hseg.r])
            if sg == 0:
                kb.emit('dve', lambda e: e.memset(un[:, :, 0:3], 0.0), writes=[un.r])
            else:
                kb.emit('dve', lambda e: e.tensor_copy(out=un[:, :, 0:3], in_=halo[:]), reads=[halo.r], writes=[un.r])
            for tt in range(NT):
                hv = _View(hseg, slice(tt * TT, (tt + 1) * TT))
                rmsnorm_fm(hv, mng, lambda k, tt=tt: un[:, k, 3 + tt * TT:3 + (tt + 1) * TT], un.r)
            kb.emit('dve', lambda e: e.tensor_copy(out=halo[:], in_=un[:, :, SEG:SEG + 3]), reads=[un.r], writes=[halo.r])

            for n in range(4):
                if ('rw', 'gla', 'ret', 'lru')[n] not in MIX:
                    kb.emit('pool', lambda e, n=n: e.memset(yg[n][:], 0.0), writes=[yg[n].r])

            def LRUgen():
                for j in range(4):
                    w = wtile()
                    for v in range(4):
                        c = AUGOFF['lru_x%d' % v] + j * 128
                        load_w(w, Wb[l][:, c:c + 128], 128, col0=v * 128, batch=(v > 0))
                    wg = wtile()
                    c = AUGOFF['lru_g'] + j * 128
                    load_w(wg, Wb[l][:, c:c + 128], 128)
                    for tt in range(NT):
                        t0 = tt * TT
                        _o = 5 * ((j * NT + tt) % 2)
                        xc, xcb, rr, ii, aa, uu = WK[_o], WB16[(j * NT + tt) % 2], WK[_o + 1], WK[_o + 2], WK[_o + 3], WK[_o + 4]
                        hh = rr
                        p = ps()
                        for v in range(4):
                            fm_proj(p, w, v * 128, 128, t0, shift=v - 3, start=(v == 0), stop=(v == 3))
                        kb.emit('act', lambda e, p=p, j=j: e.activation(out=xc[:], in_=p[:], func=AF.Identity, bias=lcb[:, j:j + 1]),
                                reads=[p.r, lcb.r], writes=[xc.r])
                        yield
                        kb.emit('dve', lambda e: e.tensor_copy(out=xcb[:], in_=xc[:]), reads=[xc.r], writes=[xcb.r])
                        yield
                        p1 = ps()
                        kb.emit('pe', lambda e, p1=p1, j=j: e.matmul(p1[:], lhsT=wab[:, 0, j, :], rhs=xcb[:], start=True, stop=True),
                                reads=[wab.r, xcb.r], writes=[p1.r])
                        p2 = ps()
                        kb.emit('pe', lambda e, p2=p2, j=j: e.matmul(p2[:], lhsT=wab[:, 1, j, :], rhs=xcb[:], start=True, stop=True),
                                reads=[wab.r, xcb.r], writes=[p2.r])
                        kb.emit('act', lambda e, p1=p1, j=j: e.activation(out=rr[:], in_=p1[:], func=AF.Sigmoid, bias=lba[:, j:j + 1]),
                                reads=[p1.r, lba.r], writes=[rr.r])
                        yield
                        kb.emit('act', lambda e, p2=p2, j=j: e.activation(out=ii[:], in_=p2[:], func=AF.Sigmoid, bias=lbx[:, j:j + 1]),
                                reads=[p2.r, lbx.r], writes=[ii.r])
                        yield
                        kb.emit('act', lambda e, j=j: e.activation(out=aa[:], in_=rr[:], func=AF.Exp, scale=lc[:, j:j + 1]),
                                reads=[rr.r, lc.r], writes=[aa.r])
                        yield
                        kb.emit('act', lambda e, j=j: e.activation(out=uu[:], in_=rr[:], func=AF.Exp, scale=lc2[:, j:j + 1]),
                                reads=[rr.r, lc2.r], writes=[uu.r])
                        yield
                        kb.emit('act', lambda e: e.activation(out=uu[:], in_=uu[:], func=AF.Ln, scale=-1.0, bias=eps_t[:, 2:3]),
                                reads=[uu.r, eps_t.r], writes=[uu.r])
                        yield
                        kb.emit('act', lambda e: e.activation(out=uu[:], in_=uu[:], func=AF.Exp, scale=0.5), reads=[uu.r], writes=[uu.r])
                        yield
                        kb.emit('dve', lambda e: e.tensor_tensor(out=ii[:], in0=ii[:], in1=xc[:], op=ALU.mult), reads=[ii.r, xc.r], writes=[ii.r])
                        yield
                        kb.emit('dve', lambda e: e.tensor_tensor(out=uu[:], in0=uu[:], in1=ii[:], op=ALU.mult), reads=[uu.r, ii.r], writes=[uu.r])
                        yield
                        kb.emit('dve', lambda e, j=j: e.tensor_tensor_scan(out=hh[:], data0=aa[:], data1=uu[:], initial=lcarry[:, j:j + 1],
                                                                           op0=ALU.mult, op1=ALU.add),
                                reads=[aa.r, uu.r, lcarry.r], writes=[hh.r])
                        yield
                        kb.emit('act', lambda e, j=j: e.activation(out=lcarry[:, j:j + 1], in_=hh[:, TT - 1:TT], func=AF.Copy),
                                reads=[hh.r], writes=[lcarry.r])
                        yield
                        p3 = ps()
                        fm_proj(p3, wg, 0, 128, t0)
                        kb.emit('act', lambda e, p3=p3: e.activation(out=ii[:], in_=p3[:], func=AF.Silu), reads=[p3.r], writes=[ii.r])
                        yield
                        kb.emit('dve', lambda e, j=j, t0=t0: e.tensor_tensor(out=yg[3][:, j, t0:t0 + TT], in0=hh[:], in1=ii[:], op=ALU.mult),
                                reads=[hh.r, ii.r], writes=[yg[3].r])
                        yield


                return
                yield
            if 'rw' in MIX:
                NC8 = TT // 64
                C0 = float(np.exp(-0.5))
                HV = (slice(0, 64), slice(64, 128))
                for tt in range(NT):
                    t0 = tt * TT
                    wc = wtile()
                    for i_, nm_ in enumerate(('rw_wlo_c', 'rw_wlo_p', 'rw_alo_c', 'rw_alo_p')):
                        load_w(wc, Wb[l][:, AUGOFF[nm_]:AUGOFF[nm_] + 64], 64, col0=i_ * 64, batch=(i_ > 0))
                    p = ps()
                    fm_proj(p, wc, 0, 64, t0, start=True, stop=False)
                    fm_proj(p, wc, 64, 64, t0, shift=-1, start=False, stop=True)
                    kb.emit('act', lambda e, p=p: e.activation(out=twb[:], in_=p[0:64, :], func=AF.Tanh), reads=[p.r], writes=[twb.r])
                    p = ps()
                    fm_proj(p, wc, 128, 64, t0, start=True, stop=False)
                    fm_proj(p, wc, 192, 64, t0, shift=-1, start=False, stop=True)
                    kb.emit('act', lambda e, p=p: e.activation(out=alb[:], in_=p[0:64, :], func=AF.Copy), reads=[p.r], writes=[alb.r])
                    def P1(hp, B):
                        AR, BKb, VT, BTm, KTm, gC, bonus, sgate = B
                        wa_ = wtile()
                        for i_, nm_ in enumerate(('rw_r_c', 'rw_r_p', 'rw_k_c', 'rw_k_p')):
                            c = AUGOFF[nm_] + hp * 128
                            load_w(wa_, Wb[l][:, c:c + 128], 128, col0=i_ * 128, batch=(i_ > 0))
                        wb_ = wtile()
                        for i_, nm_ in enumerate(('rw_v_c', 'rw_v_p', 'rw_g')):
                            c = AUGOFF[nm_] + hp * 128
                            load_w(wb_, Wb[l][:, c:c + 128], 128, col0=i_ * 128, batch=(i_ > 0))
                        r32, k32, v32, sg, asg, kkn, kmod, cs, eG, tmp = [WK[i] for i in range(10)]

                        def proj2(wt_, ccur, cprev):
                            pp = ps()
                            fm_proj(pp, wt_, ccur, 128, t0, start=True, stop=False)
                            fm_proj(pp, wt_, cprev, 128, t0, shift=-1, start=False, stop=True)
                            return pp
                        pr = proj2(wa_, 0, 128)
                        kb.emit('act', lambda e, pr=pr: e.activation(out=r32[:], in_=pr[:], func=AF.Copy), reads=[pr.r], writes=[r32.r])
                        yield
                        pk = proj2(wa_, 256, 384)
                        kb.emit('act', lambda e, pk=pk: e.activation(out=k32[:], in_=pk[:], func=AF.Copy), reads=[pk.r], writes=[k32.r])
                        yield
                        pv = proj2(wb_, 0, 128)
                        kb.emit('act', lambda e, pv=pv: e.activation(out=v32[:], in_=pv[:], func=AF.Copy), reads=[pv.r], writes=[v32.r])
                        yield
                        pw = ps()
                        kb.emit('pe', lambda e, pw=pw, hp=hp: e.matmul(pw[:], lhsT=w2b[:, hp * 128:(hp + 1) * 128], rhs=twb[:], start=True, stop=True),
                                reads=[w2b.r, twb.r], writes=[pw.r])
                        kb.emit('act', lambda e, pw=pw, hp=hp: e.activation(out=sg[:], in_=pw[:], func=AF.Sigmoid, bias=w0t[:, hp:hp + 1]),
                                reads=[pw.r, w0t.r], writes=[sg.r])
                        yield
                        pa_ = ps()
                        kb.emit('pe', lambda e, pa_=pa_, hp=hp: e.matmul(pa_[:], lhsT=a2b[:, hp * 128:(hp + 1) * 128], rhs=alb[:], start=True, stop=True),
                                reads=[a2b.r, alb.r], writes=[pa_.r])
                        kb.emit('act', lambda e, pa_=pa_, hp=hp: e.activation(out=asg[:], in_=pa_[:], func=AF.Sigmoid, bias=a0t[:, hp:hp + 1]),
                                reads=[pa_.r, a0t.r], writes=[asg.r])
                        yield
                        kb.emit('dve', lambda e, hp=hp: e.tensor_scalar(out=kkn[:], in0=k32[:], scalar1=kkt[:, hp:hp + 1], scalar2=None, op0=ALU.mult),
                                reads=[k32.r, kkt.r], writes=[kkn.r])
                        yield
                        kb.emit('act', lambda e: e.activation(out=tmp[:], in_=kkn[:], func=AF.Square), reads=[kkn.r], writes=[tmp.r])
                        yield
                        pn = ps()
                        kb.emit('pe', lambda e, pn=pn: e.matmul(pn[:], lhsT=ones_bd[:], rhs=tmp[:], start=True, stop=True), reads=[ones_bd.r, tmp.r], writes=[pn.r])
                        kb.emit('act', lambda e, pn=pn: e.activation(out=tmp[:], in_=pn[:], func=AF.Ln, bias=eps_t[:, 3:4]), reads=[pn.r, eps_t.r], writes=[tmp.r])
                        yield
                        kb.emit('act', lambda e: e.activation(out=tmp[:], in_=tmp[:], func=AF.Exp, scale=-0.5), reads=[tmp.r], writes=[tmp.r])
                        yield
                        kb.emit('dve', lambda e: e.tensor_tensor(out=kkn[:], in0=kkn[:], in1=tmp[:], op=ALU.mult), reads=[kkn.r, tmp.r], writes=[kkn.r])
                        yield
                        kb.emit('dve', lambda e, hp=hp: e.tensor_scalar(out=tmp[:], in0=asg[:], scalar1=-1.0, scalar2=kat[:, hp:hp + 1], op0=ALU.add, op1=ALU.mult),
                                reads=[asg.r, kat.r], writes=[tmp.r])
                        yield
                        kb.emit('dve', lambda e: e.scalar_tensor_tensor(out=kmod[:], in0=tmp[:], scalar=1.0, in1=k32[:], op0=ALU.add, op1=ALU.mult),
                                reads=[tmp.r, k32.r], writes=[kmod.r])
                        yield
                        kb.emit('dve', lambda e, hp=hp: e.scalar_tensor_tensor(out=tmp[:], in0=r32[:], scalar=rkt[:, hp:hp + 1], in1=kmod[:], op0=ALU.mult, op1=ALU.mult),
                                reads=[r32.r, rkt.r, kmod.r], writes=[tmp.r])
                        yield
                        pbn = ps()
                        kb.emit('pe', lambda e, pbn=pbn: e.matmul(pbn[:], lhsT=ones_bd[:], rhs=tmp[:], start=True, stop=True), reads=[ones_bd.r, tmp.r], writes=[pbn.r])
                        kb.emit('dve', lambda e, pbn=pbn: e.tensor_tensor(out=bonus[:], in0=pbn[:], in1=v32[:], op=ALU.mult), reads=[pbn.r, v32.r], writes=[bonus.r])
                        yield
                        for c in range(NC8):
                            kb.emit('dve', lambda e, c=c: e.tensor_tensor_scan(out=cs[:, c * 64:(c + 1) * 64], data0=ones_f[:, 0:64], data1=sg[:, c * 64:(c + 1) * 64],
                                                                               initial=0.0, op0=ALU.mult, op1=ALU.add), reads=[sg.r, ones_f.r], writes=[cs.r])
                            yield
                        kb.emit('act', lambda e: e.activation(out=eG[:], in_=cs[:], func=AF.Exp, scale=-C0), reads=[cs.r], writes=[eG.r])
                        yield
                        kb.emit('act', lambda e: e.activation(out=gC[:], in_=eG[:].rearrange("p (c t) -> p c t", t=64)[:, :, 63], func=AF.Copy), reads=[eG.r], writes=[gC.r])
                        yield
                        kb.emit('dve', lambda e: e.tensor_tensor(out=AR[:, :, 64:128], in0=r32[:].rearrange("p (c t) -> p c t", t=64),
                                                                 in1=eG[:].rearrange("p (c t) -> p c t", t=64), op=ALU.mult), reads=[r32.r, eG.r], writes=[AR.r])
                        yield
                        kb.emit('dve', lambda e: e.tensor_tensor(out=tmp[:], in0=cs[:], in1=sg[:], op=ALU.subtract), reads=[cs.r, sg.r], writes=[tmp.r])
                        yield
                        kb.emit('act', lambda e: e.activation(out=tmp[:], in_=tmp[:], func=AF.Exp, scale=-C0), reads=[tmp.r], writes=[tmp.r])
                        yield
                        kb.emit('dve', lambda e: e.scalar_tensor_tensor(out=AR[:, :, 0:64], in0=kkn[:].rearrange("p (c t) -> p c t", t=64), scalar=-1.0,
                                                                        in1=tmp[:].rearrange("p (c t) -> p c t", t=64), op0=ALU.mult, op1=ALU.mult),
                                reads=[kkn.r, tmp.r], writes=[AR.r])
                        yield
                        kb.emit('act', lambda e: e.activation(out=eG[:], in_=cs[:], func=AF.Exp, scale=C0), reads=[cs.r], writes=[eG.r])
                        yield
                        kb.emit('dve', lambda e: e.tensor_tensor(out=tmp[:], in0=kkn[:], in1=asg[:], op=ALU.mult), reads=[kkn.r, asg.r], writes=[tmp.r])
                        yield
                        kb.emit('dve', lambda e: e.tensor_tensor(out=BK32[:, 0, :], in0=tmp[:], in1=eG[:], op=ALU.mult), reads=[tmp.r, eG.r], writes=[BK32.r])
                        yield
                        kb.emit('dve', lambda e: e.tensor_tensor(out=BK32[:, 1, :], in0=kmod[:], in1=eG[:], op=ALU.mult), reads=[kmod.r, eG.r], writes=[BK32.r])
                        yield
                        kb.emit('act', lambda e: e.activation(out=BKb[:], in_=BK32[:], func=AF.Copy), reads=[BK32.r], writes=[BKb.r])
                        yield
                        kb.emit('act', lambda e: e.activation(out=vb[:], in_=v32[:], func=AF.Copy), reads=[v32.r], writes=[vb.r])
                        yield
                        for (srcfn, srcres, dstt) in ((lambda c, hv: vb[hv, c * 64:(c + 1) * 64], vb.r, VT), (lambda c, hv: BKb[hv, 0, c * 64:(c + 1) * 64], BKb.r, BTm),
                                                      (lambda c, hv: BKb[hv, 1, c * 64:(c + 1) * 64], BKb.r, KTm)):
                            ptp = ps()
                            for c in range(NC8):
                                for hv in HV:
                                    kb.emit('pe', lambda e, ptp=ptp, c=c, hv=hv, srcfn=srcfn: e.matmul(ptp[hv, c * 64:(c + 1) * 64], lhsT=srcfn(c, hv), rhs=identb[hv, 0, :], start=True, stop=True),
                                            reads=[srcres, identb.r], writes=[ptp.r])
                            kb.emit('act', lambda e, ptp=ptp, dstt=dstt: e.activation(out=dstt[:].rearrange("p c v -> p (c v)"), in_=ptp[:], func=AF.Copy),
                                    reads=[ptp.r], writes=[dstt.r])
                            yield
                        pg = ps()
                        fm_proj(pg, wb_, 256, 128, t0)
                        kb.emit('act', lambda e, pg=pg: e.activation(out=sgate[:], in_=pg[:], func=AF.Silu), reads=[pg.r], writes=[sgate.r])
                        yield
                        yield
                    def P2S(hp, B):
                        AR, BKb, VT, BTm, KTm, gC, bonus, sgate = B
                        tmp = tmp2
                        for c in range(NC8):
                            pm_ = ps()
                            for hv in HV:
                                kb.emit('pe', lambda e, pm_=pm_, c=c, hv=hv: e.matmul(pm_[hv, 0:128], lhsT=BKb[hv, 0, c * 64:(c + 1) * 64], rhs=AR[hv, c, :], start=True, stop=True),
                                        reads=[BKb.r, AR.r], writes=[pm_.r])
                                kb.emit('pe', lambda e, pm_=pm_, c=c, hv=hv: e.matmul(pm_[hv, 128:256], lhsT=BKb[hv, 1, c * 64:(c + 1) * 64], rhs=AR[hv, c, :], start=True, stop=True),
                                        reads=[BKb.r, AR.r], writes=[pm_.r])
                                kb.emit('pe', lambda e, pm_=pm_, c=c, hv=hv: e.matmul(pm_[hv, 256:320], lhsT=AR[hv, c, 0:64], rhs=BKb[hv, 0, c * 64:(c + 1) * 64], start=True, stop=True),
                                        reads=[BKb.r, AR.r], writes=[pm_.r])
                            kb.emit('dve', lambda e, pm_=pm_, c=c: e.tensor_tensor(out=Mall[:, c, :], in0=pm_[:, 0:320], in1=maskR[:], op=ALU.mult),
                                    reads=[pm_.r, maskR.r], writes=[Mall.r])
                            yield
                        kb.emit('dve', lambda e: e.tensor_copy(out=AN[0][:, :, 0:64], in_=Mall[:, :, 256:320]), reads=[Mall.r], writes=[AN[0].r])
                        yield
                        kb.emit('dve', lambda e: e.tensor_copy(out=AN[0][:, :, 64:128], in_=Mall[:, :, 0:64]), reads=[Mall.r], writes=[AN[0].r])
                        yield
                        kb.emit('dve', lambda e: e.tensor_tensor(out=TTb[:], in0=Mall[:, :, 0:64], in1=identb[:, 0:1, :].to_broadcast([128, NC8, 64]), op=ALU.add),
                                reads=[Mall.r, identb.r], writes=[TTb.r])
                        yield
                        for lev in range(5):
                            src_, dst_ = AN[lev % 2], AN[(lev + 1) % 2]
                            for half in range(2):
                                pd = ps()
                                for cc in range(4):
                                    c = half * 4 + cc
                                    for hv in HV:
                                        kb.emit('pe', lambda e, pd=pd, c=c, cc=cc, src_=src_, hv=hv: e.matmul(pd[hv, cc * 128:cc * 128 + 64], lhsT=src_[hv, c, 64:128], rhs=src_[hv, c, 0:64], start=True, stop=True),
                                                reads=[src_.r], writes=[pd.r])
                                        if lev < 4:
                                            kb.emit('pe', lambda e, pd=pd, c=c, cc=cc, src_=src_, hv=hv: e.matmul(pd[hv, cc * 128 + 64:cc * 128 + 128], lhsT=src_[hv, c, 0:64], rhs=src_[hv, c, 64:128], start=True, stop=True),
                                                    reads=[src_.r], writes=[pd.r])
                                kb.emit('act', lambda e, pd=pd, half=half, dst_=dst_: e.activation(out=dst_[:, half * 4:(half + 1) * 4, :].rearrange("p c x -> p (c x)"), in_=pd[:], func=AF.Copy),
                                        reads=[pd.r], writes=[dst_.r])
                                yield
                            pt_ = ps()
                            for c in range(NC8):
                                for hv in HV:
                                    kb.emit('pe', lambda e, pt_=pt_, c=c, dst_=dst_, hv=hv: e.matmul(pt_[hv, c * 64:(c + 1) * 64], lhsT=dst_[hv, c, 0:64], rhs=TTb[hv, c, :], start=True, stop=True),
                                            reads=[dst_.r, TTb.r], writes=[pt_.r])
                            kb.emit('dve', lambda e, pt_=pt_: e.tensor_tensor(out=TTb[:].rearrange("p c x -> p (c x)"), in0=TTb[:].rearrange("p c x -> p (c x)"), in1=pt_[:], op=ALU.add),
                                    reads=[pt_.r, TTb.r], writes=[TTb.r])
                            yield
                        ST = STs[hp]
                        for c in range(NC8):
                            px = ps()
                            for hv in HV:
                                kb.emit('pe', lambda e, px=px, c=c, hv=hv: e.matmul(px[hv, 0:64], lhsT=AR[hv, c, 0:64], rhs=ST[hv, :], start=True, stop=False), reads=[AR.r, ST.r], writes=[px.r])
                                kb.emit('pe', lambda e, px=px, c=c, hv=hv: e.matmul(px[hv, 0:64], lhsT=Mall[hv, c, 128:192], rhs=VT[hv, c, :], start=False, stop=True), reads=[Mall.r, VT.r], writes=[px.r])
                            kb.emit('act', lambda e, px=px: e.activation(out=Xb[:], in_=px[:, 0:64], func=AF.Copy), reads=[px.r], writes=[Xb.r])
                            yield
                            pu = ps()
                            for hv in HV:
                                kb.emit('pe', lambda e, pu=pu, c=c, hv=hv: e.matmul(pu[hv, 0:64], lhsT=TTb[hv, c, :], rhs=Xb[hv, :], start=True, stop=True), reads=[TTb.r, Xb.r], writes=[pu.r])
                            kb.emit('dve', lambda e, pu=pu: e.tensor_copy(out=Ub[:], in_=pu[:, 0:64]), reads=[pu.r], writes=[Ub.r])
                            yield
                            py = ps()
                            pS = ps()
                            for hv in HV:
                                kb.emit('pe', lambda e, py=py, c=c, hv=hv: e.matmul(py[hv, 0:64], lhsT=ST[hv, :], rhs=AR[hv, c, 64:128], start=True, stop=False), reads=[AR.r, ST.r], writes=[py.r])
                                kb.emit('pe', lambda e, py=py, c=c, hv=hv: e.matmul(py[hv, 0:64], lhsT=Ub[hv, :], rhs=Mall[hv, c, 64:128], start=False, stop=False), reads=[Ub.r, Mall.r], writes=[py.r])
                                kb.emit('pe', lambda e, py=py, c=c, hv=hv: e.matmul(py[hv, 0:64], lhsT=VT[hv, c, :], rhs=Mall[hv, c, 192:256], start=False, stop=True), reads=[VT.r, Mall.r], writes=[py.r])
                            for hv in HV:
                                kb.emit('pe', lambda e, pS=pS, c=c, hv=hv: e.matmul(pS[hv, 0:64], lhsT=BTm[hv, c, :], rhs=Ub[hv, :], start=True, stop=False), reads=[BTm.r, Ub.r], writes=[pS.r])
                                kb.emit('pe', lambda e, pS=pS, c=c, hv=hv: e.matmul(pS[hv, 0:64], lhsT=KTm[hv, c, :], rhs=VT[hv, c, :], start=False, stop=False), reads=[KTm.r, VT.r], writes=[pS.r])
                                kb.emit('pe', lambda e, pS=pS, hv=hv: e.matmul(pS[hv, 0:64], lhsT=identb[hv, 0, :], rhs=ST[hv, :], start=False, stop=True), reads=[identb.r, ST.r], writes=[pS.r])
                            kb.emit('act', lambda e, py=py, c=c: e.activation(out=yT[:, c * 64:(c + 1) * 64], in_=py[:, 0:64], func=AF.Copy), reads=[py.r], writes=[yT.r])
                            yield
                            kb.emit('dve', lambda e, pS=pS, c=c: e.tensor_scalar(out=ST[:], in0=pS[:, 0:64], scalar1=gC[:, c:c + 1], scalar2=None, op0=ALU.mult),
                                    reads=[pS.r, gC.r], writes=[ST.r])
                            yield
                        pm2 = ps()
                        kb.emit('pe', lambda e, pm2=pm2: e.matmul(pm2[:], lhsT=ones_bd[:], rhs=yT[:], start=True, stop=True), reads=[ones_bd.r, yT.r], writes=[pm2.r])
                        kb.emit('dve', lambda e, pm2=pm2: e.scalar_tensor_tensor(out=yT[:], in0=pm2[:], scalar=-1.0 / 64.0, in1=yT[:], op0=ALU.mult, op1=ALU.add),
                                reads=[pm2.r, yT.r], writes=[yT.r])
                        yield
                        kb.emit('act', lambda e: e.activation(out=tmp[:], in_=yT[:], func=AF.Square), reads=[yT.r], writes=[tmp.r])
                        yield
                        pv2 = ps()
                        kb.emit('pe', lambda e, pv2=pv2: e.matmul(pv2[:], lhsT=ones_bd[:], rhs=tmp[:], start=True, stop=True), reads=[ones_bd.r, tmp.r], writes=[pv2.r])
                        kb.emit('act', lambda e, pv2=pv2: e.activation(out=tmp[:], in_=pv2[:], func=AF.Ln, scale=1.0 / 64.0, bias=eps_t[:, 1:2]), reads=[pv2.r, eps_t.r], writes=[tmp.r])
                        yield
                        kb.emit('act', lambda e: e.activation(out=tmp[:], in_=tmp[:], func=AF.Exp, scale=-0.5), reads=[tmp.r], writes=[tmp.r])
                        yield
                        kb.emit('dve', lambda e: e.tensor_tensor(out=yT[:], in0=yT[:], in1=tmp[:], op=ALU.mult), reads=[yT.r, tmp.r], writes=[yT.r])
                        yield
                        kb.emit('dve', lambda e, hp=hp: e.tensor_scalar(out=yT[:], in0=yT[:], scalar1=lngt[:, hp:hp + 1], scalar2=lnbt[:, hp:hp + 1], op0=ALU.mult, op1=ALU.add),
                                reads=[yT.r, lngt.r, lnbt.r], writes=[yT.r])
                        yield
                        kb.emit('dve', lambda e: e.tensor_tensor(out=yT[:], in0=yT[:], in1=bonus[:], op=ALU.add), reads=[yT.r, bonus.r], writes=[yT.r])
                        yield
                        kb.emit('dve', lambda e, hp=hp: e.tensor_tensor(out=yg[0][:, hp, t0:t0 + TT], in0=yT[:], in1=sgate[:], op=ALU.mult), reads=[yT.r, sgate.r], writes=[yg[0].r])
                        yield
                        yield
                    prev = None
                    for hp in range(4):
                        gens = [P1(hp, RWB[hp % 2])] + ([prev] if prev is not None else [])
                        _interleave(gens)
                        prev = P2S(hp, RWB[hp % 2])
                    _interleave([prev] + ([LRUgen()] if ('lru' in MIX and tt == NT - 1) else []))

            if 'rw' not in MIX and 'lru' in MIX:
                _interleave([LRUgen()])
            for n_, nm in ((1, 'gla'), (2, 'ret')):
                if nm not in MIX:
                    continue
                isg = (nm == 'gla')
                S32, Sb = (Sg32, Sgb) if isg else (Sr32, Srb)
                gn = gng if isg else rgg
                wqk = wtile()
                load_w(wqk, Wb[l][:, AUGOFF[nm + '_q']:AUGOFF[nm + '_q'] + 256], 256, col0=0)
                load_w(wqk, Wb[l][:, AUGOFF[nm + '_k']:AUGOFF[nm + '_k'] + 256], 256, col0=256, batch=True)
                wv = wtile()
                load_w(wv, Wb[l][:, AUGOFF[nm + '_v']:AUGOFF[nm + '_v'] + 512], 512)
                if isg:
                    wg = wtile()
                    load_w(wg, Wb[l][:, AUGOFF[nm + '_g']:AUGOFF[nm + '_g'] + 512], 512)
                if isg:
                    kb.dma('sp', wflo[:], Wb[l][:, AUGOFF['gla_flo']:AUGOFF['gla_flo'] + 16].rearrange("(k p) c -> p k c", p=128),
                           wflo.r, reads=[wdram_r], writes=[wflo.r])
                else:
                    wsw = wtile()
                    load_w(wsw, Wb[l][:, AUGOFF['ret_qs']:AUGOFF['ret_qs'] + 256], 256, col0=0)
                    load_w(wsw, Wb[l][:, AUGOFF['ret_ks']:AUGOFF['ret_ks'] + 256], 256, col0=256, batch=True)
                for tt in range(NT):
                    t0 = tt * TT
                    NCH = TT // 128
                    for c in range(NCH):
                        p = ps()
                        for k in range(KC):
                            kb.emit('pe', lambda e, k=k, p=p, c=c: e.matmul(p[:], lhsT=un[:, k, 3 + t0 + c * 128:3 + t0 + (c + 1) * 128], rhs=wv[:, k, :],
                                                                            start=(k == 0), stop=(k == KC - 1)), reads=[un.r, wv.r], writes=[p.r])
                        kb.emit('act', lambda e, p=p, c=c: e.activation(out=vT[c][:], in_=p[:], func=AF.Copy), reads=[p.r], writes=[vT[c].r])
                    if isg:
                        pf = ps()
                        fm_proj(pf, wflo, 0, 16, t0)
                        kb.emit('act', lambda e, pf=pf: e.activation(out=flo_b[:], in_=pf[0:16, :], func=AF.Copy), reads=[pf.r], writes=[flo_b.r])
                    else:
                        kb.dma('sp', posi[:], positions[s0 + t0:s0 + t0 + TT].partition_broadcast(64), posi.r, writes=[posi.r])
                        A_, B_, C_ = WK[0], WK[1], WK[2]
                        kb.emit('dve', lambda e: e.tensor_copy(out=A_[0:64, :], in_=posi[:]), reads=[posi.r], writes=[A_.r])
                        kb.emit('dve', lambda e: e.tensor_scalar(out=A_[0:64, :], in0=A_[0:64, :], scalar1=invf[:, 0:1], scalar2=1.0 / (2 * np.pi),
                                                                 op0=ALU.mult, op1=ALU.mult), reads=[A_.r, invf.r], writes=[A_.r])
                        kb.emit('dve', lambda e: e.tensor_copy(out=posi[:], in_=A_[0:64, :]), reads=[A_.r], writes=[posi.r])
                        kb.emit('dve', lambda e: e.tensor_copy(out=B_[0:64, :], in_=posi[:]), reads=[posi.r], writes=[B_.r])
                        kb.emit('dve', lambda e: e.tensor_tensor(out=A_[0:64, :], in0=A_[0:64, :], in1=B_[0:64, :], op=ALU.subtract),
                                reads=[A_.r, B_.r], writes=[A_.r])
                        kb.emit('act', lambda e: e.activation(out=B_[0:64, :], in_=A_[0:64, :], func=AF.Sin, scale=float(np.pi)), reads=[A_.r], writes=[B_.r])
                        kb.emit('act', lambda e: e.activation(out=C_[0:64, :], in_=A_[0:64, :], func=AF.Sin, scale=float(np.pi / 2)), reads=[A_.r], writes=[C_.r])
                        kb.emit('dve', lambda e: e.tensor_tensor(out=cosT[0:64, :], in0=B_[0:64, :], in1=B_[0:64, :], op=ALU.mult), reads=[B_.r], writes=[cosT.r])
                        kb.emit('dve', lambda e: e.tensor_scalar(out=cosT[0:64, :], in0=cosT[0:64, :], scalar1=-2.0, scalar2=1.0, op0=ALU.mult, op1=ALU.add),
                                reads=[cosT.r], writes=[cosT.r])
                        kb.emit('dve', lambda e: e.tensor_tensor(out=C_[0:64, :], in0=C_[0:64, :], in1=C_[0:64, :], op=ALU.mult), reads=[C_.r], writes=[C_.r])
                        kb.emit('dve', lambda e: e.tensor_scalar(out=C_[0:64, :], in0=C_[0:64, :], scalar1=-4.0, scalar2=2.0, op0=ALU.mult, op1=ALU.add),
                                reads=[C_.r], writes=[C_.r])
                        kb.emit('dve', lambda e: e.tensor_tensor(out=sinT[0:64, :], in0=C_[0:64, :], in1=B_[0:64, :], op=ALU.mult), reads=[C_.r, B_.r], writes=[sinT.r])
                    def _dec(h):
                        pq = ps()
                        fm_proj(pq, wqk, h * 64, 64, t0)
                        pk = ps()
                        fm_proj(pk, wqk, 256 + h * 64, 64, t0)
                        if isg:
                            plf = ps()
                            kb.emit('pe', lambda e, plf=plf, h=h: e.matmul(plf[0:64, :], lhsT=fup_b[0:16, h * 64:(h + 1) * 64], rhs=flo_b[0:16, :], start=True, stop=True),
                                    reads=[fup_b.r, flo_b.r], writes=[plf.r])
                            cs, eb, enb = (WK[0], WK[1], WK[2]) if h % 2 == 0 else (WK[5], WK[6], WK[7])
                            kb.emit('act', lambda e, plf=plf, h=h: e.activation(out=cs[0:64, :], in_=plf[0:64, :], func=AF.Exp, scale=-1.0, bias=nfb[:, h:h + 1]),
                                    reads=[plf.r, nfb.r], writes=[cs.r])
                            yield
                            kb.emit('act', lambda e: e.activation(out=cs[0:64, :], in_=cs[0:64, :], func=AF.Ln, bias=eps_t[0:64, 2:3]), reads=[cs.r, eps_t.r], writes=[cs.r])
                            yield
                            for c in range(NCH):
                                kb.emit('dve', lambda e, c=c: e.tensor_tensor_scan(out=eb[0:64, c * 128:(c + 1) * 128], data0=ones_f[0:64, :], data1=cs[0:64, c * 128:(c + 1) * 128],
                                                                                   initial=0.0, op0=ALU.mult, op1=ALU.add), reads=[cs.r, ones_f.r], writes=[eb.r])
                                yield
                            kb.emit('act', lambda e: e.activation(out=enb[0:64, :], in_=eb[0:64, :], func=AF.Exp, scale=1.0 / 16.0), reads=[eb.r], writes=[enb.r])
                            yield
                            kb.emit('act', lambda e: e.activation(out=eb[0:64, :], in_=eb[0:64, :], func=AF.Exp, scale=-1.0 / 16.0), reads=[eb.r], writes=[eb.r])
                            yield
                            kb.emit('dve', lambda e, pq=pq, h=h: e.scalar_tensor_tensor(out=qd[:, h, :], in0=pq[0:64, :], scalar=0.125, in1=eb[0:64, :], op0=ALU.mult, op1=ALU.mult),
                                    reads=[pq.r, eb.r], writes=[qd.r])
                            yield
                            kb.emit('dve', lambda e, pk=pk, h=h: e.tensor_tensor(out=kd32[:, h, :], in0=pk[0:64, :], in1=enb[0:64, :], op=ALU.mult),
                                    reads=[pk.r, enb.r], writes=[kd32.r])
                            yield
                            for c in range(NCH):
                                kb.emit('act', lambda e, c=c, h=h: e.activation(out=ebl[:, h, c:c + 1], in_=eb[0:64, c * 128 + 127:c * 128 + 128], func=AF.Copy),
                                        reads=[eb.r], writes=[ebl.r])
                                yield
                        else:
                            pqs = ps()
                            fm_proj(pqs, wsw, h * 64, 64, t0)
                            pks = ps()
                            fm_proj(pks, wsw, 256 + h * 64, 64, t0)
                            q1, q2 = (WK[3], WK[4]) if h % 2 == 0 else (WK[8], WK[9])
                            for (pa, pb, dst, tab) in ((pq, pqs, qd, ebR), (pk, pks, kd32, enbR)):
                                kb.emit('dve', lambda e, pa=pa: e.tensor_tensor(out=q1[0:64, :], in0=pa[0:64, :], in1=cosT[0:64, :], op=ALU.mult), reads=[pa.r, cosT.r], writes=[q1.r])
                                yield
                                kb.emit('dve', lambda e, pb=pb: e.tensor_tensor(out=q2[0:64, :], in0=pb[0:64, :], in1=sinT[0:64, :], op=ALU.mult), reads=[pb.r, sinT.r], writes=[q2.r])
                                yield
                                kb.emit('dve', lambda e: e.tensor_tensor(out=q1[0:64, :], in0=q1[0:64, :], in1=q2[0:64, :], op=ALU.add), reads=[q1.r, q2.r], writes=[q1.r])
                                yield
                                kb.emit('dve', lambda e, dst=dst, tab=tab, h=h: e.tensor_tensor(
                                    out=dst[:, h, :].rearrange("p (c t) -> p c t", t=128), in0=q1[0:64, :].rearrange("p (c t) -> p c t", t=128),
                                    in1=tab[:, h:h + 1, :].to_broadcast([64, NCH, 128]), op=ALU.mult), reads=[q1.r, tab.r], writes=[dst.r])
                                yield
                    for h0 in (0, 2):
                        _interleave([_dec(h0), _dec(h0 + 1)])
                    kb.emit('act', lambda e: e.activation(out=kd[:], in_=kd32[:], func=AF.Copy), reads=[kd32.r], writes=[kd.r])
                    for c in range(NCH):
                        cs_ = slice(c * 128, (c + 1) * 128)
                        pa = ps()
                        for h in range(4):
                            kb.emit('pe', lambda e, pa=pa, h=h, cs_=cs_: e.matmul(pa[:, h * 128:(h + 1) * 128], lhsT=kd[:, h, cs_], rhs=qd[:, h, cs_], start=True, stop=True),
                                    reads=[kd.r, qd.r], writes=[pa.r])
                        kb.emit('dve', lambda e, pa=pa: e.tensor_tensor(out=attb[:], in0=pa[:], in1=mask4[:], op=ALU.mult), reads=[pa.r, mask4.r], writes=[attb.r])
                        po = ps()
                        for h in range(4):
                            kb.emit('pe', lambda e, po=po, h=h, c=c: e.matmul(po[:, h * 128:(h + 1) * 128], lhsT=vT[c][:, h * 128:(h + 1) * 128], rhs=attb[:, h * 128:(h + 1) * 128],
                                                                              start=True, stop=False), reads=[vT[c].r, attb.r], writes=[po.r])
                            kb.emit('pe', lambda e, po=po, h=h, cs_=cs_: e.matmul(po[:, h * 128:(h + 1) * 128], lhsT=Sb[:, h, :], rhs=qd[:, h, cs_], start=False, stop=True),
                                    reads=[Sb.r, qd.r], writes=[po.r])
                        kb.emit('act', lambda e, po=po, cs_=cs_: e.activation(out=oall[:, :, cs_], in_=po[:].rearrange("p (h t) -> p h t", h=4), func=AF.Copy),
                                reads=[po.r], writes=[oall.r])
                        pt = ps()
                        for h in range(4):
                            kb.emit('pe', lambda e, pt=pt, h=h, cs_=cs_: e.transpose(out=pt[:, h * 64:(h + 1) * 64], in_=kd32[:, h, cs_], identity=ident[0:64, 0:64]),
                                    reads=[kd32.r, ident.r], writes=[pt.r])
                        kb.emit('dve', lambda e, pt=pt: e.tensor_copy(out=kdT[:], in_=pt[:, 0:256]), reads=[pt.r], writes=[kdT.r])
                        pss = ps()
                        for h in range(4):
                            kb.emit('pe', lambda e, pss=pss, h=h, c=c: e.matmul(pss[0:64, h * 128:(h + 1) * 128], lhsT=kdT[:, h * 64:(h + 1) * 64], rhs=vT[c][:, h * 128:(h + 1) * 128],
                                                                                start=True, stop=True), reads=[kdT.r, vT[c].r], writes=[pss.r])
                        kb.emit('dve', lambda e, pss=pss: e.tensor_tensor(out=S32[:].rearrange("p h v -> p (h v)"), in0=S32[:].rearrange("p h v -> p (h v)"), in1=pss[0:64, :], op=ALU.add),
                                reads=[pss.r, S32.r], writes=[S32.r])
                        for h in range(4):
                            if isg:
                                kb.emit('dve', lambda e, h=h, c=c: e.tensor_scalar(out=S32[:, h, :], in0=S32[:, h, :], scalar1=ebl[:, h, c:c + 1], scalar2=None, op0=ALU.mult),
                                        reads=[S32.r, ebl.r], writes=[S32.r])
                            else:
                                kb.emit('dve', lambda e, h=h: e.tensor_scalar(out=S32[:, h, :], in0=S32[:, h, :], scalar1=float(RET_G[h] ** 128), scalar2=None, op0=ALU.mult),
                                        reads=[S32.r], writes=[S32.r])
                        kb.emit('act', lambda e: e.activation(out=Sb[:], in_=S32[:], func=AF.Copy), reads=[S32.r], writes=[Sb.r])
                    if not isg:
                        wg = wtile()
                        load_w(wg, Wb[l][:, AUGOFF[nm + '_g']:AUGOFF[nm + '_g'] + 512], 512)
                    def _nrm(h):
                        o_h, sqq, rs_, gg = (WK[5], WK[6], WK[7], WK[8]) if h % 2 == 0 else (WK[0], WK[1], WK[2], WK[3])
                        if isg:
                            kb.emit('act', lambda e, h=h: e.activation(out=sqq[:], in_=oall[:, h, :], func=AF.Square), reads=[oall.r], writes=[sqq.r])
                            yield
                            pv_ = ps()
                            kb.emit('pe', lambda e, pv_=pv_: e.matmul(pv_[:], lhsT=ones_f[:], rhs=sqq[:], start=True, stop=True), reads=[ones_f.r, sqq.r], writes=[pv_.r])
                            src_o = None
                        else:
                            pm_ = ps()
                            kb.emit('pe', lambda e, pm_=pm_, h=h: e.matmul(pm_[:], lhsT=ones_f[:], rhs=oall[:, h, :], start=True, stop=True), reads=[ones_f.r, oall.r], writes=[pm_.r])
                            kb.emit('dve', lambda e, pm_=pm_, h=h: e.scalar_tensor_tensor(out=o_h[:], in0=pm_[:], scalar=-1.0 / 128.0, in1=oall[:, h, :], op0=ALU.mult, op1=ALU.add),
                                    reads=[pm_.r, oall.r], writes=[o_h.r])
                            yield
                            kb.emit('act', lambda e: e.activation(out=sqq[:], in_=o_h[:], func=AF.Square), reads=[o_h.r], writes=[sqq.r])
                            yield
                            pv_ = ps()
                            kb.emit('pe', lambda e, pv_=pv_: e.matmul(pv_[:], lhsT=ones_f[:], rhs=sqq[:], start=True, stop=True), reads=[ones_f.r, sqq.r], writes=[pv_.r])
                        kb.emit('act', lambda e, pv_=pv_: e.activation(out=rs_[:], in_=pv_[:], func=AF.Ln, scale=1.0 / 128.0, bias=eps_t[:, 0:1]), reads=[pv_.r, eps_t.r], writes=[rs_.r])
                        yield
                        kb.emit('act', lambda e: e.activation(out=rs_[:], in_=rs_[:], func=AF.Exp, scale=-0.5), reads=[rs_.r], writes=[rs_.r])
                        yield
                        if isg:
                            kb.emit('dve', lambda e, h=h: e.scalar_tensor_tensor(out=rs_[:], in0=oall[:, h, :], scalar=gn[:, h:h + 1], in1=rs_[:], op0=ALU.mult, op1=ALU.mult),
                                    reads=[oall.r, gn.r, rs_.r], writes=[rs_.r])
                            yield
                        else:
                            kb.emit('dve', lambda e, h=h: e.scalar_tensor_tensor(out=rs_[:], in0=o_h[:], scalar=gn[:, h:h + 1], in1=rs_[:], op0=ALU.mult, op1=ALU.mult),
                                    reads=[o_h.r, gn.r, rs_.r], writes=[rs_.r])
                            yield
                        pg = ps()
                        fm_proj(pg, wg, h * 128, 128, t0)
                        kb.emit('act', lambda e, pg=pg: e.activation(out=gg[:], in_=pg[:], func=AF.Silu), reads=[pg.r], writes=[gg.r])
                        yield
                        kb.emit('dve', lambda e, h=h, n_=n_: e.tensor_tensor(out=yg[n_][:, h, t0:t0 + TT], in0=rs_[:], in1=gg[:], op=ALU.mult),
                                reads=[rs_.r, gg.r], writes=[yg[n_].r])
                        yield

                    for h0 in (0, 2):
                        _interleave([_nrm(h0), _nrm(h0 + 1)])
            for j in range(KC):
                wgt = wtile()
                for n in range(4):
                    c = AUGOFF['merge%d' % (2 * n + j // 4)] + (j % 4) * 128
                    load_w(wgt, Wb[l][:, c:c + 128], 128, col0=n * 128, batch=(n > 0))
                wbr = wtile()
                for n in range(4):
                    load_w(wbr, WBR[l][n * 512:(n + 1) * 512, j * 128:(j + 1) * 128], 128, col0=n * 128, kc=4, batch=(n > 0))
                for tt in range(NT):
                    t0 = tt * TT
                    acc = WK[6]
                    for n in range(4):
                        gt = WK[7] if n % 2 == 0 else WK[8]
                        pg = ps()
                        fm_proj(pg, wgt, n * 128, 128, t0)
                        pb = ps()
                        for k in range(4):
                            kb.emit('pe', lambda e, k=k, n=n, pb=pb, t0=t0: e.matmul(pb[:], lhsT=wbr[:, k, n * 128:(n + 1) * 128],
                                                                                      rhs=yg[n][:, k, t0:t0 + TT], start=(k == 0), stop=(k == 3)),
                                    reads=[wbr.r, yg[n].r], writes=[pb.r])
                        kb.emit('act', lambda e, pg=pg: e.activation(out=gt[:], in_=pg[:], func=AF.Sigmoid), reads=[pg.r], writes=[gt.r])
                        if n == 0:
                            kb.emit('dve', lambda e, pb=pb: e.tensor_tensor(out=acc[:], in0=gt[:], in1=pb[:], op=ALU.mult),
                                    reads=[gt.r, pb.r], writes=[acc.r])
                        else:
                            kb.emit('dve', lambda e, pb=pb: e.tensor_tensor(out=gt[:], in0=gt[:], in1=pb[:], op=ALU.mult),
                                    reads=[gt.r, pb.r], writes=[gt.r])
                            if n < 3:
                                kb.emit('dve', lambda e: e.tensor_tensor(out=acc[:], in0=acc[:], in1=gt[:], op=ALU.add),
                                        reads=[acc.r, gt.r], writes=[acc.r])
                            else:
                                kb.emit('dve', lambda e, j=j, t0=t0: e.tensor_tensor(out=merged[:, j, t0:t0 + TT], in0=acc[:], in1=gt[:], op=ALU.add),
                                        reads=[acc.r, gt.r], writes=[merged.r])
            for j in range(KC):
                w = wtile()
                load_w(w, WO[l][:, j * 128:(j + 1) * 128], 128)
                for tt in range(NT):
                    t0 = tt * TT
                    p = ps()
                    for k in range(KC):
                        kb.emit('pe', lambda e, k=k, p=p, w=w, t0=t0: e.matmul(p[:], lhsT=w[:, k, 0:128], rhs=merged[:, k, t0:t0 + TT],
                                                                               start=(k == 0), stop=(k == KC - 1)),
                                reads=[w.r, merged.r], writes=[p.r])
                    kb.emit('dve', lambda e, p=p, j=j, t0=t0: e.tensor_tensor(out=hseg[:, j, t0:t0 + TT], in0=hseg[:, j, t0:t0 + TT], in1=p[:], op=ALU.add),
                            reads=[p.r, hseg.r], writes=[hseg.r])

            if XA:
                for tt in range(NT):
                    hv = _View(hseg, slice(tt * TT, (tt + 1) * TT))
                    rmsnorm_fm(hv, xng, lambda k, tt=tt: un[:, k, 3 + tt * TT:3 + (tt + 1) * TT], un.r)
                for j in range(KC):
                    w = wtile()
                    load_w(w, WQ[l][:, j * 128:(j + 1) * 128], 128)
                    for tt in range(NT):
                        t0 = tt * TT
                        p = ps()
                        fm_proj(p, w, 0, 128, t0)
                        kb.emit('act', lambda e, p=p, j=j, t0=t0: e.activation(out=merged[:, j, t0:t0 + TT], in_=p[:], func=AF.Copy),
                                reads=[p.r], writes=[merged.r])
                for tt in range(NT):
                    t0 = tt * TT
                    for hh_ in range(4):
                        pT = [WB16[0], WB16[1]] if hh_ % 2 == 0 else [WB16[2], WB16[3]]
                        psum_ = ps()
                        for mb in range(2):
                            p = ps()
                            for k2 in range(2):
                                kb.emit('pe', lambda e, p=p, k2=k2, mb=mb, hh_=hh_, t0=t0: e.matmul(
                                    p[:], lhsT=kT[:, hh_ * 2 + k2, mb * 128:(mb + 1) * 128], rhs=merged[:, hh_ * 2 + k2, t0:t0 + TT],
                                    start=(k2 == 0), stop=(k2 == 1)), reads=[kT.r, merged.r], writes=[p.r])
                            kb.emit('act', lambda e, p=p, mb=mb: e.activation(out=pT[mb][:], in_=p[:], func=AF.Exp, scale=1.0 / 16.0),
                                    reads=[p.r], writes=[pT[mb].r])
                        for mb in range(2):
                            kb.emit('pe', lambda e, mb=mb, psum_=psum_: e.matmul(psum_[:], lhsT=ones_b[:], rhs=pT[mb][:], start=(mb == 0), stop=(mb == 1)),
                                    reads=[ones_b.r, pT[mb].r], writes=[psum_.r])
                        rs = WK[8] if hh_ % 2 == 0 else WK[9]
                        kb.emit('act', lambda e, psum_=psum_, rs=rs: e.activation(out=rs[:], in_=psum_[:], func=AF.Ln), reads=[psum_.r], writes=[rs.r])
                        kb.emit('act', lambda e, rs=rs: e.activation(out=rs[:], in_=rs[:], func=AF.Exp, scale=-1.0), reads=[rs.r], writes=[rs.r])
                        for dh in range(2):
                            po = ps()
                            for mb in range(2):
                                kb.emit('pe', lambda e, po=po, mb=mb, dh=dh, hh_=hh_: e.matmul(
                                    po[:], lhsT=vtm[:, mb, hh_ * 256 + dh * 128:hh_ * 256 + (dh + 1) * 128], rhs=pT[mb][:],
                                    start=(mb == 0), stop=(mb == 1)), reads=[vtm.r, pT[mb].r], writes=[po.r])
                            dst = yg[hh_ // 2]
                            kb.emit('dve', lambda e, po=po, dst=dst, hh_=hh_, dh=dh, t0=t0: e.tensor_tensor(
                                out=dst[:, (hh_ % 2) * 2 + dh, t0:t0 + TT], in0=po[:], in1=rs[:], op=ALU.mult),
                                    reads=[po.r, rs.r], writes=[dst.r])
                for j in range(KC):
                    w = wtile()
                    load_w(w, WXO[l][:, j * 128:(j + 1) * 128], 128)
                    for tt in range(NT):
                        t0 = tt * TT
                        p = ps()
                        for k in range(KC):
                            src = yg[k // 4]
                            kb.emit('pe', lambda e, k=k, p=p, w=w, t0=t0, src=src: e.matmul(p[:], lhsT=w[:, k, 0:128], rhs=src[:, k % 4, t0:t0 + TT],
                                                                                            start=(k == 0), stop=(k == KC - 1)),
                                    reads=[w.r, src.r], writes=[p.r])
                        kb.emit('dve', lambda e, p=p, j=j, t0=t0: e.tensor_tensor(out=hseg[:, j, t0:t0 + TT], in0=hseg[:, j, t0:t0 + TT], in1=p[:], op=ALU.add),
                                reads=[p.r, hseg.r], writes=[hseg.r])
            kb.dma('pool', hT_v[:, :, s0:s0 + SEG], hseg[:], hseg.r, reads=[hseg.r], writes=[hT_r])

    kb.new_scope()
    _tl.clear()
    hs_t = [T(kb, "hs%d" % i, [128, KC, TT], F32) for i in range(2)]
    yn = T(kb, "yn", [128, KC, TT], F32)
    otm = [T(kb, "otm%d" % i, [128, D], F32) for i in range(2)]
    oi = 0
    for tt in range(S // TT):
        hs = hs_t[tt % 2]
        kb.dma('sp', hs[:], hT_v[:, :, tt * TT:(tt + 1) * TT], hs.r, reads=[hT_r], writes=[hs.r])
        rmsnorm_fm(hs, fng, lambda k: yn[:, k, :], yn.r)
        for tb in range(TT // 128):
            ot = otm[oi % 2]
            oi += 1
            for half in range(2):
                p = ps()
                for kk in range(4):
                    k = half * 4 + kk
                    kb.emit('pe', lambda e, p=p, kk=kk, k=k, tb=tb: e.transpose(
                        out=p[:, kk * 128:(kk + 1) * 128], in_=yn[:, k, tb * 128:(tb + 1) * 128], identity=ident[:]),
                            reads=[yn.r, ident.r], writes=[p.r])
                if half:
                    kb.emit('act', lambda e, p=p, ot=ot: e.activation(out=ot[:, 512:1024], in_=p[:], func=AF.Copy),
                            reads=[p.r], writes=[ot.r])
                else:
                    kb.emit('dve', lambda e, p=p, ot=ot: e.tensor_copy(out=ot[:, 0:512], in_=p[:]),
                            reads=[p.r], writes=[ot.r])
            r0 = tt * TT + tb * 128
            kb.dma('pool', out[r0:r0 + 128, :], ot[:], ot.r, reads=[ot.r])
    kb.wait_all('pool', [t.r for t in otm])
    kb.new_scope()
    kb.scope.close()
    build.ninst = kb.ninst
    return nc, es


PARAM_NAMES = ('mix_norm_g', 'w_in', 'rw_mu', 'rw_w0', 'rw_w2', 'rw_a0', 'rw_a2', 'rw_k_k', 'rw_k_a', 'rw_r_k', 'rw_ln_g',
               'rw_ln_b', 'gla_f_up', 'gla_f_b', 'gla_norm_g', 'ret_gn_g', 'lru_conv_w', 'lru_conv_b', 'lru_wa', 'lru_ba',
               'lru_wx', 'lru_bx', 'lru_lambda', 'w_branch', 'w_out', 'xa_norm_g', 'xa_mem_norm_g', 'xa_wq', 'xa_wkv',
               'xa_wo', 'final_norm_g')


def core_inputs(inputs, b, S):
    m = {"x": np.ascontiguousarray(inputs['x'][b, :S]), "mem": np.ascontiguousarray(inputs['mem'][b]),
         "positions": np.ascontiguousarray(inputs['positions'][b, :S]).astype(np.int32)}
    for n in PARAM_NAMES:
        m[n] = np.ascontiguousarray(inputs[n])
    return m


def kernel(**inputs):
    S = inputs['x'].shape[1]
    nc, es = build(S, 2)
    in_maps = [core_inputs(inputs, c % 2, S) for c in range(8)]
    res = run_bass_kernel_spmd(nc, in_maps, core_ids=list(range(8)))
    es.close()
    return np.stack([res.results[0]["out"], res.results[1]["out"]], axis=0)
```
